# Optimizing a Trainium2 kernel written in Bass

```python
import math
import jax, jax.numpy as jnp
from jax import lax
import numpy as np

D_MODEL = 1024
BATCH = 2
SEQ = 8192
DEPTH = 2

EPS = 1e-6
CHUNK = 128
SSD_INNER = D_MODEL
SSD_HEADDIM = 64
SSD_HEADS = SSD_INNER // SSD_HEADDIM
SSD_GROUPS = 2
SSD_STATE = 128
SSD_CONV = 4
SSD_CONV_DIM = SSD_INNER + 2 * SSD_GROUPS * SSD_STATE
LRU_WIDTH = D_MODEL
LRU_BLOCKS = 4
LRU_BLOCK = LRU_WIDTH // LRU_BLOCKS
LRU_CONV = 4
LRU_C = 8.0
RET_HEADS = 4
RET_QK_DIM = D_MODEL // (2 * RET_HEADS)
RET_V_DIM = D_MODEL // RET_HEADS
RET_WIDTH = RET_HEADS * RET_V_DIM
ROPE_BASE = 10000.0
FFN_DIM = 3 * D_MODEL
FFN_CONV = 3
N_BRANCHES = 3
IN_SPLITS = (SSD_INNER, SSD_CONV_DIM, SSD_HEADS, LRU_WIDTH, LRU_WIDTH,
             RET_HEADS * RET_QK_DIM, RET_HEADS * RET_QK_DIM, RET_WIDTH, RET_WIDTH,
             N_BRANCHES * D_MODEL)
IN_PROJ_DIM = sum(IN_SPLITS)

kernel_name = 'hybrid_ssd_rglru_retention_gated_block'


def rmsnorm(x, g):
    x32 = x.astype(jnp.float32)
    y = x32 * lax.rsqrt(jnp.mean(x32 * x32, axis=-1, keepdims=True) + EPS)
    return (y * g.astype(jnp.float32)).astype(x.dtype)


def causal_dwconv(x, w, b):
    k = w.shape[0]
    y = lax.conv_general_dilated(
        x, w[:, None, :].astype(x.dtype), window_strides=(1,), padding=[(k - 1, 0)],
        dimension_numbers=('NWC', 'WIO', 'NWC'), feature_group_count=x.shape[-1])
    return y + b


def ssd_chunked(xdt, a, bm, cm):
    b, s, h, p = xdt.shape
    g, n = bm.shape[-2:]
    r = h // g
    c, L = s // CHUNK, CHUNK
    xdt = xdt.reshape(b, c, L, g, r, p)
    a = a.reshape(b, c, L, g, r)
    bm = bm.reshape(b, c, L, g, n)
    cm = cm.reshape(b, c, L, g, n)
    a_cs = jnp.cumsum(a, axis=2)
    causal = jnp.tril(jnp.ones((L, L), dtype=bool))
    seg = a_cs[:, :, :, None] - a_cs[:, :, None]
    decay = jnp.exp(jnp.where(causal[:, :, None, None], seg, -jnp.inf))
    cb = jnp.einsum('bclgn,bcsgn->bclsg', cm, bm)
    y_diag = jnp.einsum('bclsgr,bcsgrp->bclgrp', cb[..., None] * decay, xdt)
    state_decay = jnp.exp(a_cs[:, :, -1:] - a_cs)
    chunk_states = jnp.einsum('bclgn,bclgr,bclgrp->bcgrpn', bm, state_decay, xdt)
    chunk_decay = jnp.exp(a_cs[:, :, -1])

    def step(state, inp):
        st, dec = inp
        return state * dec[..., None, None] + st, state

    init = jnp.zeros((b, g, r, p, n), dtype=jnp.float32)
    _, s_prev = lax.scan(step, init, (jnp.moveaxis(chunk_states, 1, 0), jnp.moveaxis(chunk_decay, 1, 0)))
    s_prev = jnp.moveaxis(s_prev, 0, 1)
    y_off = jnp.einsum('bclgn,bcgrpn,bclgr->bclgrp', cm, s_prev, jnp.exp(a_cs))
    return (y_diag + y_off).reshape(b, s, h, p)


def ssd_mixer(z, xbc, dt_raw, conv_w, conv_b, dt_bias, a_log, d_skip, norm_w):
    b, s, _ = z.shape
    xbc = jax.nn.silu(causal_dwconv(xbc, conv_w, conv_b))
    xs, bm, cm = jnp.split(xbc, [SSD_INNER, SSD_INNER + SSD_GROUPS * SSD_STATE], axis=-1)
    xs = xs.reshape(b, s, SSD_HEADS, SSD_HEADDIM).astype(jnp.float32)
    bm = bm.reshape(b, s, SSD_GROUPS, SSD_STATE).astype(jnp.float32)
    cm = cm.reshape(b, s, SSD_GROUPS, SSD_STATE).astype(jnp.float32)
    dt = jax.nn.softplus(dt_raw.astype(jnp.float32) + dt_bias.astype(jnp.float32))
    a = -jnp.exp(a_log.astype(jnp.float32))
    y = ssd_chunked(xs * dt[..., None], dt * a, bm, cm)
    y = y + xs * d_skip.astype(jnp.float32)[:, None]
    y = y.reshape(b, s, SSD_INNER) * jax.nn.silu(z.astype(jnp.float32))
    return rmsnorm(y, norm_w).astype(z.dtype)


def _lru_combine(left, right):
    a1, b1 = left
    a2, b2 = right
    return a1 * a2, a2 * b1 + b2


def rglru_mixer(y_gate, xb, conv_w, conv_b, w_a, b_a, w_i, b_i, lam):
    b, s, _ = xb.shape
    xb = causal_dwconv(xb, conv_w, conv_b).astype(jnp.float32)
    xblk = xb.reshape(b, s, LRU_BLOCKS, LRU_BLOCK)
    r = jax.nn.sigmoid(jnp.einsum('bski,kij->bskj', xblk, w_a).reshape(b, s, LRU_WIDTH) + b_a)
    i = jax.nn.sigmoid(jnp.einsum('bski,kij->bskj', xblk, w_i).reshape(b, s, LRU_WIDTH) + b_i)
    log_a = -LRU_C * r * jax.nn.softplus(-lam.astype(jnp.float32))
    a = jnp.exp(log_a)
    u = jnp.sqrt(-jnp.expm1(2.0 * log_a)) * (i * xb)
    _, h = lax.associative_scan(_lru_combine, (a, u), axis=1)
    return (h * jax.nn.gelu(y_gate.astype(jnp.float32))).astype(y_gate.dtype)


def rotary(x):
    s, d = x.shape[1], x.shape[-1]
    inv = 1.0 / (ROPE_BASE ** jnp.linspace(0.0, 1.0, d // 2, dtype=jnp.float32))
    ang = jnp.arange(s, dtype=jnp.float32)[:, None] * inv[None]
    cos = jnp.cos(ang)[None, :, None]
    sin = jnp.sin(ang)[None, :, None]
    x1 = x[..., 0::2]
    x2 = x[..., 1::2]
    return jnp.stack([x1 * cos - x2 * sin, x1 * sin + x2 * cos], axis=-1).reshape(x.shape)


def retention_chunked(q, k, v, log_gamma):
    b, s, h, dk = q.shape
    dv = v.shape[-1]
    c, L = s // CHUNK, CHUNK
    q = q.reshape(b, c, L, h, dk)
    k = k.reshape(b, c, L, h, dk)
    v = v.reshape(b, c, L, h, dv)
    idx = jnp.arange(L, dtype=jnp.float32)
    diff = idx[:, None] - idx[None, :]
    dmask = jnp.where(diff[None] >= 0, jnp.exp(jnp.maximum(diff, 0.0)[None] * log_gamma[:, None, None]), 0.0)
    scores = jnp.einsum('bclhd,bcshd->bchls', q, k) * dmask[None, None]
    inner = jnp.einsum('bchls,bcshe->bclhe', scores, v)
    k_decay = jnp.exp((L - 1.0 - idx)[:, None] * log_gamma[None])
    chunk_kv = jnp.einsum('bclhd,lh,bclhe->bchde', k, k_decay, v)
    chunk_decay = jnp.exp(L * log_gamma)

    def step(state, kv):
        return state * chunk_decay[:, None, None] + kv, state

    init = jnp.zeros((b, h, dk, dv), dtype=jnp.float32)
    _, s_prev = lax.scan(step, init, jnp.moveaxis(chunk_kv, 1, 0))
    s_prev = jnp.moveaxis(s_prev, 0, 1)
    q_decay = jnp.exp((idx + 1.0)[:, None] * log_gamma[None])
    cross = jnp.einsum('bclhd,lh,bchde->bclhe', q, q_decay, s_prev)
    return (inner + cross).reshape(b, s, h, dv)


def retention_mixer(q, k, v, g):
    b, s, _ = q.shape
    qh = rotary(q.reshape(b, s, RET_HEADS, RET_QK_DIM).astype(jnp.float32))
    kh = rotary(k.reshape(b, s, RET_HEADS, RET_QK_DIM).astype(jnp.float32)) * (RET_QK_DIM ** -0.5)
    vh = v.reshape(b, s, RET_HEADS, RET_V_DIM).astype(jnp.float32)
    log_gamma = jnp.log(1.0 - jnp.exp2(-5.0 - jnp.arange(RET_HEADS, dtype=jnp.float32)))
    y = retention_chunked(qh, kh, vh, log_gamma)
    y = y * lax.rsqrt(jnp.mean(y * y, axis=-1, keepdims=True) + EPS)
    y = y.reshape(b, s, RET_WIDTH) * jax.nn.silu(g.astype(jnp.float32))
    return y.astype(q.dtype)


def conv_ffn(h, w_in, conv_w, conv_b, w_out):
    a, u = jnp.split(h @ w_in, 2, axis=-1)
    a = causal_dwconv(a, conv_w, conv_b)
    return (jax.nn.gelu(a) * u) @ w_out


def setup_inputs(seed: int = 0) -> dict:
    key = jax.random.key(seed)
    ks = jax.random.split(key, 24)
    f32 = jnp.float32
    nrm = lambda k, shape, scale: scale * jax.random.normal(k, shape, dtype=f32)
    x = jax.random.normal(ks[0], (BATCH, SEQ, D_MODEL), dtype=f32)
    norm_mix = 1.0 + nrm(ks[1], (DEPTH, D_MODEL), 0.02)
    w_in = nrm(ks[2], (DEPTH, D_MODEL, IN_PROJ_DIM), D_MODEL ** -0.5)
    ssd_conv_w = nrm(ks[3], (DEPTH, SSD_CONV, SSD_CONV_DIM), SSD_CONV ** -0.5)
    ssd_conv_b = nrm(ks[4], (DEPTH, SSD_CONV_DIM), 0.02)
    dt0 = jnp.exp(jax.random.uniform(ks[5], (DEPTH, SSD_HEADS), dtype=f32,
                                     minval=math.log(1e-3), maxval=math.log(1e-1)))
    ssd_dt_bias = dt0 + jnp.log(-jnp.expm1(-dt0))
    ssd_a_log = jnp.log(jax.random.uniform(ks[6], (DEPTH, SSD_HEADS), dtype=f32, minval=1.0, maxval=16.0))
    ssd_d = 1.0 + nrm(ks[7], (DEPTH, SSD_HEADS), 0.1)
    ssd_norm = 1.0 + nrm(ks[8], (DEPTH, SSD_INNER), 0.02)
    lru_conv_w = nrm(ks[9], (DEPTH, LRU_CONV, LRU_WIDTH), LRU_CONV ** -0.5)
    lru_conv_b = nrm(ks[10], (DEPTH, LRU_WIDTH), 0.02)
    lru_w_a = nrm(ks[11], (DEPTH, LRU_BLOCKS, LRU_BLOCK, LRU_BLOCK), LRU_BLOCK ** -0.5)
    lru_b_a = nrm(ks[12], (DEPTH, LRU_WIDTH), 0.02)
    lru_w_i = nrm(ks[13], (DEPTH, LRU_BLOCKS, LRU_BLOCK, LRU_BLOCK), LRU_BLOCK ** -0.5)
    lru_b_i = nrm(ks[14], (DEPTH, LRU_WIDTH), 0.02)
    a8 = jax.random.uniform(ks[15], (DEPTH, LRU_WIDTH), dtype=f32, minval=0.9, maxval=0.999)
    a0 = a8 ** (1.0 / LRU_C)
    lru_lambda = jnp.log(a0) - jnp.log1p(-a0)
    w_branch = nrm(ks[16], (DEPTH, N_BRANCHES, D_MODEL, D_MODEL), D_MODEL ** -0.5)
    w_o = nrm(ks[17], (DEPTH, D_MODEL, D_MODEL), D_MODEL ** -0.5)
    norm_ffn = 1.0 + nrm(ks[18], (DEPTH, D_MODEL), 0.02)
    ffn_w_in = nrm(ks[19], (DEPTH, D_MODEL, 2 * FFN_DIM), D_MODEL ** -0.5)
    ffn_conv_w = nrm(ks[20], (DEPTH, FFN_CONV, FFN_DIM), FFN_CONV ** -0.5)
    ffn_conv_b = nrm(ks[21], (DEPTH, FFN_DIM), 0.02)
    ffn_w_out = nrm(ks[22], (DEPTH, FFN_DIM, D_MODEL), FFN_DIM ** -0.5)
    norm_final = 1.0 + nrm(ks[23], (D_MODEL,), 0.02)
    return {'x': x, 'norm_mix': norm_mix, 'w_in': w_in,
            'ssd_conv_w': ssd_conv_w, 'ssd_conv_b': ssd_conv_b, 'ssd_dt_bias': ssd_dt_bias,
            'ssd_a_log': ssd_a_log, 'ssd_d': ssd_d, 'ssd_norm': ssd_norm,
            'lru_conv_w': lru_conv_w, 'lru_conv_b': lru_conv_b, 'lru_w_a': lru_w_a, 'lru_b_a': lru_b_a,
            'lru_w_i': lru_w_i, 'lru_b_i': lru_b_i, 'lru_lambda': lru_lambda,
            'w_branch': w_branch, 'w_o': w_o, 'norm_ffn': norm_ffn,
            'ffn_w_in': ffn_w_in, 'ffn_conv_w': ffn_conv_w, 'ffn_conv_b': ffn_conv_b,
            'ffn_w_out': ffn_w_out, 'norm_final': norm_final}


def reference(x, norm_mix, w_in, ssd_conv_w, ssd_conv_b, ssd_dt_bias, ssd_a_log, ssd_d, ssd_norm,
              lru_conv_w, lru_conv_b, lru_w_a, lru_b_a, lru_w_i, lru_b_i, lru_lambda,
              w_branch, w_o, norm_ffn, ffn_w_in, ffn_conv_w, ffn_conv_b, ffn_w_out, norm_final):
    b, s, _ = x.shape
    offsets = np.cumsum(IN_SPLITS)[:-1].tolist()
    for l in range(DEPTH):
        h = rmsnorm(x, norm_mix[l])
        proj = h @ w_in[l]
        (z, xbc, dt_raw, lru_y, lru_x, q, k, v, ret_g, gate_pre) = jnp.split(proj, offsets, axis=-1)
        y_ssd = ssd_mixer(z, xbc, dt_raw, ssd_conv_w[l], ssd_conv_b[l], ssd_dt_bias[l],
                          ssd_a_log[l], ssd_d[l], ssd_norm[l])
        y_lru = rglru_mixer(lru_y, lru_x, lru_conv_w[l], lru_conv_b[l], lru_w_a[l], lru_b_a[l],
                            lru_w_i[l], lru_b_i[l], lru_lambda[l])
        y_ret = retention_mixer(q, k, v, ret_g)
        branches = jnp.stack([y_ssd, y_lru, y_ret], axis=2)
        branch_d = jnp.einsum('bskw,kwd->bskd', branches, w_branch[l])
        gates = jax.nn.sigmoid(gate_pre.reshape(b, s, N_BRANCHES, D_MODEL))
        merged = jnp.sum(gates * branch_d, axis=2)
        x = x + merged @ w_o[l]
        h = rmsnorm(x, norm_ffn[l])
        x = x + conv_ffn(h, ffn_w_in[l], ffn_conv_w[l], ffn_conv_b[l], ffn_w_out[l])
    return rmsnorm(x, norm_final)
```

```python
import math
from contextlib import ExitStack
import numpy as np
import concourse.bass as bass
import concourse.mybir as mybir
from concourse.bass_utils import run_bass_kernel_spmd

F32 = mybir.dt.float32
BF16 = mybir.dt.bfloat16
ALU = mybir.AluOpType
AF = mybir.ActivationFunctionType
NDS = 24
EPS = 1e-6
NT = 512
D = 1024
GAM = [1.0 - 2.0 ** (-5.0 - h) for h in range(4)]


class Buf:
    __slots__ = ("w", "r", "x")

    def __init__(self, excl=False):
        self.w = None
        self.r = []
        self.x = excl


def bufs(n):
    return [Buf() for _ in range(n)]


class Sched:
    def __init__(self, nc, es):
        self.nc = nc
        self.names = ["pe", "act", "dve", "pool", "sp"]
        self.sem = {k: es.enter_context(nc.semaphore("s_" + k)) for k in self.names}
        self.dsem = [es.enter_context(nc.semaphore("d%d" % i)) for i in range(NDS)]
        self.cnt = {k: 0 for k in self.names}
        self.dcnt = [0] * NDS
        self.dnext = 0
        self.dnext_pool = 0
        self.waited = {k: {} for k in self.names}
        self.prog = {k: [] for k in self.names}

    def _semobj(self, key):
        return self.sem[key] if isinstance(key, str) else self.dsem[key[1]]

    def _deps(self, eng, reads, writes, extra=()):
        need = {}
        for b in reads:
            if b.w is not None:
                k, v = b.w
                if need.get(k, 0) < v:
                    need[k] = v
        for b in writes:
            if b.w is not None:
                k, v = b.w
                if need.get(k, 0) < v:
                    need[k] = v
            for (k, v) in b.r:
                if need.get(k, 0) < v:
                    need[k] = v
        for (k, v) in extra:
            if need.get(k, 0) < v:
                need[k] = v
        waits = []
        wd = self.waited[eng]
        for k, v in need.items():
            if k == "pe" and eng == "pe":
                continue
            if wd.get(k, 0) < v:
                waits.append((k, v))
                wd[k] = v
        return waits

    def _commit(self, tok, reads, writes):
        for b in reads:
            b.r.append(tok)
        for b in writes:
            b.w = tok
            b.r = []

    def op(self, eng, fn, reads=(), writes=()):
        if any(b.x for b in reads):
            writes = list(writes) + [b for b in reads if b.x]
            reads = [b for b in reads if not b.x]
        waits = self._deps(eng, reads, writes)
        self.cnt[eng] += 1
        tok = (eng, self.cnt[eng])
        self.prog[eng].append((waits, fn, (eng, 1)))
        self._commit(tok, reads, writes)
        return tok

    def dma(self, eng, out, in_, reads=(), writes=()):
        if eng == "pool":
            j = 16 + self.dnext_pool
            self.dnext_pool = (self.dnext_pool + 1) % 8
        else:
            j = self.dnext
            self.dnext = (self.dnext + 1) % 16
        extra = []
        if self.dcnt[j] > 0:
            extra.append((("d", j), 16 * self.dcnt[j]))
        waits = self._deps(eng, reads, writes, extra)
        self.dcnt[j] += 1
        tok = (("d", j), 16 * self.dcnt[j])

        def fn(e, out=out, in_=in_):
            return e.dma_start(out=out, in_=in_)

        self.prog[eng].append((waits, fn, (("d", j), 16)))
        self._commit(tok, reads, writes)
        return tok

    def barrier(self, with_sp=False):
        ce = ["pe", "act", "dve", "pool"]
        for e in ["act", "dve", "pool"] + (["sp"] if with_sp else []):
            waits = []
            for o in ce:
                if o == e or self.cnt[o] == 0:
                    continue
                if self.waited[e].get(o, 0) < self.cnt[o]:
                    waits.append((o, self.cnt[o]))
                    self.waited[e][o] = self.cnt[o]
            if waits:
                self.prog[e].append((waits, None, None))

    def final_wait(self, eng, bl):
        waits = self._deps(eng, bl, ())
        self.prog[eng].append((waits, None, None))

    def emit(self):
        import bisect
        nc = self.nc
        needed = {k: set() for k in self.names}
        for engname in self.names:
            for waits, fn, inc in self.prog[engname]:
                for (k, v) in waits:
                    if isinstance(k, str):
                        needed[k].add(v)
        ranks = {k: sorted(v) for k, v in needed.items()}

        def semval(k, v):
            if isinstance(k, str):
                return bisect.bisect_right(ranks[k], v)
            return v

        with nc.Block() as block:
            def run(engname, e):
                idx = 0
                for waits, fn, inc in self.prog[engname]:
                    for (k, v) in waits:
                        e.wait_ge(self._semobj(k), semval(k, v))
                    if fn is None:
                        continue
                    ins = fn(e)
                    if isinstance(inc[0], str):
                        idx += 1
                        if idx in needed[engname]:
                            ins.then_inc(self._semobj(inc[0]), 1)
                    else:
                        ins.then_inc(self._semobj(inc[0]), inc[1])

            @block.tensor
            def _(e):
                run("pe", e)

            @block.scalar
            def _(e):
                run("act", e)

            @block.vector
            def _(e):
                run("dve", e)

            @block.gpsimd
            def _(e):
                run("pool", e)

            @block.sync
            def _(e):
                run("sp", e)
        self.stats = {k: (self.cnt[k], len(ranks[k])) for k in self.names}


C_Z, C_XS, C_B, C_C, C_DT = 0, 1024, 2048, 2304, 2560
C_LY, C_LX, C_Q, C_K, C_V, C_G, C_GATE = 2576, 3600, 4624, 5136, 5648, 6672, 7696
_PERM = np.concatenate([np.arange(0, 128, 2), np.arange(1, 128, 2)])
_SWAP = np.concatenate([np.arange(1, 128, 2), np.arange(0, 128, 2)])


def unit_list():
    u = [("XS0", 4096), ("XS1", 4096), ("BC", 4096), ("DT", 128), ("Z0", 4096), ("Z1", 4096),
         ("WB0a", 4096), ("G0a", 4096), ("WB0b", 4096), ("G0b", 4096),
         ("LX0", 4096), ("LX1", 4096), ("LG", 4096), ("LY0", 4096), ("LY1", 4096),
         ("WB1a", 4096), ("G1a", 4096), ("WB1b", 4096), ("G1b", 4096),
         ("Q", 4096), ("QS", 4096), ("K", 4096), ("KS", 4096), ("V0", 4096), ("V1", 4096),
         ("GR0", 4096), ("GR1", 4096),
         ("WB2a", 4096), ("G2a", 4096), ("WB2b", 4096), ("G2b", 4096),
         ("WOa", 4096), ("WOb", 4096)]
    u += [("F%d" % i, 4096) for i in range(12)]
    u += [("FO%d" % i, 3072) for i in range(8)]
    return u


UNITS = unit_list()
WTOT = sum(n for _, n in UNITS)


def _kn(w):
    K, n = w.shape
    return np.ascontiguousarray(w.reshape(K // 128, 128, n).transpose(1, 0, 2)).reshape(128, (K // 128) * n)


def host_wstream(inp, l):
    w_in = inp["w_in"][l]
    wb = inp["w_branch"][l]
    parts = {}
    parts["XS0"] = _kn(w_in[:, C_XS:C_XS + 512])
    parts["XS1"] = _kn(w_in[:, C_XS + 512:C_XS + 1024])
    parts["BC"] = _kn(w_in[:, C_B:C_B + 512])
    parts["DT"] = _kn(w_in[:, C_DT:C_DT + 16])
    parts["Z0"] = _kn(w_in[:, 0:512])
    parts["Z1"] = _kn(w_in[:, 512:1024])
    for k in range(3):
        parts["WB%da" % k] = _kn(wb[k][:, 0:512])
        parts["WB%db" % k] = _kn(wb[k][:, 512:1024])
        parts["G%da" % k] = _kn(w_in[:, C_GATE + k * 1024:C_GATE + k * 1024 + 512])
        parts["G%db" % k] = _kn(w_in[:, C_GATE + k * 1024 + 512:C_GATE + (k + 1) * 1024])
    parts["LX0"] = _kn(w_in[:, C_LX:C_LX + 512])
    parts["LX1"] = _kn(w_in[:, C_LX + 512:C_LX + 1024])
    lg = []
    for m in ("lru_w_a", "lru_w_i"):
        for k in range(4):
            lg.append(_kn(inp[m][l][k]))
    parts["LG"] = np.concatenate(lg, axis=1)
    parts["LY0"] = _kn(w_in[:, C_LY:C_LY + 512])
    parts["LY1"] = _kn(w_in[:, C_LY + 512:C_LY + 1024])
    pcols = np.concatenate([h * 128 + _PERM for h in range(4)])
    scols = np.concatenate([h * 128 + _SWAP for h in range(4)])
    parts["Q"] = _kn(w_in[:, C_Q + pcols])
    parts["QS"] = _kn(w_in[:, C_Q + scols])
    parts["K"] = _kn(w_in[:, C_K + pcols])
    parts["KS"] = _kn(w_in[:, C_K + scols])
    parts["V0"] = _kn(w_in[:, C_V:C_V + 512])
    parts["V1"] = _kn(w_in[:, C_V + 512:C_V + 1024])
    parts["GR0"] = _kn(w_in[:, C_G:C_G + 512])
    parts["GR1"] = _kn(w_in[:, C_G + 512:C_G + 1024])
    wo = inp["w_o"][l]
    parts["WOa"] = _kn(wo[:, 0:512])
    parts["WOb"] = _kn(wo[:, 512:1024])
    fi = inp["ffn_w_in"][l]
    for i in range(12):
        cols = np.concatenate([np.arange(i * 256, i * 256 + 256), 3072 + np.arange(i * 256, i * 256 + 256)])
        parts["F%d" % i] = _kn(fi[:, cols])
    fo = inp["ffn_w_out"][l]
    for i in range(8):
        parts["FO%d" % i] = _kn(fo[:, i * 128:(i + 1) * 128])
    out = np.empty((128, WTOT), np.float32)
    o = 0
    for name, n in UNITS:
        a = parts[name]
        assert a.shape == (128, n), (name, a.shape, n)
        out[:, o:o + n] = a
        o += n
    return out


def _par_layout():
    off = {}
    o = 0
    for name, n in [("g_mix", 8), ("g_ffn", 8), ("scw", 48), ("scb", 12), ("snorm", 8), ("lcw", 32), ("lcb", 8),
                    ("lba", 8), ("lbi", 8), ("llam", 8), ("fcw", 72), ("fcb", 24), ("dtb", 16), ("alog", 16),
                    ("dsk", 16), ("g_fin", 8)]:
        off[name] = o
        o += n
    return off, o


POFF, NPAR = _par_layout()


def _pp(v):
    return np.ascontiguousarray(v.reshape(-1, 128).T)


def host_params(inp, l):
    P = np.zeros((128, NPAR), np.float32)

    def put(name, arr):
        P[:, POFF[name]:POFF[name] + arr.shape[1]] = arr

    put("g_mix", _pp(inp["norm_mix"][l]))
    put("g_ffn", _pp(inp["norm_ffn"][l]))
    scw = inp["ssd_conv_w"][l]
    put("scw", np.stack([_pp(scw[k]) for k in range(4)], axis=2).reshape(128, 48))
    put("scb", _pp(inp["ssd_conv_b"][l]))
    put("snorm", _pp(inp["ssd_norm"][l]))
    lcw = inp["lru_conv_w"][l]
    put("lcw", np.stack([_pp(lcw[k]) for k in range(4)], axis=2).reshape(128, 32))
    put("lcb", _pp(inp["lru_conv_b"][l]))
    put("lba", _pp(inp["lru_b_a"][l]))
    put("lbi", _pp(inp["lru_b_i"][l]))
    put("llam", _pp(inp["lru_lambda"][l]))
    fcw = inp["ffn_conv_w"][l]
    put("fcw", np.stack([_pp(fcw[k]) for k in range(3)], axis=2).reshape(128, 72))
    put("fcb", _pp(inp["ffn_conv_b"][l]))
    put("dtb", np.tile(inp["ssd_dt_bias"][l][None, :], (128, 1)))
    put("alog", np.tile(inp["ssd_a_log"][l][None, :], (128, 1)))
    put("dsk", np.tile(inp["ssd_d"][l][None, :], (128, 1)))
    put("g_fin", _pp(inp["norm_final"]))
    return P


COFF = {"ident": 0, "tri": 128, "dmask": 256, "qd": 768, "kdec": 1280}
NCST = 1284


def host_consts():
    C = np.zeros((128, NCST), np.float32)
    C[:, 0:128] = np.eye(128, dtype=np.float32)
    p = np.arange(128)
    C[:, 128:256] = (p[:, None] <= p[None, :]).astype(np.float32)
    sc = 128.0 ** -0.5
    for h in range(4):
        lg = math.log(GAM[h])
        diff = (p[None, :] - p[:, None]).astype(np.float64)
        dm = np.where(diff >= 0, np.exp(np.maximum(diff, 0) * lg), 0.0) * sc
        C[:, 256 + h * 128:256 + (h + 1) * 128] = dm.astype(np.float32)
        C[:, 768 + h * 128:768 + (h + 1) * 128] = np.exp((p + 1.0) * lg).astype(np.float32)[None, :]
        C[:, 1280 + h] = (np.exp((127.0 - p) * lg) * sc).astype(np.float32)
    return C


def host_rope(S):
    inv = (1.0 / (10000.0 ** np.linspace(0.0, 1.0, 64, dtype=np.float32))).astype(np.float32)
    ang = (np.arange(S, dtype=np.float32)[:, None] * inv[None]).astype(np.float32)
    c = np.cos(ang).astype(np.float32).T
    s = np.sin(ang).astype(np.float32).T
    rc = np.concatenate([c, c], axis=0)
    rs = np.concatenate([-s, s], axis=0)
    return np.ascontiguousarray(rc), np.ascontiguousarray(rs)


class _SkipPhase(Exception):
    pass


class Phase(ExitStack):
    def __init__(self, name, stages):
        super().__init__()
        self.name = name
        self.stages = stages

    def check(self):
        if self.stages is not None and self.name not in self.stages:
            raise _SkipPhase()

    def __exit__(self, et, ev, tb):
        r = super().__exit__(et, ev, tb)
        return r or (et is _SkipPhase)


_PH_UNITS = {"ssd": ("XS", "BC", "DT", "Z", "WB0", "G0"), "lru": ("LX", "LG", "LY", "WB1", "G1"),
             "ret": ("Q", "K", "V", "GR", "WB2", "G2", "WO"), "ffn": ("F",)}


def _unit_phase(name):
    for ph, pre in _PH_UNITS.items():
        for p in pre:
            if name.startswith(p) and not (p == "F" and False):
                return ph
    raise KeyError(name)


def build(NBLK, STAGES=None):
    T = NBLK * NT
    nc = bass.Bass("TRN2", target_bir_lowering=False)
    x_d = nc.dram_tensor("x", [T, D], F32, kind="ExternalInput").ap()
    ws_d = nc.dram_tensor("ws", [2, 128, WTOT], F32, kind="ExternalInput").ap()
    par_d = nc.dram_tensor("par", [2, 128, NPAR], F32, kind="ExternalInput").ap()
    cst_d = nc.dram_tensor("cst", [128, NCST], F32, kind="ExternalInput").ap()
    rc_d = nc.dram_tensor("rcos", [128, T], F32, kind="ExternalInput").ap()
    rs_d = nc.dram_tensor("rsin", [128, T], F32, kind="ExternalInput").ap()
    out_d = nc.dram_tensor("out", [T, D], F32, kind="ExternalOutput").ap()
    wscr = nc.dram_tensor("wscr", [2, 128, WTOT], BF16).ap()

    es = ExitStack()
    with es:
        S = Sched(nc, es)

        uid = [0]

        def on(name):
            return STAGES is None or name in STAGES

        def var(name):
            return STAGES is not None and name in STAGES

        def sb(name, shape, dt=F32, stack=es):
            uid[0] += 1
            return stack.enter_context(nc.sbuf_tensor("%s_%d" % (name, uid[0]), shape, dt))

        cst = sb("cst_sb", [128, NCST]); b_cst = Buf()
        par = [sb("par_sb%d" % l, [128, NPAR]) for l in range(2)]; b_par = bufs(2)
        ident_b = sb("ident_b", [128, 128], BF16)
        ones_b = sb("ones_b", [128, 128], BF16)
        ones_f = sb("ones_f", [128, 128])
        b_c2 = Buf()
        A_bc = [sb("A_bc%d" % l, [128, 16]) for l in range(2)]
        cpar = [sb("cpar%d" % l, [128, 8]) for l in range(2)]
        c2par = [sb("c2par%d" % l, [128, 8]) for l in range(2)]
        b_der = bufs(2)
        Sssd = [sb("Sssd%d" % l, [128, 1024]) for l in range(2)]; b_Sssd = bufs(2)
        Sssd_b = [sb("Sssdb%d" % l, [128, 1024], BF16) for l in range(2)]; b_Sssd_b = bufs(2)
        Sret = [sb("Sret%d" % l, [128, 1024]) for l in range(2)]; b_Sret = bufs(2)
        Sret_b = [sb("Sretb%d" % l, [128, 1024], BF16) for l in range(2)]; b_Sret_b = bufs(2)
        halo_s = [sb("halo_s%d" % l, [128, 12, 3]) for l in range(2)]; b_halo_s = [bufs(12) for _ in range(2)]
        halo_l = [sb("halo_l%d" % l, [128, 8, 3]) for l in range(2)]; b_halo_l = [bufs(8) for _ in range(2)]
        halo_f = [sb("halo_f%d" % l, [128, 24, 2]) for l in range(2)]; b_halo_f = [bufs(24) for _ in range(2)]
        hlru = [sb("hlru%d" % l, [128, 8]) for l in range(2)]; b_hlru = [bufs(8) for _ in range(2)]
        x_fm = sb("x_fm", [128, 8, NT]); b_x = bufs(8)
        hT = sb("hT", [128, 8, NT], BF16); b_hT = bufs(8)
        merged = sb("merged", [128, 8, NT]); b_mg = bufs(8)
        ybr = sb("ybr", [128, 8, NT], BF16); b_ybr = bufs(8)
        rstd = sb("rstd", [128, NT]); b_rstd = Buf()
        lnv = sb("lnv", [128, NT]); b_lnv = Buf()
        sqt = [sb("sqt%d" % i, [128, NT], BF16) for i in range(2)]; b_sqt = bufs(2)
        spt = [sb("spt%d" % i, [128, 64]) for i in range(3)]; b_spt = Buf()
        NSLOT = 4
        wbuf = [sb("wbuf%d" % i, [128, 4096], BF16) for i in range(NSLOT)]; b_wbuf = bufs(NSLOT)
        PB = [es.enter_context(nc.psum_tensor("pb%d" % i, [128, 512], F32)) for i in range(8)]
        b_PB = [Buf(True) for _ in range(8)]

        IDf = cst[:, 0:128]
        TRI = cst[:, 128:256]

        def PE(out, lhsT, rhs, start, stop, R, W):
            S.op("pe", lambda e: e.matmul(out, lhsT=lhsT, rhs=rhs, start=start, stop=stop), R, W)

        def ACT(out, in_, func, R, W, bias=None, scale=None, accum=None):
            kw = {}
            if bias is not None:
                kw["bias"] = bias
            if scale is not None:
                kw["scale"] = scale
            if accum is not None:
                kw["accum_out"] = accum
            S.op("act", lambda e: e.activation(out=out, in_=in_, func=func, **kw), R, W)

        def TT(eng, out, a, b, op, R, W):
            S.op(eng, lambda e: e.tensor_tensor(out=out, in0=a, in1=b, op=op), R, W)

        def TS(eng, out, a, s1, s2, op0, op1, R, W):
            if s2 is None:
                S.op(eng, lambda e: e.tensor_scalar(out=out, in0=a, scalar1=s1, scalar2=None, op0=op0), R, W)
            else:
                S.op(eng, lambda e: e.tensor_scalar(out=out, in0=a, scalar1=s1, scalar2=s2, op0=op0, op1=op1), R, W)

        def STT(out, in0, scalar, in1, op0, op1, R, W):
            S.op("dve", lambda e: e.scalar_tensor_tensor(out=out, in0=in0, scalar=scalar, in1=in1, op0=op0, op1=op1), R, W)

        def SCAN(out, d0, d1, init, R, W):
            S.op("dve", lambda e: e.tensor_tensor_scan(out=out, data0=d0, data1=d1, initial=init, op0=ALU.mult, op1=ALU.add), R, W)

        def CP(eng, out, in_, R, W):
            if eng == "act":
                S.op("act", lambda e: e.copy(out=out, in_=in_), R, W)
            else:
                S.op(eng, lambda e: e.tensor_copy(out=out, in_=in_), R, W)

        def MSET(eng, ap, val, W):
            S.op(eng, lambda e: e.memset(ap, val), (), W)

        def precast():
            PW = 4096
            with Phase('precast', STAGES) as ph:
                ph.check()
                NB_ = 3
                stg = [sb("stg%d" % i, [128, PW], F32, ph) for i in range(NB_)]; b_stg = bufs(NB_)
                stb = [sb("stb%d" % i, [128, PW], BF16, ph) for i in range(NB_)]; b_stb = bufs(NB_)
                n = 0
                for l in range(2):
                    pl = []
                    for lo in range(0, WTOT, PW):
                        hi = min(WTOT, lo + PW)
                        q = n % NB_
                        S.dma("sp", stg[q][:, 0:hi - lo], ws_d[l, :, lo:hi], (), [b_stg[q]])
                        CP(("dve", "pool")[n % 2], stb[q][:, 0:hi - lo], stg[q][:, 0:hi - lo], [b_stg[q]], [b_stb[q]])
                        b = Buf()
                        S.dma("act", wscr[l, :, lo:hi], stb[q][:, 0:hi - lo], [b_stb[q]], [b])
                        pl.append((lo, hi, b))
                        n += 1
                    pieces.append(pl)
                S.final_wait("sp", [b_ for pl in pieces for (_, _, b_) in pl])
                S.barrier(with_sp=True)

        pieces = []

        stream = []
        for blk in range(NBLK):
            for l in range(2):
                o = 0
                for name, n in UNITS:
                    if STAGES is None or _unit_phase(name) in STAGES:
                        stream.append((l, name, o, n))
                    o += n
        wstate = {"next_load": 0, "next_get": 0}
        slot_of = {}

        def w_load_next(slot):
            i = wstate["next_load"]
            if i >= len(stream):
                return
            l, name, o, n = stream[i]
            deps = [b for (lo, hi, b) in pieces[l] if lo < o + n and hi > o]
            S.dma("sp", wbuf[slot][:, 0:n], wscr[l, :, o:o + n], deps, [b_wbuf[slot]])
            slot_of[i] = slot
            wstate["next_load"] = i + 1

        def w_get(l, name):
            i = wstate["next_get"]
            assert stream[i][0] == l and stream[i][1] == name, (stream[i], l, name)
            wstate["next_get"] = i + 1
            s = slot_of[i]
            return s

        def w_rel(slot):
            w_load_next(slot)

        precast()
        S.dma("sp", cst[:], cst_d, (), [b_cst])
        for l in range(2):
            S.dma("sp", par[l][:], par_d[l], (), [b_par[l]])
        for s_ in range(NSLOT):
            w_load_next(s_)
        CP("dve", ident_b[:], IDf, [b_cst], [b_c2])
        MSET("pool", ones_b[:], 1.0, [b_c2])
        MSET("pool", ones_f[:], 1.0, [b_c2])
        for l in range(2):
            P = par[l]
            ACT(A_bc[l][:], P[:, POFF["alog"]:POFF["alog"] + 16], AF.Exp, [b_par[l]], [b_der[l]])
            TS("dve", A_bc[l][:], A_bc[l][:], -1.0, None, ALU.mult, None, [b_der[l]], [b_der[l]])
            MSET("pool", Sssd[l][:], 0.0, [b_Sssd[l]])
            MSET("pool", Sssd_b[l][:], 0.0, [b_Sssd_b[l]])
            MSET("pool", Sret[l][:], 0.0, [b_Sret[l]])
            MSET("pool", Sret_b[l][:], 0.0, [b_Sret_b[l]])
            MSET("pool", halo_s[l][:], 0.0, b_halo_s[l])
            MSET("pool", halo_l[l][:], 0.0, b_halo_l[l])
            MSET("pool", halo_f[l][:], 0.0, b_halo_f[l])
            MSET("pool", hlru[l][:], 0.0, b_hlru[l])

        def softplus(dst, src, n, R, W):
            t0, t1, t2 = spt[0][:, 0:n], spt[1][:, 0:n], spt[2][:, 0:n]
            bs = [b_spt]
            ACT(t0, src, AF.Abs, R, bs)
            ACT(t0, t0, AF.Exp, bs, bs, scale=-1.0)
            TS("dve", t1, t0, 2.0, None, ALU.add, None, bs, bs)
            S.op("dve", lambda e: e.reciprocal(out=t1, in_=t1), bs, bs)
            TT("dve", t0, t0, t1, ALU.mult, bs, bs)
            TT("dve", t1, t0, t0, ALU.mult, bs, bs)
            TS("dve", t2, t1, 1.0 / 9, 1.0 / 7, ALU.mult, ALU.add, bs, bs)
            for cst_ in (1.0 / 5, 1.0 / 3, 1.0):
                TT("dve", t2, t2, t1, ALU.mult, bs, bs)
                TS("dve", t2, t2, cst_, None, ALU.add, None, bs, bs)
            TT("dve", t0, t0, t2, ALU.mult, bs, bs)
            TS("dve", t1, src, 0.0, None, ALU.max, None, list(R) + bs, bs)
            STT(dst, t0, 2.0, t1, ALU.mult, ALU.add, bs, W)

        for l in range(2):
            TS("dve", c2par[l][:], par[l][:, POFF["llam"]:POFF["llam"] + 8], -1.0, None, ALU.mult, None, [b_par[l]], [b_der[l]])
            softplus(cpar[l][:], c2par[l][:], 8, [b_der[l]], [b_der[l]])
            TS("dve", c2par[l][:], cpar[l][:], -16.0, None, ALU.mult, None, [b_der[l]], [b_der[l]])
            TS("dve", cpar[l][:], cpar[l][:], -8.0, None, ALU.mult, None, [b_der[l]], [b_der[l]])

        rr = {"pb": 0, "ev": 0}

        def nextbank(lst):
            i = lst[rr["pb"] % len(lst)]
            rr["pb"] += 1
            return i

        def rmsnorm_to_hT(goff, P, bP):
            bk = 7
            for c in range(8):
                q = c % 2
                ACT(sqt[q][:], x_fm[:, c, :], AF.Square, [b_x[c]], [b_sqt[q]])
                PE(PB[bk][:], ones_b[:], sqt[q][:], c == 0, c == 7, [b_c2, b_sqt[q]], [b_PB[bk]])
            ACT(lnv[:], PB[bk][:], AF.Ln, [b_PB[bk]], [b_lnv], bias=EPS, scale=1.0 / D)
            ACT(rstd[:], lnv[:], AF.Exp, [b_lnv], [b_rstd], scale=-0.5)
            for c in range(8):
                STT(hT[:, c, :], x_fm[:, c, :], P[:, goff + c:goff + c + 1], rstd[:], ALU.mult, ALU.mult,
                    [b_x[c], bP, b_rstd], [b_hT[c]])

        def proj_fm(slot, j, bank, ncols=512, R_extra=()):
            for kc in range(8):
                PE(PB[bank][:], wbuf[slot][:, kc * ncols + j * 128:kc * ncols + (j + 1) * 128], hT[:, kc, :],
                   kc == 0, kc == 7, [b_wbuf[slot], b_hT[kc]], [b_PB[bank]])

        def proj_tm(slot, tc, bank):
            for kc in range(8):
                PE(PB[bank][:], hT[:, kc, tc * 128:(tc + 1) * 128], wbuf[slot][:, kc * 512:(kc + 1) * 512],
                   kc == 0, kc == 7, [b_wbuf[slot], b_hT[kc]], [b_PB[bank]])

        def branch_merge(l, k, gsb, b_gsb, tmpm, b_tmpm, merged_b=None, b_mgb=None):
            for half in range(2):
                sw = w_get(l, "WB%d%s" % (k, "ab"[half]))
                sg = w_get(l, "G%d%s" % (k, "ab"[half]))
                for j in range(4):
                    oc = half * 4 + j
                    bd = nextbank([0, 1, 2, 3])
                    for kc in range(8):
                        PE(PB[bd][:], wbuf[sw][:, kc * 512 + j * 128:kc * 512 + (j + 1) * 128], ybr[:, kc, :],
                           kc == 0, kc == 7, [b_wbuf[sw], b_ybr[kc]], [b_PB[bd]])
                    bg = nextbank([4, 5, 6, 7])
                    proj_fm(sg, j, bg)
                    q = oc % 2
                    gq = gsb[q]
                    tq = tmpm[q][:, 0:NT]
                    ACT(gq, PB[bg][:], AF.Sigmoid, [b_PB[bg]], [b_gsb[q]])
                    if k == 0:
                        TT("dve", merged[:, oc, :], gq, PB[bd][:], ALU.mult, [b_gsb[q], b_PB[bd]], [b_mg[oc]])
                    else:
                        TT("dve", tq, gq, PB[bd][:], ALU.mult, [b_gsb[q], b_PB[bd]], [b_tmpm[q]])
                        if k == 1:
                            TT("pool", merged[:, oc, :], merged[:, oc, :], tq, ALU.add, [b_mg[oc], b_tmpm[q]], [b_mg[oc]])
                        else:
                            TT("pool", merged_b[:, oc, :], merged[:, oc, :], tq, ALU.add, [b_mg[oc], b_tmpm[q]], [b_mgb[oc]])
                w_rel(sw)
                w_rel(sg)

        b_out = Buf()
        for blk in range(NBLK):
            t0 = blk * NT
            with Phase('xload', STAGES) as ph:
                ph.check()
                xtm = [sb("xtm%d" % i, [128, D], F32, ph) for i in range(4)]; b_xtm = bufs(4)
                for tc in range(4):
                    S.dma("sp", xtm[tc][:], x_d[t0 + tc * 128:t0 + (tc + 1) * 128, :], (), [b_xtm[tc]])
                for c in range(8):
                    bk = c % 4
                    for tc in range(4):
                        PE(PB[bk][:, tc * 128:(tc + 1) * 128], xtm[tc][:, c * 128:(c + 1) * 128], IDf, True, True,
                           [b_xtm[tc], b_cst], [b_PB[bk]])
                    CP("act" if c % 2 == 0 else "dve", x_fm[:, c, :], PB[bk][:], [b_PB[bk]], [b_x[c]])
                S.barrier()

            for l in range(2):
                P = par[l]; bP = b_par[l]
                rmsnorm_to_hT(POFF["g_mix"], P, bP)

                with Phase('ssd', STAGES) as ph:
                    ph.check()
                    raw = [sb("raw%d" % i, [128, NT + 3], F32, ph) for i in range(2)]; b_raw = bufs(2)
                    acc = [sb("acc%d" % i, [128, NT], F32, ph) for i in range(2)]; b_acc = bufs(2)
                    xbc = sb("xbc", [128, 12, NT], BF16, ph); b_xbc = bufs(12)
                    dt_tm = sb("dt_tm", [128, 4, 16], F32, ph); b_dt = Buf()
                    a_tm = sb("a_tm", [128, 4, 16], F32, ph); b_a = Buf()
                    xdt = sb("xdt", [128, 1024], BF16, ph); b_xdt = Buf()
                    xdsd = [sb("xdsd%d" % i, [128, 1024], BF16, ph) for i in range(2)]; b_xdsd = bufs(2)
                    xsD = sb("xsD", [128, 1024], F32, ph); b_xsD = Buf()
                    B_tm = [sb("B_tm%d" % i, [128, 256], BF16, ph) for i in range(2)]; b_Btm = bufs(2)
                    CBm = sb("CBm", [128, 2, 128], F32, ph); b_CBm = Buf()
                    Et = [sb("Et%d" % i, [128, 4, 128], F32, ph) for i in range(2)]; b_Et = bufs(2)
                    MT = sb("MT", [128, 16, 128], BF16, ph); b_MT = bufs(4)
                    smA = [sb("smA%d" % i, [128, 8, 16], F32, ph) for i in range(2)]; b_smA = [bufs(8) for _ in range(2)]
                    smB = sb("smB", [128, 4], F32, ph); b_smB = Buf()
                    t1 = sb("t1", [128, 1024], F32, ph); b_t1 = Buf()
                    yd = [sb("yd%d" % i, [128, 1024], F32, ph) for i in range(2)]; b_yd = bufs(2)
                    silz = [sb("silz%d" % i, [128, 1024], F32, ph) for i in range(2)]; b_silz = bufs(2)
                    yn2 = [sb("yn%d" % i, [128, 1024], BF16, ph) for i in range(2)]; b_yn2 = bufs(2)
                    gsb = [r_[:, 0:NT] for r_ in raw]; b_gsb = b_raw
                    tmpm = acc; b_tmpm = b_acc

                    for ui, uname in enumerate(("XS0", "XS1", "BC")):
                        sl = w_get(l, uname)
                        for j in range(4):
                            cc = ui * 4 + j
                            bk = nextbank([0, 1, 2, 3])
                            proj_fm(sl, j, bk)
                            q = cc % 2
                            CP("act", raw[q][:, 3:NT + 3], PB[bk][:], [b_PB[bk]], [b_raw[q]])
                            CP("pool", raw[q][:, 0:3], halo_s[l][:, cc, :], [b_halo_s[l][cc]], [b_raw[q]])
                            CP("pool", halo_s[l][:, cc, :], raw[q][:, NT:NT + 3], [b_raw[q]], [b_halo_s[l][cc]])
                            wo_ = POFF["scw"] + cc * 4
                            TS("dve", acc[q][:], raw[q][:, 0:NT], P[:, wo_:wo_ + 1], P[:, POFF["scb"] + cc:POFF["scb"] + cc + 1],
                               ALU.mult, ALU.add, [b_raw[q], bP], [b_acc[q]])
                            for k in range(1, 4):
                                STT(acc[q][:], raw[q][:, k:k + NT], P[:, wo_ + k:wo_ + k + 1], acc[q][:], ALU.mult, ALU.add,
                                    [b_raw[q], bP, b_acc[q]], [b_acc[q]])
                            ACT(xbc[:, cc, :], acc[q][:], AF.Silu, [b_acc[q]], [b_xbc[cc]])
                        w_rel(sl)
                    sl = w_get(l, "DT")
                    for tc in range(4):
                        for kc in range(8):
                            PE(PB[4][:, tc * 16:(tc + 1) * 16], hT[:, kc, tc * 128:(tc + 1) * 128], wbuf[sl][:, kc * 16:(kc + 1) * 16],
                               kc == 0, kc == 7, [b_wbuf[sl], b_hT[kc]], [b_PB[4]])
                    w_rel(sl)
                    TT("dve", dt_tm[:], PB[4][:, 0:64].rearrange("p (a b) -> p a b", a=4),
                       P[:, POFF["dtb"]:POFF["dtb"] + 16].unsqueeze(1).to_broadcast([128, 4, 16]), ALU.add, [b_PB[4], bP], [b_dt])
                    dtv = dt_tm[:].rearrange("p a b -> p (a b)")
                    softplus(dtv, dtv, 64, [b_dt], [b_dt])
                    TT("dve", a_tm[:], dt_tm[:], A_bc[l][:].unsqueeze(1).to_broadcast([128, 4, 16]), ALU.mult, [b_dt, b_der[l]], [b_a])
                    sz0 = w_get(l, "Z0")
                    sz1 = w_get(l, "Z1")
                    def ssd_A(tc):
                        p = tc % 2
                        ts_ = slice(tc * 128, (tc + 1) * 128)
                        sm = smA[p]; b_sm = b_smA[p]
                        for zh, sz in enumerate((sz0, sz1)):
                            proj_tm(sz, tc, zh)
                            ACT(silz[p][:, zh * 512:(zh + 1) * 512], PB[zh][:], AF.Silu, [b_PB[zh]], [b_silz[p]])
                        for c in range(8):
                            bk = 2 + c // 4
                            PE(PB[bk][:, (c % 4) * 128:(c % 4 + 1) * 128], xbc[:, c, ts_], ident_b[:], True, True,
                               [b_xbc[c], b_c2], [b_PB[bk]])
                        for hh in range(2):
                            TT("dve", xdt[:, hh * 512:(hh + 1) * 512].rearrange("p (h d) -> p h d", h=8),
                               PB[2 + hh][:].rearrange("p (h d) -> p h d", h=8),
                               dt_tm[:, tc, hh * 8:(hh + 1) * 8].unsqueeze(2).to_broadcast([128, 8, 64]), ALU.mult,
                               [b_PB[2 + hh], b_dt], [b_xdt])
                            TT("dve", xsD[:, hh * 512:(hh + 1) * 512].rearrange("p (h d) -> p h d", h=8),
                               PB[2 + hh][:].rearrange("p (h d) -> p h d", h=8),
                               P[:, POFF["dsk"] + hh * 8:POFF["dsk"] + (hh + 1) * 8].unsqueeze(2).to_broadcast([128, 8, 64]), ALU.mult,
                               [b_PB[2 + hh], bP], [b_xsD])
                        for g in range(2):
                            PE(PB[4][:, g * 128:(g + 1) * 128], xbc[:, 8 + g, ts_], ident_b[:], True, True, [b_xbc[8 + g], b_c2], [b_PB[4]])
                        for g in range(2):
                            PE(PB[4][:, 256 + g * 128:256 + (g + 1) * 128], xbc[:, 8 + g, ts_], xbc[:, 10 + g, ts_], True, True,
                               [b_xbc[8 + g], b_xbc[10 + g]], [b_PB[4]])
                        CP("act", B_tm[p][:], PB[4][:, 0:256], [b_PB[4]], [b_Btm[p]])
                        TT("dve", CBm[:], PB[4][:, 256:512].rearrange("p (g l) -> p g l", g=2),
                           TRI.unsqueeze(1).to_broadcast([128, 2, 128]), ALU.mult, [b_PB[4], b_cst], [b_CBm])
                        PE(PB[5][:, 0:16], TRI, a_tm[:, tc, :], True, True, [b_cst, b_a], [b_PB[5]])
                        PE(PB[5][:, 16:32], ones_f[:], a_tm[:, tc, :], True, True, [b_c2, b_a], [b_PB[5]])
                        TS("dve", sm[:, 0, :], PB[5][:, 0:16], -1.0, None, ALU.mult, None, [b_PB[5]], [b_sm[0]])
                        CP("dve", sm[:, 1, :], PB[5][:, 0:16], [b_PB[5]], [b_sm[1]])
                        ACT(sm[:, 2, :], PB[5][:, 0:16], AF.Exp, [b_PB[5]], [b_sm[2]])
                        TT("dve", sm[:, 3, :], PB[5][:, 16:32], sm[:, 1, :], ALU.subtract, [b_PB[5], b_sm[1]], [b_sm[3]])
                        ACT(sm[:, 4, :], sm[:, 3, :], AF.Exp, [b_sm[3]], [b_sm[4]])
                        ACT(sm[:, 5, :], PB[5][:, 16:32], AF.Exp, [b_PB[5]], [b_sm[5]])
                        for hg in range(4):
                            bk = 6 + hg % 2
                            for i in range(4):
                                h = hg * 4 + i
                                PE(PB[bk][:, i * 128:(i + 1) * 128], a_tm[:, tc, h:h + 1].to_broadcast([128, 128]), TRI, True, True,
                                   [b_a, b_cst], [b_PB[bk]])
                            TT("dve", Et[hg % 2][:], PB[bk][:].rearrange("p (h l) -> p h l", h=4),
                               sm[:, 1, hg * 4:(hg + 1) * 4].unsqueeze(2).to_broadcast([128, 4, 128]), ALU.subtract,
                               [b_PB[bk], b_sm[1]], [b_Et[hg % 2]])
                            ACT(Et[hg % 2][:], Et[hg % 2][:], AF.Abs, [b_Et[hg % 2]], [b_Et[hg % 2]])
                            ACT(Et[hg % 2][:], Et[hg % 2][:], AF.Exp, [b_Et[hg % 2]], [b_Et[hg % 2]], scale=-1.0)
                            TT("dve", MT[:, hg * 4:(hg + 1) * 4, :], Et[hg % 2][:],
                               CBm[:, hg // 2, :].unsqueeze(1).to_broadcast([128, 4, 128]), ALU.mult,
                               [b_Et[hg % 2], b_CBm], [b_MT[hg]])
                        for h in range(16):
                            bk = h // 8
                            PE(PB[bk][:, (h % 8) * 64:(h % 8 + 1) * 64], MT[:, h, :], xdt[:, h * 64:(h + 1) * 64], True, True,
                               [b_MT[h // 4], b_xdt], [b_PB[bk]])
                        for g in range(2):
                            TT("dve", yd[p][:, g * 512:(g + 1) * 512], PB[g][:], xsD[:, g * 512:(g + 1) * 512], ALU.add,
                               [b_PB[g], b_xsD], [b_yd[p]])
                        TT("pool", xdsd[p][:].rearrange("p (h d) -> p h d", h=16), xdt[:].rearrange("p (h d) -> p h d", h=16),
                           sm[:, 4, :].unsqueeze(2).to_broadcast([128, 16, 64]), ALU.mult, [b_xdt, b_sm[4]], [b_xdsd[p]])

                    def ssd_B(tc):
                        p = tc % 2
                        ts_ = slice(tc * 128, (tc + 1) * 128)
                        sm = smA[p]; b_sm = b_smA[p]
                        for g in range(2):
                            PE(PB[2 + g][:], xbc[:, 10 + g, ts_], Sssd_b[l][:, g * 512:(g + 1) * 512], True, True,
                               [b_xbc[10 + g], b_Sssd_b[l]], [b_PB[2 + g]])
                        for g in range(2):
                            PE(PB[6 + g][:], B_tm[p][:, g * 128:(g + 1) * 128], xdsd[p][:, g * 512:(g + 1) * 512], True, True,
                               [b_Btm[p], b_xdsd[p]], [b_PB[6 + g]])
                        TT("dve", Sssd[l][:].rearrange("p (h d) -> p h d", h=16), Sssd[l][:].rearrange("p (h d) -> p h d", h=16),
                           sm[:, 5, :].unsqueeze(2).to_broadcast([128, 16, 64]), ALU.mult, [b_Sssd[l], b_sm[5]], [b_Sssd[l]])
                        for g in range(2):
                            TT("dve", Sssd[l][:, g * 512:(g + 1) * 512], Sssd[l][:, g * 512:(g + 1) * 512], PB[6 + g][:], ALU.add,
                               [b_Sssd[l], b_PB[6 + g]], [b_Sssd[l]])
                        CP("act", Sssd_b[l][:], Sssd[l][:], [b_Sssd[l]], [b_Sssd_b[l]])
                        for g in range(2):
                            TT("dve", t1[:, g * 512:(g + 1) * 512].rearrange("p (h d) -> p h d", h=8),
                               PB[2 + g][:].rearrange("p (h d) -> p h d", h=8),
                               sm[:, 2, g * 8:(g + 1) * 8].unsqueeze(2).to_broadcast([128, 8, 64]), ALU.mult,
                               [b_PB[2 + g], b_sm[2]], [b_t1])
                        TT("dve", t1[:], t1[:], yd[p][:], ALU.add, [b_t1, b_yd[p]], [b_t1])
                        TT("dve", t1[:], t1[:], silz[p][:], ALU.mult, [b_t1, b_silz[p]], [b_t1])
                        yn = yn2[p]; b_yn = b_yn2[p]
                        MSET("pool", smB[:, 0:1], 0.0, [b_smB])
                        ACT(yn[:], t1[:], AF.Square, [b_t1], [b_yn, b_smB], accum=smB[:, 0:1])
                        ACT(smB[:, 1:2], smB[:, 0:1], AF.Ln, [b_smB], [b_smB], bias=EPS, scale=1.0 / 1024)
                        ACT(smB[:, 2:3], smB[:, 1:2], AF.Exp, [b_smB], [b_smB], scale=-0.5)
                        TS("dve", yn[:], t1[:], smB[:, 2:3], None, ALU.mult, None, [b_t1, b_smB], [b_yn])

                    def ssd_B2(tc):
                        p = tc % 2
                        ts_ = slice(tc * 128, (tc + 1) * 128)
                        yn = yn2[p]; b_yn = b_yn2[p]
                        for c in range(8):
                            bk = 4 + c // 4
                            PE(PB[bk][:, (c % 4) * 128:(c % 4 + 1) * 128], yn[:, c * 128:(c + 1) * 128], ident_b[:], True, True,
                               [b_yn, b_c2], [b_PB[bk]])
                        for c in range(8):
                            bk = 4 + c // 4
                            TS("dve", ybr[:, c, ts_], PB[bk][:, (c % 4) * 128:(c % 4 + 1) * 128],
                               P[:, POFF["snorm"] + c:POFF["snorm"] + c + 1], None, ALU.mult, None, [b_PB[bk], bP], [b_ybr[c]])

                    ssd_A(0)
                    for tc in range(4):
                        if tc < 3:
                            ssd_A(tc + 1)
                        if tc > 0:
                            ssd_B2(tc - 1)
                        ssd_B(tc)
                    ssd_B2(3)
                    w_rel(sz0)
                    w_rel(sz1)
                    branch_merge(l, 0, gsb, b_gsb, tmpm, b_tmpm)
                    S.barrier()

                with Phase('lru', STAGES) as ph:
                    ph.check()
                    raw = [sb("lraw%d" % i, [128, NT + 3], F32, ph) for i in range(2)]; b_raw = bufs(2)
                    xc = sb("xc", [128, 8, NT], F32, ph); b_xc = bufs(8)
                    xcb = sb("xcb", [128, 8, NT], BF16, ph); b_xcb = bufs(8)
                    rt = [sb("rt%d" % i, [128, NT], F32, ph) for i in range(2)]; b_rt = bufs(2)
                    it = [sb("it%d" % i, [128, NT], F32, ph) for i in range(2)]; b_it = bufs(2)
                    at = [sb("at%d" % i, [128, NT], F32, ph) for i in range(2)]; b_at = bufs(2)
                    a2t = [sb("a2t%d" % i, [128, NT], F32, ph) for i in range(2)]; b_a2t = bufs(2)
                    ut = [sb("ut%d" % i, [128, NT], F32, ph) for i in range(2)]; b_ut = bufs(2)
                    hs = [sb("hs%d" % i, [128, NT], F32, ph) for i in range(2)]; b_hs = bufs(2)
                    gl = [sb("gl%d" % i, [128, NT], F32, ph) for i in range(2)]; b_gl = bufs(2)
                    gsb = [r_[:] for r_ in rt]; b_gsb = b_rt
                    tmpm = it; b_tmpm = b_it
                    for ui, uname in enumerate(("LX0", "LX1")):
                        sl = w_get(l, uname)
                        for j in range(4):
                            c = ui * 4 + j
                            bk = nextbank([0, 1, 2, 3])
                            proj_fm(sl, j, bk)
                            q = c % 2
                            CP("act", raw[q][:, 3:NT + 3], PB[bk][:], [b_PB[bk]], [b_raw[q]])
                            CP("pool", raw[q][:, 0:3], halo_l[l][:, c, :], [b_halo_l[l][c]], [b_raw[q]])
                            CP("pool", halo_l[l][:, c, :], raw[q][:, NT:NT + 3], [b_raw[q]], [b_halo_l[l][c]])
                            wo_ = POFF["lcw"] + c * 4
                            TS("dve", xc[:, c, :], raw[q][:, 0:NT], P[:, wo_:wo_ + 1], P[:, POFF["lcb"] + c:POFF["lcb"] + c + 1],
                               ALU.mult, ALU.add, [b_raw[q], bP], [b_xc[c]])
                            for k in range(1, 4):
                                STT(xc[:, c, :], raw[q][:, k:k + NT], P[:, wo_ + k:wo_ + k + 1], xc[:, c, :], ALU.mult, ALU.add,
                                    [b_raw[q], bP, b_xc[c]], [b_xc[c]])
                            CP("act", xcb[:, c, :], xc[:, c, :], [b_xc[c]], [b_xcb[c]])
                        w_rel(sl)
                    slg = w_get(l, "LG")
                    sly = [w_get(l, "LY0"), w_get(l, "LY1")]
                    for c in range(8):
                        k, j, q = c // 2, c % 2, c % 2
                        for m, (dst, bdst, boff) in enumerate(((rt, b_rt, "lba"), (it, b_it, "lbi"))):
                            bk = nextbank([0, 1, 2, 3])
                            base = (m * 4 + k) * 512
                            for kc in range(2):
                                PE(PB[bk][:], wbuf[slg][:, base + kc * 256 + j * 128:base + kc * 256 + (j + 1) * 128], xcb[:, 2 * k + kc, :],
                                   kc == 0, kc == 1, [b_wbuf[slg], b_xcb[2 * k + kc]], [b_PB[bk]])
                            ACT(dst[q][:], PB[bk][:], AF.Sigmoid, [b_PB[bk], bP], [bdst[q]], bias=P[:, POFF[boff] + c:POFF[boff] + c + 1])
                        ACT(at[q][:], rt[q][:], AF.Exp, [b_rt[q], b_der[l]], [b_at[q]], scale=cpar[l][:, c:c + 1])
                        ACT(a2t[q][:], rt[q][:], AF.Exp, [b_rt[q], b_der[l]], [b_a2t[q]], scale=c2par[l][:, c:c + 1])
                        ACT(a2t[q][:], a2t[q][:], AF.Sqrt, [b_a2t[q]], [b_a2t[q]], bias=1.0, scale=-1.0)
                        TT("dve", ut[q][:], it[q][:], xc[:, c, :], ALU.mult, [b_it[q], b_xc[c]], [b_ut[q]])
                        TT("dve", ut[q][:], ut[q][:], a2t[q][:], ALU.mult, [b_ut[q], b_a2t[q]], [b_ut[q]])
                        SCAN(hs[q][:], at[q][:], ut[q][:], hlru[l][:, c:c + 1], [b_at[q], b_ut[q], b_hlru[l][c]], [b_hs[q]])
                        CP("pool", hlru[l][:, c:c + 1], hs[q][:, NT - 1:NT], [b_hs[q]], [b_hlru[l][c]])
                        bk = nextbank([4, 5, 6, 7])
                        proj_fm(sly[c // 4], c % 4, bk)
                        ACT(gl[q][:], PB[bk][:], AF.Gelu, [b_PB[bk]], [b_gl[q]])
                        TT("dve", ybr[:, c, :], hs[q][:], gl[q][:], ALU.mult, [b_hs[q], b_gl[q]], [b_ybr[c]])
                    w_rel(slg)
                    w_rel(sly[0])
                    w_rel(sly[1])
                    branch_merge(l, 1, gsb, b_gsb, tmpm, b_tmpm)
                    S.barrier()

                with Phase('ret', STAGES) as ph:
                    ph.check()
                    qr = sb("qr", [128, 4, NT], BF16, ph); b_qr = bufs(4)
                    qrd = sb("qrd", [128, 4, NT], BF16, ph); b_qrd = bufs(4)
                    kr = sb("kr", [128, 4, NT], BF16, ph); b_kr = bufs(4)
                    ra = [sb("ra%d" % i, [128, NT], F32, ph) for i in range(2)]; b_ra = bufs(2)
                    rb = [sb("rb%d" % i, [128, NT], F32, ph) for i in range(2)]; b_rb = bufs(2)
                    v_tm2 = [sb("v_tm%d" % i, [128, 1024], BF16, ph) for i in range(2)]; b_v2 = bufs(2)
                    silg2 = [sb("silg%d" % i, [128, 1024], F32, ph) for i in range(2)]; b_silg2 = bufs(2)
                    PT2 = [sb("PT%d" % i, [128, 4, 128], BF16, ph) for i in range(2)]; b_PT2 = bufs(2)
                    k_tm2 = [sb("k_tm%d" % i, [128, 4, 128], BF16, ph) for i in range(2)]; b_ktm2 = bufs(2)
                    ryn2 = [sb("ryn%d" % i, [128, 1024], BF16, ph) for i in range(2)]; b_ryn2 = bufs(2)
                    junk = sb("rjunk", [128, 256], BF16, ph); b_junk = Buf()
                    ss4 = sb("ss4", [128, 12], F32, ph); b_ss4 = Buf()
                    gsb = [r_[:] for r_ in ra]; b_gsb = b_ra
                    tmpm = rb; b_tmpm = b_rb
                    merged_b = sb("merged_b", [128, 8, NT], BF16, ph); b_mgb = bufs(8)
                    cos_t = sb("cos_t", [128, NT], F32, ph); sin_t = sb("sin_t", [128, NT], F32, ph); b_cs = bufs(2)
                    S.dma("pool", cos_t[:], rc_d[:, t0:t0 + NT], (), [b_cs[0]])
                    S.dma("pool", sin_t[:], rs_d[:, t0:t0 + NT], (), [b_cs[1]])
                    for (un, uns, isq) in (("Q", "QS", True), ("K", "KS", False)):
                        s1 = w_get(l, un)
                        s2 = w_get(l, uns)
                        for h in range(4):
                            if not on("r1"):
                                break
                            q = h % 2
                            ba = nextbank([0, 1, 2, 3])
                            proj_fm(s1, h, ba)
                            bb = nextbank([4, 5, 6, 7])
                            proj_fm(s2, h, bb)
                            TT("dve", ra[q][:], PB[ba][:], cos_t[:], ALU.mult, [b_PB[ba], b_cs[0]], [b_ra[q]])
                            TT("dve", rb[q][:], PB[bb][:], sin_t[:], ALU.mult, [b_PB[bb], b_cs[1]], [b_rb[q]])
                            if isq:
                                TT("dve", ra[q][:], ra[q][:], rb[q][:], ALU.add, [b_ra[q], b_rb[q]], [b_ra[q]])
                                CP("act", qr[:, h, :], ra[q][:], [b_ra[q]], [b_qr[h]])
                                TT("pool", qrd[:, h, :].rearrange("p (a b) -> p a b", a=4), ra[q][:].rearrange("p (a b) -> p a b", a=4),
                                   cst[:, COFF["qd"] + h * 128:COFF["qd"] + (h + 1) * 128].unsqueeze(1).to_broadcast([128, 4, 128]),
                                   ALU.mult, [b_ra[q], b_cst], [b_qrd[h]])
                            else:
                                TT("dve", kr[:, h, :], ra[q][:], rb[q][:], ALU.add, [b_ra[q], b_rb[q]], [b_kr[h]])
                        w_rel(s1)
                        w_rel(s2)
                    sv = [w_get(l, "V0"), w_get(l, "V1")]
                    sg_ = [w_get(l, "GR0"), w_get(l, "GR1")]
                    def ret_A(tc):
                        p = tc % 2
                        ts_ = slice(tc * 128, (tc + 1) * 128)
                        v_tm, b_v, silg, b_silg, PT, b_PT, k_tm, b_ktm = v_tm2[p], b_v2[p], silg2[p], b_silg2[p], PT2[p], b_PT2[p], k_tm2[p], b_ktm2[p]
                        for vh in range(2):
                            proj_tm(sv[vh], tc, vh)
                            CP("act", v_tm[:, vh * 512:(vh + 1) * 512], PB[vh][:], [b_PB[vh]], [b_v])
                        for gh in range(2):
                            proj_tm(sg_[gh], tc, 2 + gh)
                            ACT(silg[:, gh * 512:(gh + 1) * 512], PB[2 + gh][:], AF.Silu, [b_PB[2 + gh]], [b_silg])
                        for h in range(4):
                            PE(PB[4][:, h * 128:(h + 1) * 128], kr[:, h, ts_], qr[:, h, ts_], True, True, [b_kr[h], b_qr[h]], [b_PB[4]])
                        for h in range(4):
                            PE(PB[5][:, h * 128:(h + 1) * 128], kr[:, h, ts_], ident_b[:], True, True, [b_kr[h], b_c2], [b_PB[5]])
                        TT("dve", PT[:].rearrange("p h l -> p (h l)"), PB[4][:], cst[:, COFF["dmask"]:COFF["dmask"] + 512], ALU.mult,
                           [b_PB[4], b_cst], [b_PT])
                        for h in range(4):
                            TS("dve", k_tm[:, h, :], PB[5][:, h * 128:(h + 1) * 128], cst[:, COFF["kdec"] + h:COFF["kdec"] + h + 1], None,
                               ALU.mult, None, [b_PB[5], b_cst], [b_ktm])

                    def ret_B(tc):
                        p = tc % 2
                        ts_ = slice(tc * 128, (tc + 1) * 128)
                        v_tm, b_v, silg, b_silg, PT, b_PT, k_tm, b_ktm = v_tm2[p], b_v2[p], silg2[p], b_silg2[p], PT2[p], b_PT2[p], k_tm2[p], b_ktm2[p]
                        for h in range(4):
                            bk = 6 + h // 2
                            o_ = PB[bk][:, (h % 2) * 256:(h % 2 + 1) * 256]
                            PE(o_, PT[:, h, :], v_tm[:, h * 256:(h + 1) * 256], True, False, [b_PT, b_v], [b_PB[bk]])
                            PE(o_, qrd[:, h, ts_], Sret_b[l][:, h * 256:(h + 1) * 256], False, True, [b_qrd[h], b_Sret_b[l]], [b_PB[bk]])
                        for h in range(4):
                            bk = 2 + h // 2
                            PE(PB[bk][:, (h % 2) * 256:(h % 2 + 1) * 256], k_tm[:, h, :], v_tm[:, h * 256:(h + 1) * 256], True, True,
                               [b_ktm, b_v], [b_PB[bk]])
                        for h in range(4):
                            bk = 2 + h // 2
                            S.op("act", lambda e, o_=Sret[l][:, h * 256:(h + 1) * 256], g_=float(GAM[h] ** 128): e.mul(out=o_, in_=o_, mul=g_),
                                 [b_Sret[l]], [b_Sret[l]])
                            TT("dve", Sret[l][:, h * 256:(h + 1) * 256], Sret[l][:, h * 256:(h + 1) * 256],
                               PB[bk][:, (h % 2) * 256:(h % 2 + 1) * 256], ALU.add, [b_Sret[l], b_PB[bk]], [b_Sret[l]])
                        CP("act", Sret_b[l][:], Sret[l][:], [b_Sret[l]], [b_Sret_b[l]])
                        yn = ryn2[p]; b_yn = b_ryn2[p]
                        MSET("pool", ss4[:, 0:4], 0.0, [b_ss4])
                        for h in range(4):
                            bk = 6 + h // 2
                            ACT(junk[:], PB[bk][:, (h % 2) * 256:(h % 2 + 1) * 256], AF.Square, [b_PB[bk]], [b_junk, b_ss4],
                                accum=ss4[:, h:h + 1])
                        ACT(ss4[:, 4:8], ss4[:, 0:4], AF.Ln, [b_ss4], [b_ss4], bias=EPS, scale=1.0 / 256)
                        ACT(ss4[:, 8:12], ss4[:, 4:8], AF.Exp, [b_ss4], [b_ss4], scale=-0.5)
                        for h in range(4):
                            bk = 6 + h // 2
                            STT(yn[:, h * 256:(h + 1) * 256], PB[bk][:, (h % 2) * 256:(h % 2 + 1) * 256], ss4[:, 8 + h:9 + h],
                                silg[:, h * 256:(h + 1) * 256], ALU.mult, ALU.mult, [b_PB[bk], b_ss4, b_silg], [b_yn])

                    def ret_B2(tc):
                        p = tc % 2
                        ts_ = slice(tc * 128, (tc + 1) * 128)
                        yn = ryn2[p]; b_yn = b_ryn2[p]
                        for c in range(8):
                            bk = c // 4
                            PE(PB[bk][:, (c % 4) * 128:(c % 4 + 1) * 128], yn[:, c * 128:(c + 1) * 128], ident_b[:], True, True,
                               [b_yn, b_c2], [b_PB[bk]])
                        for c in range(8):
                            bk = c // 4
                            if c % 2:
                                CP("act", ybr[:, c, ts_], PB[bk][:, (c % 4) * 128:(c % 4 + 1) * 128], [b_PB[bk]], [b_ybr[c]])
                            else:
                                TS("dve", ybr[:, c, ts_], PB[bk][:, (c % 4) * 128:(c % 4 + 1) * 128], 1.0, None, ALU.mult, None,
                                   [b_PB[bk]], [b_ybr[c]])

                    ret_A(0)
                    for tc in range(4):
                        if tc < 3:
                            ret_A(tc + 1)
                        if tc > 0:
                            ret_B2(tc - 1)
                        ret_B(tc)
                    ret_B2(3)
                    for s_ in sv + sg_:
                        w_rel(s_)
                    branch_merge(l, 2, gsb, b_gsb, tmpm, b_tmpm, merged_b, b_mgb)
                    for half in range(2):
                        sl = w_get(l, "WO" + "ab"[half])
                        for j in range(4):
                            oc = half * 4 + j
                            bk = nextbank([0, 1, 2, 3])
                            for kc in range(8):
                                PE(PB[bk][:], wbuf[sl][:, kc * 512 + j * 128:kc * 512 + (j + 1) * 128], merged_b[:, kc, :],
                                   kc == 0, kc == 7, [b_wbuf[sl], b_mgb[kc]], [b_PB[bk]])
                            TT("dve", x_fm[:, oc, :], x_fm[:, oc, :], PB[bk][:], ALU.add, [b_x[oc], b_PB[bk]], [b_x[oc]])
                        w_rel(sl)
                    S.barrier()

                rmsnorm_to_hT(POFF["g_ffn"], P, bP)
                with Phase('ffn', STAGES) as ph:
                    ph.check()
                    raw = [sb("fraw%d" % i, [128, NT + 2], F32, ph) for i in range(2)]; b_raw = bufs(2)
                    acc = [sb("facc%d" % i, [128, NT], F32, ph) for i in range(2)]; b_acc = bufs(2)
                    gel = [sb("fgel%d" % i, [128, NT], F32, ph) for i in range(2)]; b_gel = bufs(2)
                    g_fm = sb("g_fm", [128, 24, NT], BF16, ph); b_g = bufs(24)
                    for ui in range(12):
                        sl = w_get(l, "F%d" % ui)
                        for jj in range(2):
                            j = ui * 2 + jj
                            q = j % 2
                            ba = nextbank([0, 1, 2, 3])
                            proj_fm(sl, jj, ba)
                            bu = nextbank([4, 5, 6, 7])
                            proj_fm(sl, 2 + jj, bu)
                            CP("act", raw[q][:, 2:NT + 2], PB[ba][:], [b_PB[ba]], [b_raw[q]])
                            CP("pool", raw[q][:, 0:2], halo_f[l][:, j, :], [b_halo_f[l][j]], [b_raw[q]])
                            CP("pool", halo_f[l][:, j, :], raw[q][:, NT:NT + 2], [b_raw[q]], [b_halo_f[l][j]])
                            wo_ = POFF["fcw"] + j * 3
                            TS("dve", acc[q][:], raw[q][:, 0:NT], P[:, wo_:wo_ + 1], P[:, POFF["fcb"] + j:POFF["fcb"] + j + 1],
                               ALU.mult, ALU.add, [b_raw[q], bP], [b_acc[q]])
                            for k in range(1, 3):
                                STT(acc[q][:], raw[q][:, k:k + NT], P[:, wo_ + k:wo_ + k + 1], acc[q][:], ALU.mult, ALU.add,
                                    [b_raw[q], bP, b_acc[q]], [b_acc[q]])
                            ACT(gel[q][:], acc[q][:], AF.Gelu, [b_acc[q]], [b_gel[q]])
                            TT("dve", g_fm[:, j, :], gel[q][:], PB[bu][:], ALU.mult, [b_gel[q], b_PB[bu]], [b_g[j]])
                        w_rel(sl)
                    for oc in range(8):
                        sl = w_get(l, "FO%d" % oc)
                        bk = nextbank([0, 1, 2, 3])
                        for kc in range(24):
                            PE(PB[bk][:], wbuf[sl][:, kc * 128:(kc + 1) * 128], g_fm[:, kc, :], kc == 0, kc == 23,
                               [b_wbuf[sl], b_g[kc]], [b_PB[bk]])
                        TT("dve", x_fm[:, oc, :], x_fm[:, oc, :], PB[bk][:], ALU.add, [b_x[oc], b_PB[bk]], [b_x[oc]])
                        w_rel(sl)
                    S.barrier()

            with Phase('final', STAGES) as ph:
                ph.check()
                yf = sb("yf", [128, 8, NT], F32, ph); b_yf = bufs(8)
                otm = [sb("otm%d" % i, [128, D], F32, ph) for i in range(2)]; b_otm = bufs(2)
                sq32 = [sb("sq32_%d" % i, [128, NT], F32, ph) for i in range(2)]; b_sq32 = bufs(2)
                P = par[0]; bP = b_par[0]
                for c in range(8):
                    q = c % 2
                    ACT(sq32[q][:], x_fm[:, c, :], AF.Square, [b_x[c]], [b_sq32[q]])
                    PE(PB[7][:], ones_f[:], sq32[q][:], c == 0, c == 7, [b_c2, b_sq32[q]], [b_PB[7]])
                ACT(lnv[:], PB[7][:], AF.Ln, [b_PB[7]], [b_lnv], bias=EPS, scale=1.0 / D)
                ACT(rstd[:], lnv[:], AF.Exp, [b_lnv], [b_rstd], scale=-0.5)
                for c in range(8):
                    STT(yf[:, c, :], x_fm[:, c, :], P[:, POFF["g_fin"] + c:POFF["g_fin"] + c + 1], rstd[:], ALU.mult, ALU.mult,
                        [b_x[c], bP, b_rstd], [b_yf[c]])
                for tc in range(4):
                    q = tc % 2
                    for c in range(8):
                        bk = (tc % 2) * 2 + c // 4
                        PE(PB[bk][:, (c % 4) * 128:(c % 4 + 1) * 128], yf[:, c, tc * 128:(tc + 1) * 128], IDf, True, True,
                           [b_yf[c], b_cst], [b_PB[bk]])
                    for hh in range(2):
                        bk = (tc % 2) * 2 + hh
                        CP("act" if hh else "dve", otm[q][:, hh * 512:(hh + 1) * 512], PB[bk][:], [b_PB[bk]], [b_otm[q]])
                    S.dma("sp", out_d[t0 + tc * 128:t0 + (tc + 1) * 128, :], otm[q][:], [b_otm[q]], [b_out])
                S.final_wait("sp", [b_out])
                S.barrier(with_sp=True)
        S.prog["sp"].append(([((("d", j)), 16 * S.dcnt[j]) for j in range(NDS) if S.dcnt[j] > 0], None, None))
        S.emit()
    return nc


def _bias_ap_fix():
    pass


_CACHE = {}


def prep_inputs(inputs, NBLK, batch):
    T = NBLK * NT
    key = "shared"
    ws = np.stack([host_wstream(inputs, l) for l in range(2)], axis=0)
    par = np.stack([host_params(inputs, l) for l in range(2)], axis=0)
    cst = host_consts()
    rc, rs = host_rope(T)
    maps = []
    for b in batch:
        maps.append({"x": np.ascontiguousarray(inputs["x"][b, :T, :]), "ws": ws, "par": par, "cst": cst,
                     "rcos": rc, "rsin": rs})
    return maps


def kernel(**inputs):
    inputs = {k: np.asarray(v) for k, v in inputs.items()}
    B, SEQ, _ = inputs["x"].shape
    NBLK = SEQ // NT
    if NBLK not in _CACHE:
        _CACHE[NBLK] = build(NBLK)
    nc = _CACHE[NBLK]
    n_cores = 8
    batch = [i % B for i in range(n_cores)]
    maps = prep_inputs(inputs, NBLK, batch)
    res = run_bass_kernel_spmd(nc, maps, core_ids=list(range(n_cores)))
    out = np.stack([res.results[b]["out"] for b in range(B)], axis=0)
    return out.astype(np.float32)
```

```python
import math
from contextlib import ExitStack
import numpy as np
import concourse.bass as bass
import concourse.mybir as mybir
from concourse.bass_utils import run_bass_kernel_spmd

F32 = mybir.dt.float32
BF16 = mybir.dt.bfloat16
ALU = mybir.AluOpType
AF = mybir.ActivationFunctionType
NDS = 24
EPS = 1e-6
NT = 512
D = 1024
GAM = [1.0 - 2.0 ** (-5.0 - h) for h in range(4)]


class Buf:
    __slots__ = ("w", "r", "x")

    def __init__(self, excl=False):
        self.w = None
        self.r = []
        self.x = excl


def bufs(n):
    return [Buf() for _ in range(n)]


class Sched:
    def __init__(self, nc, es):
        self.nc = nc
        self.names = ["pe", "act", "dve", "pool", "sp"]
        self.sem = {k: es.enter_context(nc.semaphore("s_" + k)) for k in self.names}
        self.dsem = [es.enter_context(nc.semaphore("d%d" % i)) for i in range(NDS)]
        self.cnt = {k: 0 for k in self.names}
        self.dcnt = [0] * NDS
        self.dnext = 0
        self.dnext_pool = 0
        self.waited = {k: {} for k in self.names}
        self.prog = {k: [] for k in self.names}

    def _semobj(self, key):
        return self.sem[key] if isinstance(key, str) else self.dsem[key[1]]

    def _deps(self, eng, reads, writes, extra=()):
        need = {}
        for b in reads:
            if b.w is not None:
                k, v = b.w
                if need.get(k, 0) < v:
                    need[k] = v
        for b in writes:
            if b.w is not None:
                k, v = b.w
                if need.get(k, 0) < v:
                    need[k] = v
            for (k, v) in b.r:
                if need.get(k, 0) < v:
                    need[k] = v
        for (k, v) in extra:
            if need.get(k, 0) < v:
                need[k] = v
        waits = []
        wd = self.waited[eng]
        for k, v in need.items():
            if k == "pe" and eng == "pe":
                continue
            if wd.get(k, 0) < v:
                waits.append((k, v))
                wd[k] = v
        return waits

    def _commit(self, tok, reads, writes):
        for b in reads:
            b.r.append(tok)
        for b in writes:
            b.w = tok
            b.r = []

    def op(self, eng, fn, reads=(), writes=()):
        if any(b.x for b in reads):
            writes = list(writes) + [b for b in reads if b.x]
            reads = [b for b in reads if not b.x]
        waits = self._deps(eng, reads, writes)
        self.cnt[eng] += 1
        tok = (eng, self.cnt[eng])
        self.prog[eng].append((waits, fn, (eng, 1)))
        self._commit(tok, reads, writes)
        return tok

    def dma(self, eng, out, in_, reads=(), writes=()):
        if eng == "pool":
            j = 16 + self.dnext_pool
            self.dnext_pool = (self.dnext_pool + 1) % 8
        else:
            j = self.dnext
            self.dnext = (self.dnext + 1) % 16
        extra = []
        if self.dcnt[j] > 0:
            extra.append((("d", j), 16 * self.dcnt[j]))
        waits = self._deps(eng, reads, writes, extra)
        self.dcnt[j] += 1
        tok = (("d", j), 16 * self.dcnt[j])

        def fn(e, out=out, in_=in_):
            return e.dma_start(out=out, in_=in_)

        self.prog[eng].append((waits, fn, (("d", j), 16)))
        self._commit(tok, reads, writes)
        return tok

    def barrier(self, with_sp=False):
        ce = ["pe", "act", "dve", "pool"]
        for e in ["act", "dve", "pool"] + (["sp"] if with_sp else []):
            waits = []
            for o in ce:
                if o == e or self.cnt[o] == 0:
                    continue
                if self.waited[e].get(o, 0) < self.cnt[o]:
                    waits.append((o, self.cnt[o]))
                    self.waited[e][o] = self.cnt[o]
            if waits:
                self.prog[e].append((waits, None, None))

    def final_wait(self, eng, bl):
        waits = self._deps(eng, bl, ())
        self.prog[eng].append((waits, None, None))

    def emit(self):
        import bisect
        nc = self.nc
        needed = {k: set() for k in self.names}
        for engname in self.names:
            for waits, fn, inc in self.prog[engname]:
                for (k, v) in waits:
                    if isinstance(k, str):
                        needed[k].add(v)
        ranks = {k: sorted(v) for k, v in needed.items()}

        def semval(k, v):
            if isinstance(k, str):
                return bisect.bisect_right(ranks[k], v)
            return v

        with nc.Block() as block:
            def run(engname, e):
                idx = 0
                for waits, fn, inc in self.prog[engname]:
                    for (k, v) in waits:
                        e.wait_ge(self._semobj(k), semval(k, v))
                    if fn is None:
                        continue
                    ins = fn(e)
                    if isinstance(inc[0], str):
                        idx += 1
                        if idx in needed[engname]:
                            ins.then_inc(self._semobj(inc[0]), 1)
                    else:
                        ins.then_inc(self._semobj(inc[0]), inc[1])

            @block.tensor
            def _(e):
                run("pe", e)

            @block.scalar
            def _(e):
                run("act", e)

            @block.vector
            def _(e):
                run("dve", e)

            @block.gpsimd
            def _(e):
                run("pool", e)

            @block.sync
            def _(e):
                run("sp", e)
        self.stats = {k: (self.cnt[k], len(ranks[k])) for k in self.names}


C_Z, C_XS, C_B, C_C, C_DT = 0, 1024, 2048, 2304, 2560
C_LY, C_LX, C_Q, C_K, C_V, C_G, C_GATE = 2576, 3600, 4624, 5136, 5648, 6672, 7696
_PERM = np.concatenate([np.arange(0, 128, 2), np.arange(1, 128, 2)])
_SWAP = np.concatenate([np.arange(1, 128, 2), np.arange(0, 128, 2)])


def unit_list():
    u = [("XS0", 4096), ("XS1", 4096), ("BC", 4096), ("DT", 128), ("Z0", 4096), ("Z1", 4096),
         ("WB0a", 4096), ("G0a", 4096), ("WB0b", 4096), ("G0b", 4096),
         ("LX0", 4096), ("LX1", 4096), ("LG", 4096), ("LY0", 4096), ("LY1", 4096),
         ("WB1a", 4096), ("G1a", 4096), ("WB1b", 4096), ("G1b", 4096),
         ("Q", 4096), ("QS", 4096), ("K", 4096), ("KS", 4096), ("V0", 4096), ("V1", 4096),
         ("GR0", 4096), ("GR1", 4096),
         ("WB2a", 4096), ("G2a", 4096), ("WB2b", 4096), ("G2b", 4096),
         ("WOa", 4096), ("WOb", 4096)]
    u += [("F%d" % i, 4096) for i in range(12)]
    u += [("FO%d" % i, 3072) for i in range(8)]
    return u


UNITS = unit_list()
WTOT = sum(n for _, n in UNITS)


def _kn(w):
    K, n = w.shape
    return np.ascontiguousarray(w.reshape(K // 128, 128, n).transpose(1, 0, 2)).reshape(128, (K // 128) * n)


def host_wstream(inp, l):
    w_in = inp["w_in"][l]
    wb = inp["w_branch"][l]
    parts = {}
    parts["XS0"] = _kn(w_in[:, C_XS:C_XS + 512])
    parts["XS1"] = _kn(w_in[:, C_XS + 512:C_XS + 1024])
    parts["BC"] = _kn(w_in[:, C_B:C_B + 512])
    parts["DT"] = _kn(w_in[:, C_DT:C_DT + 16])
    parts["Z0"] = _kn(w_in[:, 0:512])
    parts["Z1"] = _kn(w_in[:, 512:1024])
    for k in range(3):
        parts["WB%da" % k] = _kn(wb[k][:, 0:512])
        parts["WB%db" % k] = _kn(wb[k][:, 512:1024])
        parts["G%da" % k] = _kn(w_in[:, C_GATE + k * 1024:C_GATE + k * 1024 + 512])
        parts["G%db" % k] = _kn(w_in[:, C_GATE + k * 1024 + 512:C_GATE + (k + 1) * 1024])
    parts["LX0"] = _kn(w_in[:, C_LX:C_LX + 512])
    parts["LX1"] = _kn(w_in[:, C_LX + 512:C_LX + 1024])
    lg = []
    for m in ("lru_w_a", "lru_w_i"):
        for k in range(4):
            lg.append(_kn(inp[m][l][k]))
    parts["LG"] = np.concatenate(lg, axis=1)
    parts["LY0"] = _kn(w_in[:, C_LY:C_LY + 512])
    parts["LY1"] = _kn(w_in[:, C_LY + 512:C_LY + 1024])
    pcols = np.concatenate([h * 128 + _PERM for h in range(4)])
    scols = np.concatenate([h * 128 + _SWAP for h in range(4)])
    parts["Q"] = _kn(w_in[:, C_Q + pcols])
    parts["QS"] = _kn(w_in[:, C_Q + scols])
    parts["K"] = _kn(w_in[:, C_K + pcols])
    parts["KS"] = _kn(w_in[:, C_K + scols])
    parts["V0"] = _kn(w_in[:, C_V:C_V + 512])
    parts["V1"] = _kn(w_in[:, C_V + 512:C_V + 1024])
    parts["GR0"] = _kn(w_in[:, C_G:C_G + 512])
    parts["GR1"] = _kn(w_in[:, C_G + 512:C_G + 1024])
    wo = inp["w_o"][l]
    parts["WOa"] = _kn(wo[:, 0:512])
    parts["WOb"] = _kn(wo[:, 512:1024])
    fi = inp["ffn_w_in"][l]
    for i in range(12):
        cols = np.concatenate([np.arange(i * 256, i * 256 + 256), 3072 + np.arange(i * 256, i * 256 + 256)])
        parts["F%d" % i] = _kn(fi[:, cols])
    fo = inp["ffn_w_out"][l]
    for i in range(8):
        parts["FO%d" % i] = _kn(fo[:, i * 128:(i + 1) * 128])
    out = np.empty((128, WTOT), np.float32)
    o = 0
    for name, n in UNITS:
        a = parts[name]
        assert a.shape == (128, n), (name, a.shape, n)
        out[:, o:o + n] = a
        o += n
    return out


def _par_layout():
    off = {}
    o = 0
    for name, n in [("g_mix", 8), ("g_ffn", 8), ("scw", 48), ("scb", 12), ("snorm", 8), ("lcw", 32), ("lcb", 8),
                    ("lba", 8), ("lbi", 8), ("llam", 8), ("fcw", 72), ("fcb", 24), ("dtb", 16), ("alog", 16),
                    ("dsk", 16), ("g_fin", 8)]:
        off[name] = o
        o += n
    return off, o


POFF, NPAR = _par_layout()


def _pp(v):
    return np.ascontiguousarray(v.reshape(-1, 128).T)


def host_params(inp, l):
    P = np.zeros((128, NPAR), np.float32)

    def put(name, arr):
        P[:, POFF[name]:POFF[name] + arr.shape[1]] = arr

    put("g_mix", _pp(inp["norm_mix"][l]))
    put("g_ffn", _pp(inp["norm_ffn"][l]))
    scw = inp["ssd_conv_w"][l]
    put("scw", np.stack([_pp(scw[k]) for k in range(4)], axis=2).reshape(128, 48))
    put("scb", _pp(inp["ssd_conv_b"][l]))
    put("snorm", _pp(inp["ssd_norm"][l]))
    lcw = inp["lru_conv_w"][l]
    put("lcw", np.stack([_pp(lcw[k]) for k in range(4)], axis=2).reshape(128, 32))
    put("lcb", _pp(inp["lru_conv_b"][l]))
    put("lba", _pp(inp["lru_b_a"][l]))
    put("lbi", _pp(inp["lru_b_i"][l]))
    put("llam", _pp(inp["lru_lambda"][l]))
    fcw = inp["ffn_conv_w"][l]
    put("fcw", np.stack([_pp(fcw[k]) for k in range(3)], axis=2).reshape(128, 72))
    put("fcb", _pp(inp["ffn_conv_b"][l]))
    put("dtb", np.tile(inp["ssd_dt_bias"][l][None, :], (128, 1)))
    put("alog", np.tile(inp["ssd_a_log"][l][None, :], (128, 1)))
    put("dsk", np.tile(inp["ssd_d"][l][None, :], (128, 1)))
    put("g_fin", _pp(inp["norm_final"]))
    return P


COFF = {"ident": 0, "tri": 128, "dmask": 256, "qd": 768, "kdec": 1280}
NCST = 1284


def host_consts():
    C = np.zeros((128, NCST), np.float32)
    C[:, 0:128] = np.eye(128, dtype=np.float32)
    p = np.arange(128)
    C[:, 128:256] = (p[:, None] <= p[None, :]).astype(np.float32)
    sc = 128.0 ** -0.5
    for h in range(4):
        lg = math.log(GAM[h])
        diff = (p[None, :] - p[:, None]).astype(np.float64)
        dm = np.where(diff >= 0, np.exp(np.maximum(diff, 0) * lg), 0.0) * sc
        C[:, 256 + h * 128:256 + (h + 1) * 128] = dm.astype(np.float32)
        C[:, 768 + h * 128:768 + (h + 1) * 128] = np.exp((p + 1.0) * lg).astype(np.float32)[None, :]
        C[:, 1280 + h] = (np.exp((127.0 - p) * lg) * sc).astype(np.float32)
    return C


def host_rope(S):
    inv = (1.0 / (10000.0 ** np.linspace(0.0, 1.0, 64, dtype=np.float32))).astype(np.float32)
    ang = (np.arange(S, dtype=np.float32)[:, None] * inv[None]).astype(np.float32)
    c = np.cos(ang).astype(np.float32).T
    s = np.sin(ang).astype(np.float32).T
    rc = np.concatenate([c, c], axis=0)
    rs = np.concatenate([-s, s], axis=0)
    return np.ascontiguousarray(rc), np.ascontiguousarray(rs)


class _SkipPhase(Exception):
    pass


class Phase(ExitStack):
    def __init__(self, name, stages):
        super().__init__()
        self.name = name
        self.stages = stages

    def check(self):
        if self.stages is not None and self.name not in self.stages:
            raise _SkipPhase()

    def __exit__(self, et, ev, tb):
        r = super().__exit__(et, ev, tb)
        return r or (et is _SkipPhase)


_PH_UNITS = {"ssd": ("XS", "BC", "DT", "Z", "WB0", "G0"), "lru": ("LX", "LG", "LY", "WB1", "G1"),
             "ret": ("Q", "K", "V", "GR", "WB2", "G2", "WO"), "ffn": ("F",)}


def _unit_phase(name):
    for ph, pre in _PH_UNITS.items():
        for p in pre:
            if name.startswith(p) and not (p == "F" and False):
                return ph
    raise KeyError(name)


def build(NBLK, STAGES=None):
    T = NBLK * NT
    nc = bass.Bass("TRN2", target_bir_lowering=False)
    x_d = nc.dram_tensor("x", [T, D], F32, kind="ExternalInput").ap()
    ws_d = nc.dram_tensor("ws", [2, 128, WTOT], F32, kind="ExternalInput").ap()
    par_d = nc.dram_tensor("par", [2, 128, NPAR], F32, kind="ExternalInput").ap()
    cst_d = nc.dram_tensor("cst", [128, NCST], F32, kind="ExternalInput").ap()
    rc_d = nc.dram_tensor("rcos", [128, T], F32, kind="ExternalInput").ap()
    rs_d = nc.dram_tensor("rsin", [128, T], F32, kind="ExternalInput").ap()
    out_d = nc.dram_tensor("out", [T, D], F32, kind="ExternalOutput").ap()
    wscr = nc.dram_tensor("wscr", [2, 128, WTOT], BF16).ap()

    es = ExitStack()
    with es:
        S = Sched(nc, es)

        uid = [0]

        def on(name):
            return STAGES is None or name in STAGES

        def var(name):
            return STAGES is not None and name in STAGES

        def sb(name, shape, dt=F32, stack=es):
            uid[0] += 1
            return stack.enter_context(nc.sbuf_tensor("%s_%d" % (name, uid[0]), shape, dt))

        cst = sb("cst_sb", [128, NCST]); b_cst = Buf()
        par = [sb("par_sb%d" % l, [128, NPAR]) for l in range(2)]; b_par = bufs(2)
        ident_b = sb("ident_b", [128, 128], BF16)
        ones_b = sb("ones_b", [128, 128], BF16)
        ones_f = sb("ones_f", [128, 128])
        b_c2 = Buf()
        A_bc = [sb("A_bc%d" % l, [128, 16]) for l in range(2)]
        cpar = [sb("cpar%d" % l, [128, 8]) for l in range(2)]
        c2par = [sb("c2par%d" % l, [128, 8]) for l in range(2)]
        b_der = bufs(2)
        Sssd = [sb("Sssd%d" % l, [128, 1024]) for l in range(2)]; b_Sssd = bufs(2)
        Sssd_b = [sb("Sssdb%d" % l, [128, 1024], BF16) for l in range(2)]; b_Sssd_b = bufs(2)
        Sret = [sb("Sret%d" % l, [128, 1024]) for l in range(2)]; b_Sret = bufs(2)
        Sret_b = [sb("Sretb%d" % l, [128, 1024], BF16) for l in range(2)]; b_Sret_b = bufs(2)
        halo_s = [sb("halo_s%d" % l, [128, 12, 3]) for l in range(2)]; b_halo_s = [bufs(12) for _ in range(2)]
        halo_l = [sb("halo_l%d" % l, [128, 8, 3]) for l in range(2)]; b_halo_l = [bufs(8) for _ in range(2)]
        halo_f = [sb("halo_f%d" % l, [128, 24, 2]) for l in range(2)]; b_halo_f = [bufs(24) for _ in range(2)]
        hlru = [sb("hlru%d" % l, [128, 8]) for l in range(2)]; b_hlru = [bufs(8) for _ in range(2)]
        x_fm = sb("x_fm", [128, 8, NT]); b_x = bufs(8)
        hT = sb("hT", [128, 8, NT], BF16); b_hT = bufs(8)
        merged = sb("merged", [128, 8, NT]); b_mg = bufs(8)
        ybr = sb("ybr", [128, 8, NT], BF16); b_ybr = bufs(8)
        rstd = sb("rstd", [128, NT]); b_rstd = Buf()
        lnv = sb("lnv", [128, NT]); b_lnv = Buf()
        sqt = [sb("sqt%d" % i, [128, NT], BF16) for i in range(2)]; b_sqt = bufs(2)
        spt = [sb("spt%d" % i, [128, 64]) for i in range(3)]; b_spt = Buf()
        NSLOT = 4
        wbuf = [sb("wbuf%d" % i, [128, 4096], BF16) for i in range(NSLOT)]; b_wbuf = bufs(NSLOT)
        PB = [es.enter_context(nc.psum_tensor("pb%d" % i, [128, 512], F32)) for i in range(8)]
        b_PB = [Buf(True) for _ in range(8)]

        IDf = cst[:, 0:128]
        TRI = cst[:, 128:256]

        def PE(out, lhsT, rhs, start, stop, R, W):
            S.op("pe", lambda e: e.matmul(out, lhsT=lhsT, rhs=rhs, start=start, stop=stop), R, W)

        def ACT(out, in_, func, R, W, bias=None, scale=None, accum=None):
            kw = {}
            if bias is not None:
                kw["bias"] = bias
            if scale is not None:
                kw["scale"] = scale
            if accum is not None:
                kw["accum_out"] = accum
            S.op("act", lambda e: e.activation(out=out, in_=in_, func=func, **kw), R, W)

        def TT(eng, out, a, b, op, R, W):
            S.op(eng, lambda e: e.tensor_tensor(out=out, in0=a, in1=b, op=op), R, W)

        def TS(eng, out, a, s1, s2, op0, op1, R, W):
            if s2 is None:
                S.op(eng, lambda e: e.tensor_scalar(out=out, in0=a, scalar1=s1, scalar2=None, op0=op0), R, W)
            else:
                S.op(eng, lambda e: e.tensor_scalar(out=out, in0=a, scalar1=s1, scalar2=s2, op0=op0, op1=op1), R, W)

        def STT(out, in0, scalar, in1, op0, op1, R, W):
            S.op("dve", lambda e: e.scalar_tensor_tensor(out=out, in0=in0, scalar=scalar, in1=in1, op0=op0, op1=op1), R, W)

        def SCAN(out, d0, d1, init, R, W):
            S.op("dve", lambda e: e.tensor_tensor_scan(out=out, data0=d0, data1=d1, initial=init, op0=ALU.mult, op1=ALU.add), R, W)

        def CP(eng, out, in_, R, W):
            if eng == "act":
                S.op("act", lambda e: e.copy(out=out, in_=in_), R, W)
            else:
                S.op(eng, lambda e: e.tensor_copy(out=out, in_=in_), R, W)

        def MSET(eng, ap, val, W):
            S.op(eng, lambda e: e.memset(ap, val), (), W)

        def precast():
            PW = 4096
            with Phase('precast', STAGES) as ph:
                ph.check()
                NB_ = 3
                stg = [sb("stg%d" % i, [128, PW], F32, ph) for i in range(NB_)]; b_stg = bufs(NB_)
                stb = [sb("stb%d" % i, [128, PW], BF16, ph) for i in range(NB_)]; b_stb = bufs(NB_)
                n = 0
                for l in range(2):
                    pl = []
                    for lo in range(0, WTOT, PW):
                        hi = min(WTOT, lo + PW)
                        q = n % NB_
                        S.dma("sp", stg[q][:, 0:hi - lo], ws_d[l, :, lo:hi], (), [b_stg[q]])
                        CP(("dve", "pool")[n % 2], stb[q][:, 0:hi - lo], stg[q][:, 0:hi - lo], [b_stg[q]], [b_stb[q]])
                        b = Buf()
                        S.dma("act", wscr[l, :, lo:hi], stb[q][:, 0:hi - lo], [b_stb[q]], [b])
                        pl.append((lo, hi, b))
                        n += 1
                    pieces.append(pl)
                S.final_wait("sp", [b_ for pl in pieces for (_, _, b_) in pl])
                S.barrier(with_sp=True)

        pieces = []

        stream = []
        for blk in range(NBLK):
            for l in range(2):
                o = 0
                for name, n in UNITS:
                    if STAGES is None or _unit_phase(name) in STAGES:
                        stream.append((l, name, o, n))
                    o += n
        wstate = {"next_load": 0, "next_get": 0}
        slot_of = {}

        def w_load_next(slot):
            i = wstate["next_load"]
            if i >= len(stream):
                return
            l, name, o, n = stream[i]
            deps = [b for (lo, hi, b) in pieces[l] if lo < o + n and hi > o]
            S.dma("sp", wbuf[slot][:, 0:n], wscr[l, :, o:o + n], deps, [b_wbuf[slot]])
            slot_of[i] = slot
            wstate["next_load"] = i + 1

        def w_get(l, name):
            i = wstate["next_get"]
            assert stream[i][0] == l and stream[i][1] == name, (stream[i], l, name)
            wstate["next_get"] = i + 1
            s = slot_of[i]
            return s

        def w_rel(slot):
            w_load_next(slot)

        precast()
        S.dma("sp", cst[:], cst_d, (), [b_cst])
        for l in range(2):
            S.dma("sp", par[l][:], par_d[l], (), [b_par[l]])
        for s_ in range(NSLOT):
            w_load_next(s_)
        CP("dve", ident_b[:], IDf, [b_cst], [b_c2])
        MSET("pool", ones_b[:], 1.0, [b_c2])
        MSET("pool", ones_f[:], 1.0, [b_c2])
        for l in range(2):
            P = par[l]
            ACT(A_bc[l][:], P[:, POFF["alog"]:POFF["alog"] + 16], AF.Exp, [b_par[l]], [b_der[l]])
            TS("dve", A_bc[l][:], A_bc[l][:], -1.0, None, ALU.mult, None, [b_der[l]], [b_der[l]])
            MSET("pool", Sssd[l][:], 0.0, [b_Sssd[l]])
            MSET("pool", Sssd_b[l][:], 0.0, [b_Sssd_b[l]])
            MSET("pool", Sret[l][:], 0.0, [b_Sret[l]])
            MSET("pool", Sret_b[l][:], 0.0, [b_Sret_b[l]])
            MSET("pool", halo_s[l][:], 0.0, b_halo_s[l])
            MSET("pool", halo_l[l][:], 0.0, b_halo_l[l])
            MSET("pool", halo_f[l][:], 0.0, b_halo_f[l])
            MSET("pool", hlru[l][:], 0.0, b_hlru[l])

        def softplus(dst, src, n, R, W):
            t0, t1, t2 = spt[0][:, 0:n], spt[1][:, 0:n], spt[2][:, 0:n]
            bs = [b_spt]
            ACT(t0, src, AF.Abs, R, bs)
            ACT(t0, t0, AF.Exp, bs, bs, scale=-1.0)
            TS("dve", t1, t0, 2.0, None, ALU.add, None, bs, bs)
            S.op("dve", lambda e: e.reciprocal(out=t1, in_=t1), bs, bs)
            TT("dve", t0, t0, t1, ALU.mult, bs, bs)
            TT("dve", t1, t0, t0, ALU.mult, bs, bs)
            TS("dve", t2, t1, 1.0 / 9, 1.0 / 7, ALU.mult, ALU.add, bs, bs)
            for cst_ in (1.0 / 5, 1.0 / 3, 1.0):
                TT("dve", t2, t2, t1, ALU.mult, bs, bs)
                TS("dve", t2, t2, cst_, None, ALU.add, None, bs, bs)
            TT("dve", t0, t0, t2, ALU.mult, bs, bs)
            TS("dve", t1, src, 0.0, None, ALU.max, None, list(R) + bs, bs)
            STT(dst, t0, 2.0, t1, ALU.mult, ALU.add, bs, W)

        for l in range(2):
            TS("dve", c2par[l][:], par[l][:, POFF["llam"]:POFF["llam"] + 8], -1.0, None, ALU.mult, None, [b_par[l]], [b_der[l]])
            softplus(cpar[l][:], c2par[l][:], 8, [b_der[l]], [b_der[l]])
            TS("dve", c2par[l][:], cpar[l][:], -16.0, None, ALU.mult, None, [b_der[l]], [b_der[l]])
            TS("dve", cpar[l][:], cpar[l][:], -8.0, None, ALU.mult, None, [b_der[l]], [b_der[l]])

        rr = {"pb": 0, "ev": 0}

        def nextbank(lst):
            i = lst[rr["pb"] % len(lst)]
            rr["pb"] += 1
            return i

        def rmsnorm_to_hT(goff, P, bP):
            bk = 7
            for c in range(8):
                q = c % 2
                ACT(sqt[q][:], x_fm[:, c, :], AF.Square, [b_x[c]], [b_sqt[q]])
                PE(PB[bk][:], ones_b[:], sqt[q][:], c == 0, c == 7, [b_c2, b_sqt[q]], [b_PB[bk]])
            ACT(lnv[:], PB[bk][:], AF.Ln, [b_PB[bk]], [b_lnv], bias=EPS, scale=1.0 / D)
            ACT(rstd[:], lnv[:], AF.Exp, [b_lnv], [b_rstd], scale=-0.5)
            for c in range(8):
                STT(hT[:, c, :], x_fm[:, c, :], P[:, goff + c:goff + c + 1], rstd[:], ALU.mult, ALU.mult,
                    [b_x[c], bP, b_rstd], [b_hT[c]])

        def proj_fm(slot, j, bank, ncols=512, R_extra=()):
            for kc in range(8):
                PE(PB[bank][:], wbuf[slot][:, kc * ncols + j * 128:kc * ncols + (j + 1) * 128], hT[:, kc, :],
                   kc == 0, kc == 7, [b_wbuf[slot], b_hT[kc]], [b_PB[bank]])

        def proj_tm(slot, tc, bank):
            for kc in range(8):
                PE(PB[bank][:], hT[:, kc, tc * 128:(tc + 1) * 128], wbuf[slot][:, kc * 512:(kc + 1) * 512],
                   kc == 0, kc == 7, [b_wbuf[slot], b_hT[kc]], [b_PB[bank]])

        def branch_merge(l, k, gsb, b_gsb, tmpm, b_tmpm, merged_b=None, b_mgb=None):
            for half in range(2):
                sw = w_get(l, "WB%d%s" % (k, "ab"[half]))
                sg = w_get(l, "G%d%s" % (k, "ab"[half]))
                for j in range(4):
                    oc = half * 4 + j
                    bd = nextbank([0, 1, 2, 3])
                    for kc in range(8):
                        PE(PB[bd][:], wbuf[sw][:, kc * 512 + j * 128:kc * 512 + (j + 1) * 128], ybr[:, kc, :],
                           kc == 0, kc == 7, [b_wbuf[sw], b_ybr[kc]], [b_PB[bd]])
                    bg = nextbank([4, 5, 6, 7])
                    proj_fm(sg, j, bg)
                    q = oc % 2
                    gq = gsb[q]
                    tq = tmpm[q][:, 0:NT]
                    ACT(gq, PB[bg][:], AF.Sigmoid, [b_PB[bg]], [b_gsb[q]])
                    if k == 0:
                        TT("dve", merged[:, oc, :], gq, PB[bd][:], ALU.mult, [b_gsb[q], b_PB[bd]], [b_mg[oc]])
                    else:
                        TT("dve", tq, gq, PB[bd][:], ALU.mult, [b_gsb[q], b_PB[bd]], [b_tmpm[q]])
                        if k == 1:
                            TT("pool", merged[:, oc, :], merged[:, oc, :], tq, ALU.add, [b_mg[oc], b_tmpm[q]], [b_mg[oc]])
                        else:
                            TT("pool", merged_b[:, oc, :], merged[:, oc, :], tq, ALU.add, [b_mg[oc], b_tmpm[q]], [b_mgb[oc]])
                w_rel(sw)
                w_rel(sg)

        b_out = Buf()
        for blk in range(NBLK):
            t0 = blk * NT
            with Phase('xload', STAGES) as ph:
                ph.check()
                xtm = [sb("xtm%d" % i, [128, D], F32, ph) for i in range(4)]; b_xtm = bufs(4)
                for tc in range(4):
                    S.dma("sp", xtm[tc][:], x_d[t0 + tc * 128:t0 + (tc + 1) * 128, :], (), [b_xtm[tc]])
                for c in range(8):
                    bk = c % 4
                    for tc in range(4):
                        PE(PB[bk][:, tc * 128:(tc + 1) * 128], xtm[tc][:, c * 128:(c + 1) * 128], IDf, True, True,
                           [b_xtm[tc], b_cst], [b_PB[bk]])
                    CP("act" if c % 2 == 0 else "dve", x_fm[:, c, :], PB[bk][:], [b_PB[bk]], [b_x[c]])
                S.barrier()

            for l in range(2):
                P = par[l]; bP = b_par[l]
                rmsnorm_to_hT(POFF["g_mix"], P, bP)

                with Phase('ssd', STAGES) as ph:
                    ph.check()
                    raw = [sb("raw%d" % i, [128, NT + 3], F32, ph) for i in range(2)]; b_raw = bufs(2)
                    acc = [sb("acc%d" % i, [128, NT], F32, ph) for i in range(2)]; b_acc = bufs(2)
                    xbc = sb("xbc", [128, 12, NT], BF16, ph); b_xbc = bufs(12)
                    dt_tm = sb("dt_tm", [128, 4, 16], F32, ph); b_dt = Buf()
                    a_tm = sb("a_tm", [128, 4, 16], F32, ph); b_a = Buf()
                    xdt = sb("xdt", [128, 1024], BF16, ph); b_xdt = Buf()
                    xdsd = [sb("xdsd%d" % i, [128, 1024], BF16, ph) for i in range(2)]; b_xdsd = bufs(2)
                    xsD = sb("xsD", [128, 1024], F32, ph); b_xsD = Buf()
                    B_tm = [sb("B_tm%d" % i, [128, 256], BF16, ph) for i in range(2)]; b_Btm = bufs(2)
                    CBm = sb("CBm", [128, 2, 128], F32, ph); b_CBm = Buf()
                    Et = [sb("Et%d" % i, [128, 4, 128], F32, ph) for i in range(2)]; b_Et = bufs(2)
                    MT = sb("MT", [128, 16, 128], BF16, ph); b_MT = bufs(4)
                    smA = [sb("smA%d" % i, [128, 8, 16], F32, ph) for i in range(2)]; b_smA = [bufs(8) for _ in range(2)]
                    smB = sb("smB", [128, 4], F32, ph); b_smB = Buf()
                    t1 = sb("t1", [128, 1024], F32, ph); b_t1 = Buf()
                    yd = [sb("yd%d" % i, [128, 1024], F32, ph) for i in range(2)]; b_yd = bufs(2)
                    silz = [sb("silz%d" % i, [128, 1024], F32, ph) for i in range(2)]; b_silz = bufs(2)
                    yn2 = [sb("yn%d" % i, [128, 1024], BF16, ph) for i in range(2)]; b_yn2 = bufs(2)
                    gsb = [r_[:, 0:NT] for r_ in raw]; b_gsb = b_raw
                    tmpm = acc; b_tmpm = b_acc

                    for ui, uname in enumerate(("XS0", "XS1", "BC")):
                        sl = w_get(l, uname)
                        for j in range(4):
                            cc = ui * 4 + j
                            bk = nextbank([0, 1, 2, 3])
                            proj_fm(sl, j, bk)
                            q = cc % 2
                            CP("act", raw[q][:, 3:NT + 3], PB[bk][:], [b_PB[bk]], [b_raw[q]])
                            CP("pool", raw[q][:, 0:3], halo_s[l][:, cc, :], [b_halo_s[l][cc]], [b_raw[q]])
                            CP("pool", halo_s[l][:, cc, :], raw[q][:, NT:NT + 3], [b_raw[q]], [b_halo_s[l][cc]])
                            wo_ = POFF["scw"] + cc * 4
                            TS("dve", acc[q][:], raw[q][:, 0:NT], P[:, wo_:wo_ + 1], P[:, POFF["scb"] + cc:POFF["scb"] + cc + 1],
                               ALU.mult, ALU.add, [b_raw[q], bP], [b_acc[q]])
                            for k in range(1, 4):
                                STT(acc[q][:], raw[q][:, k:k + NT], P[:, wo_ + k:wo_ + k + 1], acc[q][:], ALU.mult, ALU.add,
                                    [b_raw[q], bP, b_acc[q]], [b_acc[q]])
                            ACT(xbc[:, cc, :], acc[q][:], AF.Silu, [b_acc[q]], [b_xbc[cc]])
                        w_rel(sl)
                    sl = w_get(l, "DT")
                    for tc in range(4):
                        for kc in range(8):
                            PE(PB[4][:, tc * 16:(tc + 1) * 16], hT[:, kc, tc * 128:(tc + 1) * 128], wbuf[sl][:, kc * 16:(kc + 1) * 16],
                               kc == 0, kc == 7, [b_wbuf[sl], b_hT[kc]], [b_PB[4]])
                    w_rel(sl)
                    TT("dve", dt_tm[:], PB[4][:, 0:64].rearrange("p (a b) -> p a b", a=4),
                       P[:, POFF["dtb"]:POFF["dtb"] + 16].unsqueeze(1).to_broadcast([128, 4, 16]), ALU.add, [b_PB[4], bP], [b_dt])
                    dtv = dt_tm[:].rearrange("p a b -> p (a b)")
                    softplus(dtv, dtv, 64, [b_dt], [b_dt])
                    TT("dve", a_tm[:], dt_tm[:], A_bc[l][:].unsqueeze(1).to_broadcast([128, 4, 16]), ALU.mult, [b_dt, b_der[l]], [b_a])
                    sz0 = w_get(l, "Z0")
                    sz1 = w_get(l, "Z1")
                    def ssd_A(tc):
                        p = tc % 2
                        ts_ = slice(tc * 128, (tc + 1) * 128)
                        sm = smA[p]; b_sm = b_smA[p]
                        for zh, sz in enumerate((sz0, sz1)):
                            proj_tm(sz, tc, zh)
                            ACT(silz[p][:, zh * 512:(zh + 1) * 512], PB[zh][:], AF.Silu, [b_PB[zh]], [b_silz[p]])
                        for c in range(8):
                            bk = 2 + c // 4
                            PE(PB[bk][:, (c % 4) * 128:(c % 4 + 1) * 128], xbc[:, c, ts_], ident_b[:], True, True,
                               [b_xbc[c], b_c2], [b_PB[bk]])
                        for hh in range(2):
                            TT("dve", xdt[:, hh * 512:(hh + 1) * 512].rearrange("p (h d) -> p h d", h=8),
                               PB[2 + hh][:].rearrange("p (h d) -> p h d", h=8),
                               dt_tm[:, tc, hh * 8:(hh + 1) * 8].unsqueeze(2).to_broadcast([128, 8, 64]), ALU.mult,
                               [b_PB[2 + hh], b_dt], [b_xdt])
                            TT("dve", xsD[:, hh * 512:(hh + 1) * 512].rearrange("p (h d) -> p h d", h=8),
                               PB[2 + hh][:].rearrange("p (h d) -> p h d", h=8),
                               P[:, POFF["dsk"] + hh * 8:POFF["dsk"] + (hh + 1) * 8].unsqueeze(2).to_broadcast([128, 8, 64]), ALU.mult,
                               [b_PB[2 + hh], bP], [b_xsD])
                        for g in range(2):
                            PE(PB[4][:, g * 128:(g + 1) * 128], xbc[:, 8 + g, ts_], ident_b[:], True, True, [b_xbc[8 + g], b_c2], [b_PB[4]])
                        for g in range(2):
                            PE(PB[4][:, 256 + g * 128:256 + (g + 1) * 128], xbc[:, 8 + g, ts_], xbc[:, 10 + g, ts_], True, True,
                               [b_xbc[8 + g], b_xbc[10 + g]], [b_PB[4]])
                        CP("act", B_tm[p][:], PB[4][:, 0:256], [b_PB[4]], [b_Btm[p]])
                        TT("dve", CBm[:], PB[4][:, 256:512].rearrange("p (g l) -> p g l", g=2),
                           TRI.unsqueeze(1).to_broadcast([128, 2, 128]), ALU.mult, [b_PB[4], b_cst], [b_CBm])
                        PE(PB[5][:, 0:16], TRI, a_tm[:, tc, :], True, True, [b_cst, b_a], [b_PB[5]])
                        PE(PB[5][:, 16:32], ones_f[:], a_tm[:, tc, :], True, True, [b_c2, b_a], [b_PB[5]])
                        TS("dve", sm[:, 0, :], PB[5][:, 0:16], -1.0, None, ALU.mult, None, [b_PB[5]], [b_sm[0]])
                        CP("dve", sm[:, 1, :], PB[5][:, 0:16], [b_PB[5]], [b_sm[1]])
                        ACT(sm[:, 2, :], PB[5][:, 0:16], AF.Exp, [b_PB[5]], [b_sm[2]])
                        TT("dve", sm[:, 3, :], PB[5][:, 16:32], sm[:, 1, :], ALU.subtract, [b_PB[5], b_sm[1]], [b_sm[3]])
                        ACT(sm[:, 4, :], sm[:, 3, :], AF.Exp, [b_sm[3]], [b_sm[4]])
                        ACT(sm[:, 5, :], PB[5][:, 16:32], AF.Exp, [b_PB[5]], [b_sm[5]])
                        for hg in range(4):
                            bk = 6 + hg % 2
                            for i in range(4):
                                h = hg * 4 + i
                                PE(PB[bk][:, i * 128:(i + 1) * 128], a_tm[:, tc, h:h + 1].to_broadcast([128, 128]), TRI, True, True,
                                   [b_a, b_cst], [b_PB[bk]])
                            for i in range(4):
                                h = hg * 4 + i
                                ACT(Et[hg % 2][:, i, :], PB[bk][:, i * 128:(i + 1) * 128], AF.Abs, [b_PB[bk], b_sm[0]], [b_Et[hg % 2]],
                                    bias=sm[:, 0, h:h + 1])
                            ACT(Et[hg % 2][:], Et[hg % 2][:], AF.Exp, [b_Et[hg % 2]], [b_Et[hg % 2]], scale=-1.0)
                            TT("dve", MT[:, hg * 4:(hg + 1) * 4, :], Et[hg % 2][:],
                               CBm[:, hg // 2, :].unsqueeze(1).to_broadcast([128, 4, 128]), ALU.mult,
                               [b_Et[hg % 2], b_CBm], [b_MT[hg]])
                        for h in range(16):
                            bk = h // 8
                            PE(PB[bk][:, (h % 8) * 64:(h % 8 + 1) * 64], MT[:, h, :], xdt[:, h * 64:(h + 1) * 64], True, True,
                               [b_MT[h // 4], b_xdt], [b_PB[bk]])
                        for g in range(2):
                            TT("dve", yd[p][:, g * 512:(g + 1) * 512], PB[g][:], xsD[:, g * 512:(g + 1) * 512], ALU.add,
                               [b_PB[g], b_xsD], [b_yd[p]])
                        TT("pool", xdsd[p][:].rearrange("p (h d) -> p h d", h=16), xdt[:].rearrange("p (h d) -> p h d", h=16),
                           sm[:, 4, :].unsqueeze(2).to_broadcast([128, 16, 64]), ALU.mult, [b_xdt, b_sm[4]], [b_xdsd[p]])

                    def ssd_B(tc):
                        p = tc % 2
                        ts_ = slice(tc * 128, (tc + 1) * 128)
                        sm = smA[p]; b_sm = b_smA[p]
                        for g in range(2):
                            PE(PB[2 + g][:], xbc[:, 10 + g, ts_], Sssd_b[l][:, g * 512:(g + 1) * 512], True, True,
                               [b_xbc[10 + g], b_Sssd_b[l]], [b_PB[2 + g]])
                        for g in range(2):
                            PE(PB[6 + g][:], B_tm[p][:, g * 128:(g + 1) * 128], xdsd[p][:, g * 512:(g + 1) * 512], True, True,
                               [b_Btm[p], b_xdsd[p]], [b_PB[6 + g]])
                        TT("dve", Sssd[l][:].rearrange("p (h d) -> p h d", h=16), Sssd[l][:].rearrange("p (h d) -> p h d", h=16),
                           sm[:, 5, :].unsqueeze(2).to_broadcast([128, 16, 64]), ALU.mult, [b_Sssd[l], b_sm[5]], [b_Sssd[l]])
                        for g in range(2):
                            TT("dve", Sssd[l][:, g * 512:(g + 1) * 512], Sssd[l][:, g * 512:(g + 1) * 512], PB[6 + g][:], ALU.add,
                               [b_Sssd[l], b_PB[6 + g]], [b_Sssd[l]])
                        CP("act", Sssd_b[l][:], Sssd[l][:], [b_Sssd[l]], [b_Sssd_b[l]])
                        for g in range(2):
                            TT("dve", t1[:, g * 512:(g + 1) * 512].rearrange("p (h d) -> p h d", h=8),
                               PB[2 + g][:].rearrange("p (h d) -> p h d", h=8),
                               sm[:, 2, g * 8:(g + 1) * 8].unsqueeze(2).to_broadcast([128, 8, 64]), ALU.mult,
                               [b_PB[2 + g], b_sm[2]], [b_t1])
                        TT("dve", t1[:], t1[:], yd[p][:], ALU.add, [b_t1, b_yd[p]], [b_t1])
                        TT("dve", t1[:], t1[:], silz[p][:], ALU.mult, [b_t1, b_silz[p]], [b_t1])
                        yn = yn2[p]; b_yn = b_yn2[p]
                        MSET("pool", smB[:, 0:1], 0.0, [b_smB])
                        ACT(yn[:], t1[:], AF.Square, [b_t1], [b_yn, b_smB], accum=smB[:, 0:1])
                        ACT(smB[:, 1:2], smB[:, 0:1], AF.Ln, [b_smB], [b_smB], bias=EPS, scale=1.0 / 1024)
                        ACT(smB[:, 2:3], smB[:, 1:2], AF.Exp, [b_smB], [b_smB], scale=-0.5)
                        TS("dve", yn[:], t1[:], smB[:, 2:3], None, ALU.mult, None, [b_t1, b_smB], [b_yn])

                    def ssd_B2(tc):
                        p = tc % 2
                        ts_ = slice(tc * 128, (tc + 1) * 128)
                        yn = yn2[p]; b_yn = b_yn2[p]
                        for c in range(8):
                            bk = 4 + c // 4
                            PE(PB[bk][:, (c % 4) * 128:(c % 4 + 1) * 128], yn[:, c * 128:(c + 1) * 128], ident_b[:], True, True,
                               [b_yn, b_c2], [b_PB[bk]])
                        for c in range(8):
                            bk = 4 + c // 4
                            TS("dve", ybr[:, c, ts_], PB[bk][:, (c % 4) * 128:(c % 4 + 1) * 128],
                               P[:, POFF["snorm"] + c:POFF["snorm"] + c + 1], None, ALU.mult, None, [b_PB[bk], bP], [b_ybr[c]])

                    ssd_A(0)
                    for tc in range(4):
                        if tc < 3:
                            ssd_A(tc + 1)
                        if tc > 0:
                            ssd_B2(tc - 1)
                        ssd_B(tc)
                    ssd_B2(3)
                    w_rel(sz0)
                    w_rel(sz1)
                    branch_merge(l, 0, gsb, b_gsb, tmpm, b_tmpm)
                    S.barrier()

                with Phase('lru', STAGES) as ph:
                    ph.check()
                    raw = [sb("lraw%d" % i, [128, NT + 3], F32, ph) for i in range(2)]; b_raw = bufs(2)
                    xc = sb("xc", [128, 8, NT], F32, ph); b_xc = bufs(8)
                    xcb = sb("xcb", [128, 8, NT], BF16, ph); b_xcb = bufs(8)
                    rt = [sb("rt%d" % i, [128, NT], F32, ph) for i in range(2)]; b_rt = bufs(2)
                    it = [sb("it%d" % i, [128, NT], F32, ph) for i in range(2)]; b_it = bufs(2)
                    at = [sb("at%d" % i, [128, NT], F32, ph) for i in range(2)]; b_at = bufs(2)
                    a2t = [sb("a2t%d" % i, [128, NT], F32, ph) for i in range(2)]; b_a2t = bufs(2)
                    ut = [sb("ut%d" % i, [128, NT], F32, ph) for i in range(2)]; b_ut = bufs(2)
                    hs = [sb("hs%d" % i, [128, NT], F32, ph) for i in range(2)]; b_hs = bufs(2)
                    gl = [sb("gl%d" % i, [128, NT], F32, ph) for i in range(2)]; b_gl = bufs(2)
                    gsb = [r_[:] for r_ in rt]; b_gsb = b_rt
                    tmpm = it; b_tmpm = b_it
                    for ui, uname in enumerate(("LX0", "LX1")):
                        sl = w_get(l, uname)
                        for j in range(4):
                            c = ui * 4 + j
                            bk = nextbank([0, 1, 2, 3])
                            proj_fm(sl, j, bk)
                            q = c % 2
                            CP("act", raw[q][:, 3:NT + 3], PB[bk][:], [b_PB[bk]], [b_raw[q]])
                            CP("pool", raw[q][:, 0:3], halo_l[l][:, c, :], [b_halo_l[l][c]], [b_raw[q]])
                            CP("pool", halo_l[l][:, c, :], raw[q][:, NT:NT + 3], [b_raw[q]], [b_halo_l[l][c]])
                            wo_ = POFF["lcw"] + c * 4
                            TS("dve", xc[:, c, :], raw[q][:, 0:NT], P[:, wo_:wo_ + 1], P[:, POFF["lcb"] + c:POFF["lcb"] + c + 1],
                               ALU.mult, ALU.add, [b_raw[q], bP], [b_xc[c]])
                            for k in range(1, 4):
                                STT(xc[:, c, :], raw[q][:, k:k + NT], P[:, wo_ + k:wo_ + k + 1], xc[:, c, :], ALU.mult, ALU.add,
                                    [b_raw[q], bP, b_xc[c]], [b_xc[c]])
                            CP("act", xcb[:, c, :], xc[:, c, :], [b_xc[c]], [b_xcb[c]])
                        w_rel(sl)
                    slg = w_get(l, "LG")
                    sly = [w_get(l, "LY0"), w_get(l, "LY1")]
                    for k in range(4):
                        cs = (2 * k, 2 * k + 1)
                        for c in cs:
                            j, q = c % 2, c % 2
                            for m, (dst, bdst, boff) in enumerate(((rt, b_rt, "lba"), (it, b_it, "lbi"))):
                                bk = nextbank([0, 1, 2, 3])
                                base = (m * 4 + k) * 512
                                for kc in range(2):
                                    PE(PB[bk][:], wbuf[slg][:, base + kc * 256 + j * 128:base + kc * 256 + (j + 1) * 128], xcb[:, 2 * k + kc, :],
                                       kc == 0, kc == 1, [b_wbuf[slg], b_xcb[2 * k + kc]], [b_PB[bk]])
                                ACT(dst[q][:], PB[bk][:], AF.Sigmoid, [b_PB[bk], bP], [bdst[q]], bias=P[:, POFF[boff] + c:POFF[boff] + c + 1])
                        for c in cs:
                            q = c % 2
                            ACT(at[q][:], rt[q][:], AF.Exp, [b_rt[q], b_der[l]], [b_at[q]], scale=cpar[l][:, c:c + 1])
                            ACT(a2t[q][:], rt[q][:], AF.Exp, [b_rt[q], b_der[l]], [b_a2t[q]], scale=c2par[l][:, c:c + 1])
                        for c in cs:
                            q = c % 2
                            ACT(a2t[q][:], a2t[q][:], AF.Sqrt, [b_a2t[q]], [b_a2t[q]], bias=1.0, scale=-1.0)
                        for c in cs:
                            q = c % 2
                            TT("dve", ut[q][:], it[q][:], xc[:, c, :], ALU.mult, [b_it[q], b_xc[c]], [b_ut[q]])
                            TT("dve", ut[q][:], ut[q][:], a2t[q][:], ALU.mult, [b_ut[q], b_a2t[q]], [b_ut[q]])
                            SCAN(hs[q][:], at[q][:], ut[q][:], hlru[l][:, c:c + 1], [b_at[q], b_ut[q], b_hlru[l][c]], [b_hs[q]])
                            CP("pool", hlru[l][:, c:c + 1], hs[q][:, NT - 1:NT], [b_hs[q]], [b_hlru[l][c]])
                        for c in cs:
                            q = c % 2
                            bk = nextbank([4, 5, 6, 7])
                            proj_fm(sly[c // 4], c % 4, bk)
                            ACT(gl[q][:], PB[bk][:], AF.Gelu, [b_PB[bk]], [b_gl[q]])
                            TT("dve", ybr[:, c, :], hs[q][:], gl[q][:], ALU.mult, [b_hs[q], b_gl[q]], [b_ybr[c]])
                    w_rel(slg)
                    w_rel(sly[0])
                    w_rel(sly[1])
                    branch_merge(l, 1, gsb, b_gsb, tmpm, b_tmpm)
                    S.barrier()

                with Phase('ret', STAGES) as ph:
                    ph.check()
                    qr = sb("qr", [128, 4, NT], BF16, ph); b_qr = bufs(4)
                    qrd = sb("qrd", [128, 4, NT], BF16, ph); b_qrd = bufs(4)
                    kr = sb("kr", [128, 4, NT], BF16, ph); b_kr = bufs(4)
                    ra = [sb("ra%d" % i, [128, NT], F32, ph) for i in range(2)]; b_ra = bufs(2)
                    rb = [sb("rb%d" % i, [128, NT], F32, ph) for i in range(2)]; b_rb = bufs(2)
                    v_tm2 = [sb("v_tm%d" % i, [128, 1024], BF16, ph) for i in range(2)]; b_v2 = bufs(2)
                    silg2 = [sb("silg%d" % i, [128, 1024], F32, ph) for i in range(2)]; b_silg2 = bufs(2)
                    PT2 = [sb("PT%d" % i, [128, 4, 128], BF16, ph) for i in range(2)]; b_PT2 = bufs(2)
                    k_tm2 = [sb("k_tm%d" % i, [128, 4, 128], BF16, ph) for i in range(2)]; b_ktm2 = bufs(2)
                    ryn2 = [sb("ryn%d" % i, [128, 1024], BF16, ph) for i in range(2)]; b_ryn2 = bufs(2)
                    junk = sb("rjunk", [128, 256], BF16, ph); b_junk = Buf()
                    ss4 = sb("ss4", [128, 12], F32, ph); b_ss4 = Buf()
                    gsb = [r_[:] for r_ in ra]; b_gsb = b_ra
                    tmpm = rb; b_tmpm = b_rb
                    merged_b = sb("merged_b", [128, 8, NT], BF16, ph); b_mgb = bufs(8)
                    cos_t = sb("cos_t", [128, NT], F32, ph); sin_t = sb("sin_t", [128, NT], F32, ph); b_cs = bufs(2)
                    S.dma("pool", cos_t[:], rc_d[:, t0:t0 + NT], (), [b_cs[0]])
                    S.dma("pool", sin_t[:], rs_d[:, t0:t0 + NT], (), [b_cs[1]])
                    for (un, uns, isq) in (("Q", "QS", True), ("K", "KS", False)):
                        s1 = w_get(l, un)
                        s2 = w_get(l, uns)
                        for h in range(4):
                            if not on("r1"):
                                break
                            q = h % 2
                            ba = nextbank([0, 1, 2, 3])
                            proj_fm(s1, h, ba)
                            bb = nextbank([4, 5, 6, 7])
                            proj_fm(s2, h, bb)
                            TT("dve", ra[q][:], PB[ba][:], cos_t[:], ALU.mult, [b_PB[ba], b_cs[0]], [b_ra[q]])
                            TT("dve", rb[q][:], PB[bb][:], sin_t[:], ALU.mult, [b_PB[bb], b_cs[1]], [b_rb[q]])
                            if isq:
                                TT("dve", ra[q][:], ra[q][:], rb[q][:], ALU.add, [b_ra[q], b_rb[q]], [b_ra[q]])
                                CP("act", qr[:, h, :], ra[q][:], [b_ra[q]], [b_qr[h]])
                                TT("pool", qrd[:, h, :].rearrange("p (a b) -> p a b", a=4), ra[q][:].rearrange("p (a b) -> p a b", a=4),
                                   cst[:, COFF["qd"] + h * 128:COFF["qd"] + (h + 1) * 128].unsqueeze(1).to_broadcast([128, 4, 128]),
                                   ALU.mult, [b_ra[q], b_cst], [b_qrd[h]])
                            else:
                                TT("dve", kr[:, h, :], ra[q][:], rb[q][:], ALU.add, [b_ra[q], b_rb[q]], [b_kr[h]])
                        w_rel(s1)
                        w_rel(s2)
                    sv = [w_get(l, "V0"), w_get(l, "V1")]
                    sg_ = [w_get(l, "GR0"), w_get(l, "GR1")]
                    def ret_A(tc):
                        p = tc % 2
                        ts_ = slice(tc * 128, (tc + 1) * 128)
                        v_tm, b_v, silg, b_silg, PT, b_PT, k_tm, b_ktm = v_tm2[p], b_v2[p], silg2[p], b_silg2[p], PT2[p], b_PT2[p], k_tm2[p], b_ktm2[p]
                        for vh in range(2):
                            proj_tm(sv[vh], tc, vh)
                            CP("act", v_tm[:, vh * 512:(vh + 1) * 512], PB[vh][:], [b_PB[vh]], [b_v])
                        for gh in range(2):
                            proj_tm(sg_[gh], tc, 2 + gh)
                            ACT(silg[:, gh * 512:(gh + 1) * 512], PB[2 + gh][:], AF.Silu, [b_PB[2 + gh]], [b_silg])
                        for h in range(4):
                            PE(PB[4][:, h * 128:(h + 1) * 128], kr[:, h, ts_], qr[:, h, ts_], True, True, [b_kr[h], b_qr[h]], [b_PB[4]])
                        for h in range(4):
                            PE(PB[5][:, h * 128:(h + 1) * 128], kr[:, h, ts_], ident_b[:], True, True, [b_kr[h], b_c2], [b_PB[5]])
                        TT("dve", PT[:].rearrange("p h l -> p (h l)"), PB[4][:], cst[:, COFF["dmask"]:COFF["dmask"] + 512], ALU.mult,
                           [b_PB[4], b_cst], [b_PT])
                        for h in range(4):
                            TS("dve", k_tm[:, h, :], PB[5][:, h * 128:(h + 1) * 128], cst[:, COFF["kdec"] + h:COFF["kdec"] + h + 1], None,
                               ALU.mult, None, [b_PB[5], b_cst], [b_ktm])

                    def ret_B(tc):
                        p = tc % 2
                        ts_ = slice(tc * 128, (tc + 1) * 128)
                        v_tm, b_v, silg, b_silg, PT, b_PT, k_tm, b_ktm = v_tm2[p], b_v2[p], silg2[p], b_silg2[p], PT2[p], b_PT2[p], k_tm2[p], b_ktm2[p]
                        for h in range(4):
                            bk = 6 + h // 2
                            o_ = PB[bk][:, (h % 2) * 256:(h % 2 + 1) * 256]
                            PE(o_, PT[:, h, :], v_tm[:, h * 256:(h + 1) * 256], True, False, [b_PT, b_v], [b_PB[bk]])
                            PE(o_, qrd[:, h, ts_], Sret_b[l][:, h * 256:(h + 1) * 256], False, True, [b_qrd[h], b_Sret_b[l]], [b_PB[bk]])
                        for h in range(4):
                            bk = 2 + h // 2
                            PE(PB[bk][:, (h % 2) * 256:(h % 2 + 1) * 256], k_tm[:, h, :], v_tm[:, h * 256:(h + 1) * 256], True, True,
                               [b_ktm, b_v], [b_PB[bk]])
                        for h in range(4):
                            bk = 2 + h // 2
                            S.op("act", lambda e, o_=Sret[l][:, h * 256:(h + 1) * 256], g_=float(GAM[h] ** 128): e.mul(out=o_, in_=o_, mul=g_),
                                 [b_Sret[l]], [b_Sret[l]])
                            TT("dve", Sret[l][:, h * 256:(h + 1) * 256], Sret[l][:, h * 256:(h + 1) * 256],
                               PB[bk][:, (h % 2) * 256:(h % 2 + 1) * 256], ALU.add, [b_Sret[l], b_PB[bk]], [b_Sret[l]])
                        CP("act", Sret_b[l][:], Sret[l][:], [b_Sret[l]], [b_Sret_b[l]])
                        yn = ryn2[p]; b_yn = b_ryn2[p]
                        MSET("pool", ss4[:, 0:4], 0.0, [b_ss4])
                        for h in range(4):
                            bk = 6 + h // 2
                            ACT(junk[:], PB[bk][:, (h % 2) * 256:(h % 2 + 1) * 256], AF.Square, [b_PB[bk]], [b_junk, b_ss4],
                                accum=ss4[:, h:h + 1])
                        ACT(ss4[:, 4:8], ss4[:, 0:4], AF.Ln, [b_ss4], [b_ss4], bias=EPS, scale=1.0 / 256)
                        ACT(ss4[:, 8:12], ss4[:, 4:8], AF.Exp, [b_ss4], [b_ss4], scale=-0.5)
                        for h in range(4):
                            bk = 6 + h // 2
                            STT(yn[:, h * 256:(h + 1) * 256], PB[bk][:, (h % 2) * 256:(h % 2 + 1) * 256], ss4[:, 8 + h:9 + h],
                                silg[:, h * 256:(h + 1) * 256], ALU.mult, ALU.mult, [b_PB[bk], b_ss4, b_silg], [b_yn])

                    def ret_B2(tc):
                        p = tc % 2
                        ts_ = slice(tc * 128, (tc + 1) * 128)
                        yn = ryn2[p]; b_yn = b_ryn2[p]
                        for c in range(8):
                            bk = c // 4
                            PE(PB[bk][:, (c % 4) * 128:(c % 4 + 1) * 128], yn[:, c * 128:(c + 1) * 128], ident_b[:], True, True,
                               [b_yn, b_c2], [b_PB[bk]])
                        for c in range(8):
                            bk = c // 4
                            if c % 2:
                                CP("act", ybr[:, c, ts_], PB[bk][:, (c % 4) * 128:(c % 4 + 1) * 128], [b_PB[bk]], [b_ybr[c]])
                            else:
                                TS("dve", ybr[:, c, ts_], PB[bk][:, (c % 4) * 128:(c % 4 + 1) * 128], 1.0, None, ALU.mult, None,
                                   [b_PB[bk]], [b_ybr[c]])

                    ret_A(0)
                    for tc in range(4):
                        if tc < 3:
                            ret_A(tc + 1)
                        if tc > 0:
                            ret_B2(tc - 1)
                        ret_B(tc)
                    ret_B2(3)
                    for s_ in sv + sg_:
                        w_rel(s_)
                    branch_merge(l, 2, gsb, b_gsb, tmpm, b_tmpm, merged_b, b_mgb)
                    for half in range(2):
                        sl = w_get(l, "WO" + "ab"[half])
                        for j in range(4):
                            oc = half * 4 + j
                            bk = nextbank([0, 1, 2, 3])
                            for kc in range(8):
                                PE(PB[bk][:], wbuf[sl][:, kc * 512 + j * 128:kc * 512 + (j + 1) * 128], merged_b[:, kc, :],
                                   kc == 0, kc == 7, [b_wbuf[sl], b_mgb[kc]], [b_PB[bk]])
                            TT("dve", x_fm[:, oc, :], x_fm[:, oc, :], PB[bk][:], ALU.add, [b_x[oc], b_PB[bk]], [b_x[oc]])
                        w_rel(sl)
                    S.barrier()

                rmsnorm_to_hT(POFF["g_ffn"], P, bP)
                with Phase('ffn', STAGES) as ph:
                    ph.check()
                    raw = [sb("fraw%d" % i, [128, NT + 2], F32, ph) for i in range(2)]; b_raw = bufs(2)
                    acc = [sb("facc%d" % i, [128, NT], F32, ph) for i in range(2)]; b_acc = bufs(2)
                    gel = [sb("fgel%d" % i, [128, NT], F32, ph) for i in range(2)]; b_gel = bufs(2)
                    g_fm = sb("g_fm", [128, 24, NT], BF16, ph); b_g = bufs(24)
                    for ui in range(12):
                        sl = w_get(l, "F%d" % ui)
                        for jj in range(2):
                            j = ui * 2 + jj
                            q = j % 2
                            ba = nextbank([0, 1, 2, 3])
                            proj_fm(sl, jj, ba)
                            bu = nextbank([4, 5, 6, 7])
                            proj_fm(sl, 2 + jj, bu)
                            CP("act", raw[q][:, 2:NT + 2], PB[ba][:], [b_PB[ba]], [b_raw[q]])
                            CP("pool", raw[q][:, 0:2], halo_f[l][:, j, :], [b_halo_f[l][j]], [b_raw[q]])
                            CP("pool", halo_f[l][:, j, :], raw[q][:, NT:NT + 2], [b_raw[q]], [b_halo_f[l][j]])
                            wo_ = POFF["fcw"] + j * 3
                            TS("dve", acc[q][:], raw[q][:, 0:NT], P[:, wo_:wo_ + 1], P[:, POFF["fcb"] + j:POFF["fcb"] + j + 1],
                               ALU.mult, ALU.add, [b_raw[q], bP], [b_acc[q]])
                            for k in range(1, 3):
                                STT(acc[q][:], raw[q][:, k:k + NT], P[:, wo_ + k:wo_ + k + 1], acc[q][:], ALU.mult, ALU.add,
                                    [b_raw[q], bP, b_acc[q]], [b_acc[q]])
                            ACT(gel[q][:], acc[q][:], AF.Gelu, [b_acc[q]], [b_gel[q]])
                            TT("dve", g_fm[:, j, :], gel[q][:], PB[bu][:], ALU.mult, [b_gel[q], b_PB[bu]], [b_g[j]])
                        w_rel(sl)
                    for oc in range(8):
                        sl = w_get(l, "FO%d" % oc)
                        bk = nextbank([0, 1, 2, 3])
                        for kc in range(24):
                            PE(PB[bk][:], wbuf[sl][:, kc * 128:(kc + 1) * 128], g_fm[:, kc, :], kc == 0, kc == 23,
                               [b_wbuf[sl], b_g[kc]], [b_PB[bk]])
                        TT("dve", x_fm[:, oc, :], x_fm[:, oc, :], PB[bk][:], ALU.add, [b_x[oc], b_PB[bk]], [b_x[oc]])
                        w_rel(sl)
                    S.barrier()

            with Phase('final', STAGES) as ph:
                ph.check()
                yf = sb("yf", [128, 8, NT], F32, ph); b_yf = bufs(8)
                otm = [sb("otm%d" % i, [128, D], F32, ph) for i in range(2)]; b_otm = bufs(2)
                sq32 = [sb("sq32_%d" % i, [128, NT], F32, ph) for i in range(2)]; b_sq32 = bufs(2)
                P = par[0]; bP = b_par[0]
                for c in range(8):
                    q = c % 2
                    ACT(sq32[q][:], x_fm[:, c, :], AF.Square, [b_x[c]], [b_sq32[q]])
                    PE(PB[7][:], ones_f[:], sq32[q][:], c == 0, c == 7, [b_c2, b_sq32[q]], [b_PB[7]])
                ACT(lnv[:], PB[7][:], AF.Ln, [b_PB[7]], [b_lnv], bias=EPS, scale=1.0 / D)
                ACT(rstd[:], lnv[:], AF.Exp, [b_lnv], [b_rstd], scale=-0.5)
                for c in range(8):
                    STT(yf[:, c, :], x_fm[:, c, :], P[:, POFF["g_fin"] + c:POFF["g_fin"] + c + 1], rstd[:], ALU.mult, ALU.mult,
                        [b_x[c], bP, b_rstd], [b_yf[c]])
                for tc in range(4):
                    q = tc % 2
                    for c in range(8):
                        bk = (tc % 2) * 2 + c // 4
                        PE(PB[bk][:, (c % 4) * 128:(c % 4 + 1) * 128], yf[:, c, tc * 128:(tc + 1) * 128], IDf, True, True,
                           [b_yf[c], b_cst], [b_PB[bk]])
                    for hh in range(2):
                        bk = (tc % 2) * 2 + hh
                        CP("act" if hh else "dve", otm[q][:, hh * 512:(hh + 1) * 512], PB[bk][:], [b_PB[bk]], [b_otm[q]])
                    S.dma("sp", out_d[t0 + tc * 128:t0 + (tc + 1) * 128, :], otm[q][:], [b_otm[q]], [b_out])
                S.final_wait("sp", [b_out])
                S.barrier(with_sp=True)
        S.prog["sp"].append(([((("d", j)), 16 * S.dcnt[j]) for j in range(NDS) if S.dcnt[j] > 0], None, None))
        S.emit()
    return nc


def _bias_ap_fix():
    pass


_CACHE = {}


def prep_inputs(inputs, NBLK, batch):
    T = NBLK * NT
    key = "shared"
    ws = np.stack([host_wstream(inputs, l) for l in range(2)], axis=0)
    par = np.stack([host_params(inputs, l) for l in range(2)], axis=0)
    cst = host_consts()
    rc, rs = host_rope(T)
    maps = []
    for b in batch:
        maps.append({"x": np.ascontiguousarray(inputs["x"][b, :T, :]), "ws": ws, "par": par, "cst": cst,
                     "rcos": rc, "rsin": rs})
    return maps


def kernel(**inputs):
    inputs = {k: np.asarray(v) for k, v in inputs.items()}
    B, SEQ, _ = inputs["x"].shape
    NBLK = SEQ // NT
    if NBLK not in _CACHE:
        _CACHE[NBLK] = build(NBLK)
    nc = _CACHE[NBLK]
    n_cores = 8
    batch = [i % B for i in range(n_cores)]
    maps = prep_inputs(inputs, NBLK, batch)
    res = run_bass_kernel_spmd(nc, maps, core_ids=list(range(n_cores)))
    out = np.stack([res.results[b]["out"] for b in range(B)], axis=0)
    return out.astype(np.float32)
```

```python
import math
from contextlib import ExitStack
import numpy as np
import concourse.bass as bass
import concourse.mybir as mybir
from concourse.bass_utils import run_bass_kernel_spmd

F32 = mybir.dt.float32
BF16 = mybir.dt.bfloat16
ALU = mybir.AluOpType
AF = mybir.ActivationFunctionType
NDS = 24
EPS = 1e-6
NT = 512
D = 1024
GAM = [1.0 - 2.0 ** (-5.0 - h) for h in range(4)]


class Buf:
    __slots__ = ("w", "r", "x")

    def __init__(self, excl=False):
        self.w = None
        self.r = []
        self.x = excl


def bufs(n):
    return [Buf() for _ in range(n)]


class Sched:
    def __init__(self, nc, es):
        self.nc = nc
        self.names = ["pe", "act", "dve", "pool", "sp"]
        self.sem = {k: es.enter_context(nc.semaphore("s_" + k)) for k in self.names}
        self.dsem = [es.enter_context(nc.semaphore("d%d" % i)) for i in range(NDS)]
        self.cnt = {k: 0 for k in self.names}
        self.dcnt = [0] * NDS
        self.dnext = 0
        self.dnext_pool = 0
        self.waited = {k: {} for k in self.names}
        self.prog = {k: [] for k in self.names}

    def _semobj(self, key):
        return self.sem[key] if isinstance(key, str) else self.dsem[key[1]]

    def _deps(self, eng, reads, writes, extra=()):
        need = {}
        for b in reads:
            if b.w is not None:
                k, v = b.w
                if need.get(k, 0) < v:
                    need[k] = v
        for b in writes:
            if b.w is not None:
                k, v = b.w
                if need.get(k, 0) < v:
                    need[k] = v
            for (k, v) in b.r:
                if need.get(k, 0) < v:
                    need[k] = v
        for (k, v) in extra:
            if need.get(k, 0) < v:
                need[k] = v
        waits = []
        wd = self.waited[eng]
        for k, v in need.items():
            if k == "pe" and eng == "pe":
                continue
            if wd.get(k, 0) < v:
                waits.append((k, v))
                wd[k] = v
        return waits

    def _commit(self, tok, reads, writes):
        for b in reads:
            b.r.append(tok)
        for b in writes:
            b.w = tok
            b.r = []

    def op(self, eng, fn, reads=(), writes=()):
        xr = [b for b in reads if b.x and b not in writes]
        extra = []
        if xr:
            reads = [b for b in reads if b not in xr]
            for b in xr:
                if b.w is not None:
                    extra.append(b.w)
                for (k, v) in b.r:
                    if k != eng:
                        extra.append((k, v))
        waits = self._deps(eng, reads, writes, extra)
        self.cnt[eng] += 1
        tok = (eng, self.cnt[eng])
        self.prog[eng].append((waits, fn, (eng, 1)))
        self._commit(tok, reads, writes)
        for b in xr:
            b.r.append(tok)
        return tok

    def dma(self, eng, out, in_, reads=(), writes=()):
        if eng == "pool":
            j = 16 + self.dnext_pool
            self.dnext_pool = (self.dnext_pool + 1) % 8
        else:
            j = self.dnext
            self.dnext = (self.dnext + 1) % 16
        extra = []
        if self.dcnt[j] > 0:
            extra.append((("d", j), 16 * self.dcnt[j]))
        waits = self._deps(eng, reads, writes, extra)
        self.dcnt[j] += 1
        tok = (("d", j), 16 * self.dcnt[j])

        def fn(e, out=out, in_=in_):
            return e.dma_start(out=out, in_=in_)

        self.prog[eng].append((waits, fn, (("d", j), 16)))
        self._commit(tok, reads, writes)
        return tok

    def barrier(self, with_sp=False):
        ce = ["pe", "act", "dve", "pool"]
        for e in ["act", "dve", "pool"] + (["sp"] if with_sp else []):
            waits = []
            for o in ce:
                if o == e or self.cnt[o] == 0:
                    continue
                if self.waited[e].get(o, 0) < self.cnt[o]:
                    waits.append((o, self.cnt[o]))
                    self.waited[e][o] = self.cnt[o]
            if waits:
                self.prog[e].append((waits, None, None))

    def final_wait(self, eng, bl):
        waits = self._deps(eng, bl, ())
        self.prog[eng].append((waits, None, None))

    def emit(self):
        import bisect
        nc = self.nc
        needed = {k: set() for k in self.names}
        for engname in self.names:
            for waits, fn, inc in self.prog[engname]:
                for (k, v) in waits:
                    if isinstance(k, str):
                        needed[k].add(v)
        ranks = {k: sorted(v) for k, v in needed.items()}

        def semval(k, v):
            if isinstance(k, str):
                return bisect.bisect_right(ranks[k], v)
            return v

        with nc.Block() as block:
            def run(engname, e):
                idx = 0
                for waits, fn, inc in self.prog[engname]:
                    for (k, v) in waits:
                        e.wait_ge(self._semobj(k), semval(k, v))
                    if fn is None:
                        continue
                    ins = fn(e)
                    if isinstance(inc[0], str):
                        idx += 1
                        if idx in needed[engname]:
                            ins.then_inc(self._semobj(inc[0]), 1)
                    else:
                        ins.then_inc(self._semobj(inc[0]), inc[1])

            @block.tensor
            def _(e):
                run("pe", e)

            @block.scalar
            def _(e):
                run("act", e)

            @block.vector
            def _(e):
                run("dve", e)

            @block.gpsimd
            def _(e):
                run("pool", e)

            @block.sync
            def _(e):
                run("sp", e)
        self.stats = {k: (self.cnt[k], len(ranks[k])) for k in self.names}


C_Z, C_XS, C_B, C_C, C_DT = 0, 1024, 2048, 2304, 2560
C_LY, C_LX, C_Q, C_K, C_V, C_G, C_GATE = 2576, 3600, 4624, 5136, 5648, 6672, 7696
_PERM = np.concatenate([np.arange(0, 128, 2), np.arange(1, 128, 2)])
_SWAP = np.concatenate([np.arange(1, 128, 2), np.arange(0, 128, 2)])


def unit_list():
    u = [("XS0", 4096), ("XS1", 4096), ("BC", 4096), ("DT", 128), ("Z0", 4096), ("Z1", 4096),
         ("WB0a", 4096), ("G0a", 4096), ("WB0b", 4096), ("G0b", 4096),
         ("LX0", 4096), ("LX1", 4096), ("LG", 4096), ("LY0", 4096), ("LY1", 4096),
         ("WB1a", 4096), ("G1a", 4096), ("WB1b", 4096), ("G1b", 4096),
         ("Q", 4096), ("QS", 4096), ("K", 4096), ("KS", 4096), ("V0", 4096), ("V1", 4096),
         ("GR0", 4096), ("GR1", 4096),
         ("WB2a", 4096), ("G2a", 4096), ("WB2b", 4096), ("G2b", 4096),
         ("WOa", 4096), ("WOb", 4096)]
    u += [("F%d" % i, 4096) for i in range(12)]
    u += [("FO%d" % i, 3072) for i in range(8)]
    return u


UNITS = unit_list()
WTOT = sum(n for _, n in UNITS)


def _kn(w):
    K, n = w.shape
    return np.ascontiguousarray(w.reshape(K // 128, 128, n).transpose(1, 0, 2)).reshape(128, (K // 128) * n)


def host_wstream(inp, l):
    w_in = inp["w_in"][l]
    wb = inp["w_branch"][l]
    parts = {}
    parts["XS0"] = _kn(w_in[:, C_XS:C_XS + 512])
    parts["XS1"] = _kn(w_in[:, C_XS + 512:C_XS + 1024])
    parts["BC"] = _kn(w_in[:, C_B:C_B + 512])
    parts["DT"] = _kn(w_in[:, C_DT:C_DT + 16])
    parts["Z0"] = _kn(w_in[:, 0:512])
    parts["Z1"] = _kn(w_in[:, 512:1024])
    for k in range(3):
        parts["WB%da" % k] = _kn(wb[k][:, 0:512])
        parts["WB%db" % k] = _kn(wb[k][:, 512:1024])
        parts["G%da" % k] = _kn(w_in[:, C_GATE + k * 1024:C_GATE + k * 1024 + 512])
        parts["G%db" % k] = _kn(w_in[:, C_GATE + k * 1024 + 512:C_GATE + (k + 1) * 1024])
    parts["LX0"] = _kn(w_in[:, C_LX:C_LX + 512])
    parts["LX1"] = _kn(w_in[:, C_LX + 512:C_LX + 1024])
    lg = []
    for m in ("lru_w_a", "lru_w_i"):
        for k in range(4):
            lg.append(_kn(inp[m][l][k]))
    parts["LG"] = np.concatenate(lg, axis=1)
    parts["LY0"] = _kn(w_in[:, C_LY:C_LY + 512])
    parts["LY1"] = _kn(w_in[:, C_LY + 512:C_LY + 1024])
    pcols = np.concatenate([h * 128 + _PERM for h in range(4)])
    scols = np.concatenate([h * 128 + _SWAP for h in range(4)])
    parts["Q"] = _kn(w_in[:, C_Q + pcols])
    parts["QS"] = _kn(w_in[:, C_Q + scols])
    parts["K"] = _kn(w_in[:, C_K + pcols])
    parts["KS"] = _kn(w_in[:, C_K + scols])
    parts["V0"] = _kn(w_in[:, C_V:C_V + 512])
    parts["V1"] = _kn(w_in[:, C_V + 512:C_V + 1024])
    parts["GR0"] = _kn(w_in[:, C_G:C_G + 512])
    parts["GR1"] = _kn(w_in[:, C_G + 512:C_G + 1024])
    wo = inp["w_o"][l]
    parts["WOa"] = _kn(wo[:, 0:512])
    parts["WOb"] = _kn(wo[:, 512:1024])
    fi = inp["ffn_w_in"][l]
    for i in range(12):
        cols = np.concatenate([np.arange(i * 256, i * 256 + 256), 3072 + np.arange(i * 256, i * 256 + 256)])
        parts["F%d" % i] = _kn(fi[:, cols])
    fo = inp["ffn_w_out"][l]
    for i in range(8):
        parts["FO%d" % i] = _kn(fo[:, i * 128:(i + 1) * 128])
    out = np.empty((128, WTOT), np.float32)
    o = 0
    for name, n in UNITS:
        a = parts[name]
        assert a.shape == (128, n), (name, a.shape, n)
        out[:, o:o + n] = a
        o += n
    return out


def _par_layout():
    off = {}
    o = 0
    for name, n in [("g_mix", 8), ("g_ffn", 8), ("scw", 48), ("scb", 12), ("snorm", 8), ("lcw", 32), ("lcb", 8),
                    ("lba", 8), ("lbi", 8), ("llam", 8), ("fcw", 72), ("fcb", 24), ("dtb", 16), ("alog", 16),
                    ("dsk", 16), ("g_fin", 8)]:
        off[name] = o
        o += n
    return off, o


POFF, NPAR = _par_layout()


def _pp(v):
    return np.ascontiguousarray(v.reshape(-1, 128).T)


def host_params(inp, l):
    P = np.zeros((128, NPAR), np.float32)

    def put(name, arr):
        P[:, POFF[name]:POFF[name] + arr.shape[1]] = arr

    put("g_mix", _pp(inp["norm_mix"][l]))
    put("g_ffn", _pp(inp["norm_ffn"][l]))
    scw = inp["ssd_conv_w"][l]
    put("scw", np.stack([_pp(scw[k]) for k in range(4)], axis=2).reshape(128, 48))
    put("scb", _pp(inp["ssd_conv_b"][l]))
    put("snorm", _pp(inp["ssd_norm"][l]))
    lcw = inp["lru_conv_w"][l]
    put("lcw", np.stack([_pp(lcw[k]) for k in range(4)], axis=2).reshape(128, 32))
    put("lcb", _pp(inp["lru_conv_b"][l]))
    put("lba", _pp(inp["lru_b_a"][l]))
    put("lbi", _pp(inp["lru_b_i"][l]))
    put("llam", _pp(inp["lru_lambda"][l]))
    fcw = inp["ffn_conv_w"][l]
    put("fcw", np.stack([_pp(fcw[k]) for k in range(3)], axis=2).reshape(128, 72))
    put("fcb", _pp(inp["ffn_conv_b"][l]))
    put("dtb", np.tile(inp["ssd_dt_bias"][l][None, :], (128, 1)))
    put("alog", np.tile(inp["ssd_a_log"][l][None, :], (128, 1)))
    put("dsk", np.tile(inp["ssd_d"][l][None, :], (128, 1)))
    put("g_fin", _pp(inp["norm_final"]))
    return P


COFF = {"ident": 0, "tri": 128, "dmask": 256, "qd": 768, "kdec": 1280}
NCST = 1284


def host_consts():
    C = np.zeros((128, NCST), np.float32)
    C[:, 0:128] = np.eye(128, dtype=np.float32)
    p = np.arange(128)
    C[:, 128:256] = (p[:, None] <= p[None, :]).astype(np.float32)
    sc = 128.0 ** -0.5
    for h in range(4):
        lg = math.log(GAM[h])
        diff = (p[None, :] - p[:, None]).astype(np.float64)
        dm = np.where(diff >= 0, np.exp(np.maximum(diff, 0) * lg), 0.0) * sc
        C[:, 256 + h * 128:256 + (h + 1) * 128] = dm.astype(np.float32)
        C[:, 768 + h * 128:768 + (h + 1) * 128] = np.exp((p + 1.0) * lg).astype(np.float32)[None, :]
        C[:, 1280 + h] = (np.exp((127.0 - p) * lg) * sc).astype(np.float32)
    return C


def host_rope(S):
    inv = (1.0 / (10000.0 ** np.linspace(0.0, 1.0, 64, dtype=np.float32))).astype(np.float32)
    ang = (np.arange(S, dtype=np.float32)[:, None] * inv[None]).astype(np.float32)
    c = np.cos(ang).astype(np.float32).T
    s = np.sin(ang).astype(np.float32).T
    rc = np.concatenate([c, c], axis=0)
    rs = np.concatenate([-s, s], axis=0)
    return np.ascontiguousarray(rc), np.ascontiguousarray(rs)


class _SkipPhase(Exception):
    pass


class Phase(ExitStack):
    def __init__(self, name, stages):
        super().__init__()
        self.name = name
        self.stages = stages

    def check(self):
        if self.stages is not None and self.name not in self.stages:
            raise _SkipPhase()

    def __exit__(self, et, ev, tb):
        r = super().__exit__(et, ev, tb)
        return r or (et is _SkipPhase)


_PH_UNITS = {"ssd": ("XS", "BC", "DT", "Z", "WB0", "G0"), "lru": ("LX", "LG", "LY", "WB1", "G1"),
             "ret": ("Q", "K", "V", "GR", "WB2", "G2", "WO"), "ffn": ("F",)}


def _unit_phase(name):
    for ph, pre in _PH_UNITS.items():
        for p in pre:
            if name.startswith(p) and not (p == "F" and False):
                return ph
    raise KeyError(name)


def build(NBLK, STAGES=None):
    T = NBLK * NT
    nc = bass.Bass("TRN2", target_bir_lowering=False)
    x_d = nc.dram_tensor("x", [T, D], F32, kind="ExternalInput").ap()
    ws_d = nc.dram_tensor("ws", [2, 128, WTOT], F32, kind="ExternalInput").ap()
    par_d = nc.dram_tensor("par", [2, 128, NPAR], F32, kind="ExternalInput").ap()
    cst_d = nc.dram_tensor("cst", [128, NCST], F32, kind="ExternalInput").ap()
    rc_d = nc.dram_tensor("rcos", [128, T], F32, kind="ExternalInput").ap()
    rs_d = nc.dram_tensor("rsin", [128, T], F32, kind="ExternalInput").ap()
    out_d = nc.dram_tensor("out", [T, D], F32, kind="ExternalOutput").ap()
    wscr = nc.dram_tensor("wscr", [2, 128, WTOT], BF16).ap()

    es = ExitStack()
    with es:
        S = Sched(nc, es)

        uid = [0]

        def on(name):
            return STAGES is None or name in STAGES

        def var(name):
            return STAGES is not None and name in STAGES

        def sb(name, shape, dt=F32, stack=es):
            uid[0] += 1
            return stack.enter_context(nc.sbuf_tensor("%s_%d" % (name, uid[0]), shape, dt))

        cst = sb("cst_sb", [128, NCST]); b_cst = Buf()
        par = [sb("par_sb%d" % l, [128, NPAR]) for l in range(2)]; b_par = bufs(2)
        ident_b = sb("ident_b", [128, 128], BF16)
        ones_b = sb("ones_b", [128, 128], BF16)
        ones_f = sb("ones_f", [128, 128])
        b_c2 = Buf()
        A_bc = [sb("A_bc%d" % l, [128, 16]) for l in range(2)]
        cpar = [sb("cpar%d" % l, [128, 8]) for l in range(2)]
        c2par = [sb("c2par%d" % l, [128, 8]) for l in range(2)]
        b_der = bufs(2)
        Sssd = [sb("Sssd%d" % l, [128, 1024]) for l in range(2)]; b_Sssd = bufs(2)
        Sssd_b = [sb("Sssdb%d" % l, [128, 1024], BF16) for l in range(2)]; b_Sssd_b = bufs(2)
        Sret = [sb("Sret%d" % l, [128, 1024]) for l in range(2)]; b_Sret = bufs(2)
        Sret_b = [sb("Sretb%d" % l, [128, 1024], BF16) for l in range(2)]; b_Sret_b = bufs(2)
        halo_s = [sb("halo_s%d" % l, [128, 12, 3]) for l in range(2)]; b_halo_s = [bufs(12) for _ in range(2)]
        halo_l = [sb("halo_l%d" % l, [128, 8, 3]) for l in range(2)]; b_halo_l = [bufs(8) for _ in range(2)]
        halo_f = [sb("halo_f%d" % l, [128, 24, 2]) for l in range(2)]; b_halo_f = [bufs(24) for _ in range(2)]
        hlru = [sb("hlru%d" % l, [128, 8]) for l in range(2)]; b_hlru = [bufs(8) for _ in range(2)]
        x_fm = sb("x_fm", [128, 8, NT]); b_x = bufs(8)
        hT = sb("hT", [128, 8, NT], BF16); b_hT = bufs(8)
        merged = sb("merged", [128, 8, NT]); b_mg = bufs(8)
        ybr = sb("ybr", [128, 8, NT], BF16); b_ybr = bufs(8)
        rstd = sb("rstd", [128, NT]); b_rstd = Buf()
        lnv = sb("lnv", [128, NT]); b_lnv = Buf()
        sqt = [sb("sqt%d" % i, [128, NT], BF16) for i in range(2)]; b_sqt = bufs(2)
        spt = [sb("spt%d" % i, [128, 64]) for i in range(3)]; b_spt = Buf()
        NSLOT = 4
        wbuf = [sb("wbuf%d" % i, [128, 4096], BF16) for i in range(NSLOT)]; b_wbuf = bufs(NSLOT)
        PB = [es.enter_context(nc.psum_tensor("pb%d" % i, [128, 512], F32)) for i in range(8)]
        b_PB = [Buf(True) for _ in range(8)]

        IDf = cst[:, 0:128]
        TRI = cst[:, 128:256]

        def PE(out, lhsT, rhs, start, stop, R, W):
            S.op("pe", lambda e: e.matmul(out, lhsT=lhsT, rhs=rhs, start=start, stop=stop), R, W)

        def ACT(out, in_, func, R, W, bias=None, scale=None, accum=None):
            kw = {}
            if bias is not None:
                kw["bias"] = bias
            if scale is not None:
                kw["scale"] = scale
            if accum is not None:
                kw["accum_out"] = accum
            S.op("act", lambda e: e.activation(out=out, in_=in_, func=func, **kw), R, W)

        def TT(eng, out, a, b, op, R, W):
            S.op(eng, lambda e: e.tensor_tensor(out=out, in0=a, in1=b, op=op), R, W)

        def TS(eng, out, a, s1, s2, op0, op1, R, W):
            if s2 is None:
                S.op(eng, lambda e: e.tensor_scalar(out=out, in0=a, scalar1=s1, scalar2=None, op0=op0), R, W)
            else:
                S.op(eng, lambda e: e.tensor_scalar(out=out, in0=a, scalar1=s1, scalar2=s2, op0=op0, op1=op1), R, W)

        def STT(out, in0, scalar, in1, op0, op1, R, W):
            S.op("dve", lambda e: e.scalar_tensor_tensor(out=out, in0=in0, scalar=scalar, in1=in1, op0=op0, op1=op1), R, W)

        def SCAN(out, d0, d1, init, R, W):
            S.op("dve", lambda e: e.tensor_tensor_scan(out=out, data0=d0, data1=d1, initial=init, op0=ALU.mult, op1=ALU.add), R, W)

        def CP(eng, out, in_, R, W):
            if eng == "act":
                S.op("act", lambda e: e.copy(out=out, in_=in_), R, W)
            else:
                S.op(eng, lambda e: e.tensor_copy(out=out, in_=in_), R, W)

        def MSET(eng, ap, val, W):
            S.op(eng, lambda e: e.memset(ap, val), (), W)

        def precast():
            PW = 4096
            with Phase('precast', STAGES) as ph:
                ph.check()
                NB_ = 3
                stg = [sb("stg%d" % i, [128, PW], F32, ph) for i in range(NB_)]; b_stg = bufs(NB_)
                stb = [sb("stb%d" % i, [128, PW], BF16, ph) for i in range(NB_)]; b_stb = bufs(NB_)
                n = 0
                for l in range(2):
                    pl = []
                    for lo in range(0, WTOT, PW):
                        hi = min(WTOT, lo + PW)
                        q = n % NB_
                        S.dma("sp", stg[q][:, 0:hi - lo], ws_d[l, :, lo:hi], (), [b_stg[q]])
                        CP(("dve", "pool")[n % 2], stb[q][:, 0:hi - lo], stg[q][:, 0:hi - lo], [b_stg[q]], [b_stb[q]])
                        b = Buf()
                        S.dma("act", wscr[l, :, lo:hi], stb[q][:, 0:hi - lo], [b_stb[q]], [b])
                        pl.append((lo, hi, b))
                        n += 1
                    pieces.append(pl)
                S.final_wait("sp", [b_ for pl in pieces for (_, _, b_) in pl])
                S.barrier(with_sp=True)

        pieces = []

        stream = []
        for blk in range(NBLK):
            for l in range(2):
                o = 0
                for name, n in UNITS:
                    if STAGES is None or _unit_phase(name) in STAGES:
                        stream.append((l, name, o, n))
                    o += n
        wstate = {"next_load": 0, "next_get": 0}
        slot_of = {}

        def w_load_next(slot):
            i = wstate["next_load"]
            if i >= len(stream):
                return
            l, name, o, n = stream[i]
            deps = [b for (lo, hi, b) in pieces[l] if lo < o + n and hi > o]
            S.dma("sp", wbuf[slot][:, 0:n], wscr[l, :, o:o + n], deps, [b_wbuf[slot]])
            slot_of[i] = slot
            wstate["next_load"] = i + 1

        def w_get(l, name):
            i = wstate["next_get"]
            assert stream[i][0] == l and stream[i][1] == name, (stream[i], l, name)
            wstate["next_get"] = i + 1
            s = slot_of[i]
            return s

        def w_rel(slot):
            w_load_next(slot)

        precast()
        S.dma("sp", cst[:], cst_d, (), [b_cst])
        for l in range(2):
            S.dma("sp", par[l][:], par_d[l], (), [b_par[l]])
        for s_ in range(NSLOT):
            w_load_next(s_)
        CP("dve", ident_b[:], IDf, [b_cst], [b_c2])
        MSET("pool", ones_b[:], 1.0, [b_c2])
        MSET("pool", ones_f[:], 1.0, [b_c2])
        for l in range(2):
            P = par[l]
            ACT(A_bc[l][:], P[:, POFF["alog"]:POFF["alog"] + 16], AF.Exp, [b_par[l]], [b_der[l]])
            TS("dve", A_bc[l][:], A_bc[l][:], -1.0, None, ALU.mult, None, [b_der[l]], [b_der[l]])
            MSET("pool", Sssd[l][:], 0.0, [b_Sssd[l]])
            MSET("pool", Sssd_b[l][:], 0.0, [b_Sssd_b[l]])
            MSET("pool", Sret[l][:], 0.0, [b_Sret[l]])
            MSET("pool", Sret_b[l][:], 0.0, [b_Sret_b[l]])
            MSET("pool", halo_s[l][:], 0.0, b_halo_s[l])
            MSET("pool", halo_l[l][:], 0.0, b_halo_l[l])
            MSET("pool", halo_f[l][:], 0.0, b_halo_f[l])
            MSET("pool", hlru[l][:], 0.0, b_hlru[l])

        def softplus(dst, src, n, R, W):
            t0, t1, t2 = spt[0][:, 0:n], spt[1][:, 0:n], spt[2][:, 0:n]
            bs = [b_spt]
            ACT(t0, src, AF.Abs, R, bs)
            ACT(t0, t0, AF.Exp, bs, bs, scale=-1.0)
            TS("dve", t1, t0, 2.0, None, ALU.add, None, bs, bs)
            S.op("dve", lambda e: e.reciprocal(out=t1, in_=t1), bs, bs)
            TT("dve", t0, t0, t1, ALU.mult, bs, bs)
            TT("dve", t1, t0, t0, ALU.mult, bs, bs)
            TS("dve", t2, t1, 1.0 / 9, 1.0 / 7, ALU.mult, ALU.add, bs, bs)
            for cst_ in (1.0 / 5, 1.0 / 3, 1.0):
                TT("dve", t2, t2, t1, ALU.mult, bs, bs)
                TS("dve", t2, t2, cst_, None, ALU.add, None, bs, bs)
            TT("dve", t0, t0, t2, ALU.mult, bs, bs)
            TS("dve", t1, src, 0.0, None, ALU.max, None, list(R) + bs, bs)
            STT(dst, t0, 2.0, t1, ALU.mult, ALU.add, bs, W)

        for l in range(2):
            TS("dve", c2par[l][:], par[l][:, POFF["llam"]:POFF["llam"] + 8], -1.0, None, ALU.mult, None, [b_par[l]], [b_der[l]])
            softplus(cpar[l][:], c2par[l][:], 8, [b_der[l]], [b_der[l]])
            TS("dve", c2par[l][:], cpar[l][:], -16.0, None, ALU.mult, None, [b_der[l]], [b_der[l]])
            TS("dve", cpar[l][:], cpar[l][:], -8.0, None, ALU.mult, None, [b_der[l]], [b_der[l]])

        rr = {"pb": 0, "ev": 0}

        def nextbank(lst):
            i = lst[rr["pb"] % len(lst)]
            rr["pb"] += 1
            return i

        def rmsnorm_to_hT(goff, P, bP):
            bk = 7
            for c in range(8):
                q = c % 2
                ACT(sqt[q][:], x_fm[:, c, :], AF.Square, [b_x[c]], [b_sqt[q]])
                PE(PB[bk][:], ones_b[:], sqt[q][:], c == 0, c == 7, [b_c2, b_sqt[q]], [b_PB[bk]])
            ACT(lnv[:], PB[bk][:], AF.Ln, [b_PB[bk]], [b_lnv], bias=EPS, scale=1.0 / D)
            ACT(rstd[:], lnv[:], AF.Exp, [b_lnv], [b_rstd], scale=-0.5)
            for c in range(8):
                STT(hT[:, c, :], x_fm[:, c, :], P[:, goff + c:goff + c + 1], rstd[:], ALU.mult, ALU.mult,
                    [b_x[c], bP, b_rstd], [b_hT[c]])

        def proj_fm(slot, j, bank, ncols=512, R_extra=()):
            for kc in range(8):
                PE(PB[bank][:], wbuf[slot][:, kc * ncols + j * 128:kc * ncols + (j + 1) * 128], hT[:, kc, :],
                   kc == 0, kc == 7, [b_wbuf[slot], b_hT[kc]], [b_PB[bank]])

        def proj_tm(slot, tc, bank):
            for kc in range(8):
                PE(PB[bank][:], hT[:, kc, tc * 128:(tc + 1) * 128], wbuf[slot][:, kc * 512:(kc + 1) * 512],
                   kc == 0, kc == 7, [b_wbuf[slot], b_hT[kc]], [b_PB[bank]])

        def branch_merge(l, k, gsb, b_gsb, tmpm, b_tmpm, merged_b=None, b_mgb=None):
            for half in range(2):
                sw = w_get(l, "WB%d%s" % (k, "ab"[half]))
                sg = w_get(l, "G%d%s" % (k, "ab"[half]))
                for j in range(4):
                    oc = half * 4 + j
                    bd = nextbank([0, 1, 2, 3])
                    for kc in range(8):
                        PE(PB[bd][:], wbuf[sw][:, kc * 512 + j * 128:kc * 512 + (j + 1) * 128], ybr[:, kc, :],
                           kc == 0, kc == 7, [b_wbuf[sw], b_ybr[kc]], [b_PB[bd]])
                    bg = nextbank([4, 5, 6, 7])
                    proj_fm(sg, j, bg)
                    q = oc % 2
                    gq = gsb[q]
                    tq = tmpm[q][:, 0:NT]
                    ACT(gq, PB[bg][:], AF.Sigmoid, [b_PB[bg]], [b_gsb[q]])
                    if k == 0:
                        TT("dve", merged[:, oc, :], gq, PB[bd][:], ALU.mult, [b_gsb[q], b_PB[bd]], [b_mg[oc]])
                    else:
                        TT("dve", tq, gq, PB[bd][:], ALU.mult, [b_gsb[q], b_PB[bd]], [b_tmpm[q]])
                        if k == 1:
                            TT("pool", merged[:, oc, :], merged[:, oc, :], tq, ALU.add, [b_mg[oc], b_tmpm[q]], [b_mg[oc]])
                        else:
                            TT("pool", merged_b[:, oc, :], merged[:, oc, :], tq, ALU.add, [b_mg[oc], b_tmpm[q]], [b_mgb[oc]])
                w_rel(sw)
                w_rel(sg)

        b_out = Buf()
        for blk in range(NBLK):
            t0 = blk * NT
            with Phase('xload', STAGES) as ph:
                ph.check()
                xtm = [sb("xtm%d" % i, [128, D], F32, ph) for i in range(4)]; b_xtm = bufs(4)
                for tc in range(4):
                    S.dma("sp", xtm[tc][:], x_d[t0 + tc * 128:t0 + (tc + 1) * 128, :], (), [b_xtm[tc]])
                for c in range(8):
                    bk = c % 4
                    for tc in range(4):
                        PE(PB[bk][:, tc * 128:(tc + 1) * 128], xtm[tc][:, c * 128:(c + 1) * 128], IDf, True, True,
                           [b_xtm[tc], b_cst], [b_PB[bk]])
                    CP("act" if c % 2 == 0 else "dve", x_fm[:, c, :], PB[bk][:], [b_PB[bk]], [b_x[c]])
                S.barrier()

            for l in range(2):
                P = par[l]; bP = b_par[l]
                rmsnorm_to_hT(POFF["g_mix"], P, bP)

                with Phase('ssd', STAGES) as ph:
                    ph.check()
                    raw = [sb("raw%d" % i, [128, NT + 3], F32, ph) for i in range(2)]; b_raw = bufs(2)
                    acc = [sb("acc%d" % i, [128, NT], F32, ph) for i in range(2)]; b_acc = bufs(2)
                    xbc = sb("xbc", [128, 12, NT], BF16, ph); b_xbc = bufs(12)
                    dt_tm = sb("dt_tm", [128, 4, 16], F32, ph); b_dt = Buf()
                    a_tm = sb("a_tm", [128, 4, 16], F32, ph); b_a = Buf()
                    xdt = sb("xdt", [128, 1024], BF16, ph); b_xdt = Buf()
                    xdsd = [sb("xdsd%d" % i, [128, 1024], BF16, ph) for i in range(2)]; b_xdsd = bufs(2)
                    xsD = sb("xsD", [128, 1024], F32, ph); b_xsD = Buf()
                    B_tm = [sb("B_tm%d" % i, [128, 256], BF16, ph) for i in range(2)]; b_Btm = bufs(2)
                    CBm = sb("CBm", [128, 2, 128], F32, ph); b_CBm = Buf()
                    Et = [sb("Et%d" % i, [128, 4, 128], F32, ph) for i in range(2)]; b_Et = bufs(2)
                    MT = sb("MT", [128, 16, 128], BF16, ph); b_MT = bufs(4)
                    smA = [sb("smA%d" % i, [128, 8, 16], F32, ph) for i in range(2)]; b_smA = [bufs(8) for _ in range(2)]
                    smB = sb("smB", [128, 4], F32, ph); b_smB = Buf()
                    t1 = sb("t1", [128, 1024], F32, ph); b_t1 = Buf()
                    yd = [sb("yd%d" % i, [128, 1024], F32, ph) for i in range(2)]; b_yd = bufs(2)
                    silz = [sb("silz%d" % i, [128, 1024], F32, ph) for i in range(2)]; b_silz = bufs(2)
                    yn2 = [sb("yn%d" % i, [128, 1024], BF16, ph) for i in range(2)]; b_yn2 = bufs(2)
                    gsb = [r_[:, 0:NT] for r_ in raw]; b_gsb = b_raw
                    tmpm = acc; b_tmpm = b_acc

                    for ui, uname in enumerate(("XS0", "XS1", "BC")):
                        sl = w_get(l, uname)
                        for j in range(4):
                            cc = ui * 4 + j
                            bk = nextbank([0, 1, 2, 3])
                            proj_fm(sl, j, bk)
                            q = cc % 2
                            CP("act", raw[q][:, 3:NT + 3], PB[bk][:], [b_PB[bk]], [b_raw[q]])
                            CP("pool", raw[q][:, 0:3], halo_s[l][:, cc, :], [b_halo_s[l][cc]], [b_raw[q]])
                            CP("pool", halo_s[l][:, cc, :], raw[q][:, NT:NT + 3], [b_raw[q]], [b_halo_s[l][cc]])
                            wo_ = POFF["scw"] + cc * 4
                            TS("dve", acc[q][:], raw[q][:, 0:NT], P[:, wo_:wo_ + 1], P[:, POFF["scb"] + cc:POFF["scb"] + cc + 1],
                               ALU.mult, ALU.add, [b_raw[q], bP], [b_acc[q]])
                            for k in range(1, 4):
                                STT(acc[q][:], raw[q][:, k:k + NT], P[:, wo_ + k:wo_ + k + 1], acc[q][:], ALU.mult, ALU.add,
                                    [b_raw[q], bP, b_acc[q]], [b_acc[q]])
                            ACT(xbc[:, cc, :], acc[q][:], AF.Silu, [b_acc[q]], [b_xbc[cc]])
                        w_rel(sl)
                    sl = w_get(l, "DT")
                    for tc in range(4):
                        for kc in range(8):
                            PE(PB[4][:, tc * 16:(tc + 1) * 16], hT[:, kc, tc * 128:(tc + 1) * 128], wbuf[sl][:, kc * 16:(kc + 1) * 16],
                               kc == 0, kc == 7, [b_wbuf[sl], b_hT[kc]], [b_PB[4]])
                    w_rel(sl)
                    TT("dve", dt_tm[:], PB[4][:, 0:64].rearrange("p (a b) -> p a b", a=4),
                       P[:, POFF["dtb"]:POFF["dtb"] + 16].unsqueeze(1).to_broadcast([128, 4, 16]), ALU.add, [b_PB[4], bP], [b_dt])
                    dtv = dt_tm[:].rearrange("p a b -> p (a b)")
                    softplus(dtv, dtv, 64, [b_dt], [b_dt])
                    TT("dve", a_tm[:], dt_tm[:], A_bc[l][:].unsqueeze(1).to_broadcast([128, 4, 16]), ALU.mult, [b_dt, b_der[l]], [b_a])
                    sz0 = w_get(l, "Z0")
                    sz1 = w_get(l, "Z1")
                    def ssd_A(tc):
                        p = tc % 2
                        ts_ = slice(tc * 128, (tc + 1) * 128)
                        sm = smA[p]; b_sm = b_smA[p]
                        for zh, sz in enumerate((sz0, sz1)):
                            proj_tm(sz, tc, zh)
                            ACT(silz[p][:, zh * 512:(zh + 1) * 512], PB[zh][:], AF.Silu, [b_PB[zh]], [b_silz[p]])
                        for c in range(8):
                            bk = 2 + c // 4
                            PE(PB[bk][:, (c % 4) * 128:(c % 4 + 1) * 128], xbc[:, c, ts_], ident_b[:], True, True,
                               [b_xbc[c], b_c2], [b_PB[bk]])
                        for hh in range(2):
                            TT("dve", xdt[:, hh * 512:(hh + 1) * 512].rearrange("p (h d) -> p h d", h=8),
                               PB[2 + hh][:].rearrange("p (h d) -> p h d", h=8),
                               dt_tm[:, tc, hh * 8:(hh + 1) * 8].unsqueeze(2).to_broadcast([128, 8, 64]), ALU.mult,
                               [b_PB[2 + hh], b_dt], [b_xdt])
                            TT("dve", xsD[:, hh * 512:(hh + 1) * 512].rearrange("p (h d) -> p h d", h=8),
                               PB[2 + hh][:].rearrange("p (h d) -> p h d", h=8),
                               P[:, POFF["dsk"] + hh * 8:POFF["dsk"] + (hh + 1) * 8].unsqueeze(2).to_broadcast([128, 8, 64]), ALU.mult,
                               [b_PB[2 + hh], bP], [b_xsD])
                        for g in range(2):
                            PE(PB[4][:, g * 128:(g + 1) * 128], xbc[:, 8 + g, ts_], ident_b[:], True, True, [b_xbc[8 + g], b_c2], [b_PB[4]])
                        for g in range(2):
                            PE(PB[4][:, 256 + g * 128:256 + (g + 1) * 128], xbc[:, 8 + g, ts_], xbc[:, 10 + g, ts_], True, True,
                               [b_xbc[8 + g], b_xbc[10 + g]], [b_PB[4]])
                        CP("act", B_tm[p][:], PB[4][:, 0:256], [b_PB[4]], [b_Btm[p]])
                        TT("dve", CBm[:], PB[4][:, 256:512].rearrange("p (g l) -> p g l", g=2),
                           TRI.unsqueeze(1).to_broadcast([128, 2, 128]), ALU.mult, [b_PB[4], b_cst], [b_CBm])
                        PE(PB[5][:, 0:16], TRI, a_tm[:, tc, :], True, True, [b_cst, b_a], [b_PB[5]])
                        PE(PB[5][:, 16:32], ones_f[:], a_tm[:, tc, :], True, True, [b_c2, b_a], [b_PB[5]])
                        TS("dve", sm[:, 0, :], PB[5][:, 0:16], -1.0, None, ALU.mult, None, [b_PB[5]], [b_sm[0]])
                        CP("dve", sm[:, 1, :], PB[5][:, 0:16], [b_PB[5]], [b_sm[1]])
                        ACT(sm[:, 2, :], PB[5][:, 0:16], AF.Exp, [b_PB[5]], [b_sm[2]])
                        TT("dve", sm[:, 3, :], PB[5][:, 16:32], sm[:, 1, :], ALU.subtract, [b_PB[5], b_sm[1]], [b_sm[3]])
                        ACT(sm[:, 4, :], sm[:, 3, :], AF.Exp, [b_sm[3]], [b_sm[4]])
                        ACT(sm[:, 5, :], PB[5][:, 16:32], AF.Exp, [b_PB[5]], [b_sm[5]])
                        for hg in range(4):
                            bk = 6 + hg % 2
                            for i in range(4):
                                h = hg * 4 + i
                                PE(PB[bk][:, i * 128:(i + 1) * 128], a_tm[:, tc, h:h + 1].to_broadcast([128, 128]), TRI, True, True,
                                   [b_a, b_cst], [b_PB[bk]])
                            for i in range(4):
                                h = hg * 4 + i
                                ACT(Et[hg % 2][:, i, :], PB[bk][:, i * 128:(i + 1) * 128], AF.Abs, [b_PB[bk], b_sm[0]], [b_Et[hg % 2]],
                                    bias=sm[:, 0, h:h + 1])
                            ACT(Et[hg % 2][:], Et[hg % 2][:], AF.Exp, [b_Et[hg % 2]], [b_Et[hg % 2]], scale=-1.0)
                            TT("dve", MT[:, hg * 4:(hg + 1) * 4, :], Et[hg % 2][:],
                               CBm[:, hg // 2, :].unsqueeze(1).to_broadcast([128, 4, 128]), ALU.mult,
                               [b_Et[hg % 2], b_CBm], [b_MT[hg]])
                        for h in range(16):
                            bk = h // 8
                            PE(PB[bk][:, (h % 8) * 64:(h % 8 + 1) * 64], MT[:, h, :], xdt[:, h * 64:(h + 1) * 64], True, True,
                               [b_MT[h // 4], b_xdt], [b_PB[bk]])
                        for g in range(2):
                            TT("dve", yd[p][:, g * 512:(g + 1) * 512], PB[g][:], xsD[:, g * 512:(g + 1) * 512], ALU.add,
                               [b_PB[g], b_xsD], [b_yd[p]])
                        TT("pool", xdsd[p][:].rearrange("p (h d) -> p h d", h=16), xdt[:].rearrange("p (h d) -> p h d", h=16),
                           sm[:, 4, :].unsqueeze(2).to_broadcast([128, 16, 64]), ALU.mult, [b_xdt, b_sm[4]], [b_xdsd[p]])

                    def ssd_B(tc):
                        p = tc % 2
                        ts_ = slice(tc * 128, (tc + 1) * 128)
                        sm = smA[p]; b_sm = b_smA[p]
                        for g in range(2):
                            PE(PB[2 + g][:], xbc[:, 10 + g, ts_], Sssd_b[l][:, g * 512:(g + 1) * 512], True, True,
                               [b_xbc[10 + g], b_Sssd_b[l]], [b_PB[2 + g]])
                        for g in range(2):
                            PE(PB[6 + g][:], B_tm[p][:, g * 128:(g + 1) * 128], xdsd[p][:, g * 512:(g + 1) * 512], True, True,
                               [b_Btm[p], b_xdsd[p]], [b_PB[6 + g]])
                        TT("dve", Sssd[l][:].rearrange("p (h d) -> p h d", h=16), Sssd[l][:].rearrange("p (h d) -> p h d", h=16),
                           sm[:, 5, :].unsqueeze(2).to_broadcast([128, 16, 64]), ALU.mult, [b_Sssd[l], b_sm[5]], [b_Sssd[l]])
                        for g in range(2):
                            TT("dve", Sssd[l][:, g * 512:(g + 1) * 512], Sssd[l][:, g * 512:(g + 1) * 512], PB[6 + g][:], ALU.add,
                               [b_Sssd[l], b_PB[6 + g]], [b_Sssd[l]])
                        CP("act", Sssd_b[l][:], Sssd[l][:], [b_Sssd[l]], [b_Sssd_b[l]])
                        for g in range(2):
                            TT("dve", t1[:, g * 512:(g + 1) * 512].rearrange("p (h d) -> p h d", h=8),
                               PB[2 + g][:].rearrange("p (h d) -> p h d", h=8),
                               sm[:, 2, g * 8:(g + 1) * 8].unsqueeze(2).to_broadcast([128, 8, 64]), ALU.mult,
                               [b_PB[2 + g], b_sm[2]], [b_t1])
                        TT("dve", t1[:], t1[:], yd[p][:], ALU.add, [b_t1, b_yd[p]], [b_t1])
                        TT("dve", t1[:], t1[:], silz[p][:], ALU.mult, [b_t1, b_silz[p]], [b_t1])
                        yn = yn2[p]; b_yn = b_yn2[p]
                        MSET("pool", smB[:, 0:1], 0.0, [b_smB])
                        ACT(yn[:], t1[:], AF.Square, [b_t1], [b_yn, b_smB], accum=smB[:, 0:1])
                        ACT(smB[:, 1:2], smB[:, 0:1], AF.Ln, [b_smB], [b_smB], bias=EPS, scale=1.0 / 1024)
                        ACT(smB[:, 2:3], smB[:, 1:2], AF.Exp, [b_smB], [b_smB], scale=-0.5)
                        TS("dve", yn[:], t1[:], smB[:, 2:3], None, ALU.mult, None, [b_t1, b_smB], [b_yn])

                    def ssd_B2(tc):
                        p = tc % 2
                        ts_ = slice(tc * 128, (tc + 1) * 128)
                        yn = yn2[p]; b_yn = b_yn2[p]
                        for c in range(8):
                            bk = 4 + c // 4
                            PE(PB[bk][:, (c % 4) * 128:(c % 4 + 1) * 128], yn[:, c * 128:(c + 1) * 128], ident_b[:], True, True,
                               [b_yn, b_c2], [b_PB[bk]])
                        for c in range(8):
                            bk = 4 + c // 4
                            TS("dve", ybr[:, c, ts_], PB[bk][:, (c % 4) * 128:(c % 4 + 1) * 128],
                               P[:, POFF["snorm"] + c:POFF["snorm"] + c + 1], None, ALU.mult, None, [b_PB[bk], bP], [b_ybr[c]])

                    ssd_A(0)
                    for tc in range(4):
                        if tc < 3:
                            ssd_A(tc + 1)
                        if tc > 0:
                            ssd_B2(tc - 1)
                        ssd_B(tc)
                    ssd_B2(3)
                    w_rel(sz0)
                    w_rel(sz1)
                    branch_merge(l, 0, gsb, b_gsb, tmpm, b_tmpm)
                    S.barrier()

                with Phase('lru', STAGES) as ph:
                    ph.check()
                    raw = [sb("lraw%d" % i, [128, NT + 3], F32, ph) for i in range(2)]; b_raw = bufs(2)
                    xc = sb("xc", [128, 8, NT], F32, ph); b_xc = bufs(8)
                    xcb = sb("xcb", [128, 8, NT], BF16, ph); b_xcb = bufs(8)
                    rt = [sb("rt%d" % i, [128, NT], F32, ph) for i in range(2)]; b_rt = bufs(2)
                    it = [sb("it%d" % i, [128, NT], F32, ph) for i in range(2)]; b_it = bufs(2)
                    at = [sb("at%d" % i, [128, NT], F32, ph) for i in range(2)]; b_at = bufs(2)
                    a2t = [sb("a2t%d" % i, [128, NT], F32, ph) for i in range(2)]; b_a2t = bufs(2)
                    ut = [sb("ut%d" % i, [128, NT], F32, ph) for i in range(2)]; b_ut = bufs(2)
                    hs = [sb("hs%d" % i, [128, NT], F32, ph) for i in range(2)]; b_hs = bufs(2)
                    gl = [sb("gl%d" % i, [128, NT], F32, ph) for i in range(2)]; b_gl = bufs(2)
                    gsb = [r_[:] for r_ in rt]; b_gsb = b_rt
                    tmpm = it; b_tmpm = b_it
                    for ui, uname in enumerate(("LX0", "LX1")):
                        sl = w_get(l, uname)
                        for j in range(4):
                            c = ui * 4 + j
                            bk = nextbank([0, 1, 2, 3])
                            proj_fm(sl, j, bk)
                            q = c % 2
                            CP("act", raw[q][:, 3:NT + 3], PB[bk][:], [b_PB[bk]], [b_raw[q]])
                            CP("pool", raw[q][:, 0:3], halo_l[l][:, c, :], [b_halo_l[l][c]], [b_raw[q]])
                            CP("pool", halo_l[l][:, c, :], raw[q][:, NT:NT + 3], [b_raw[q]], [b_halo_l[l][c]])
                            wo_ = POFF["lcw"] + c * 4
                            TS("dve", xc[:, c, :], raw[q][:, 0:NT], P[:, wo_:wo_ + 1], P[:, POFF["lcb"] + c:POFF["lcb"] + c + 1],
                               ALU.mult, ALU.add, [b_raw[q], bP], [b_xc[c]])
                            for k in range(1, 4):
                                STT(xc[:, c, :], raw[q][:, k:k + NT], P[:, wo_ + k:wo_ + k + 1], xc[:, c, :], ALU.mult, ALU.add,
                                    [b_raw[q], bP, b_xc[c]], [b_xc[c]])
                            CP("act", xcb[:, c, :], xc[:, c, :], [b_xc[c]], [b_xcb[c]])
                        w_rel(sl)
                    slg = w_get(l, "LG")
                    sly = [w_get(l, "LY0"), w_get(l, "LY1")]
                    for c in range(8):
                        k, j, q = c // 2, c % 2, c % 2
                        for m, (dst, bdst, boff) in enumerate(((rt, b_rt, "lba"), (it, b_it, "lbi"))):
                            bk = nextbank([0, 1, 2, 3])
                            base = (m * 4 + k) * 512
                            for kc in range(2):
                                PE(PB[bk][:], wbuf[slg][:, base + kc * 256 + j * 128:base + kc * 256 + (j + 1) * 128], xcb[:, 2 * k + kc, :],
                                   kc == 0, kc == 1, [b_wbuf[slg], b_xcb[2 * k + kc]], [b_PB[bk]])
                            ACT(dst[q][:], PB[bk][:], AF.Sigmoid, [b_PB[bk], bP], [bdst[q]], bias=P[:, POFF[boff] + c:POFF[boff] + c + 1])
                        ACT(at[q][:], rt[q][:], AF.Exp, [b_rt[q], b_der[l]], [b_at[q]], scale=cpar[l][:, c:c + 1])
                        ACT(a2t[q][:], rt[q][:], AF.Exp, [b_rt[q], b_der[l]], [b_a2t[q]], scale=c2par[l][:, c:c + 1])
                        ACT(a2t[q][:], a2t[q][:], AF.Sqrt, [b_a2t[q]], [b_a2t[q]], bias=1.0, scale=-1.0)
                        TT("dve", ut[q][:], it[q][:], xc[:, c, :], ALU.mult, [b_it[q], b_xc[c]], [b_ut[q]])
                        TT("dve", ut[q][:], ut[q][:], a2t[q][:], ALU.mult, [b_ut[q], b_a2t[q]], [b_ut[q]])
                        SCAN(hs[q][:], at[q][:], ut[q][:], hlru[l][:, c:c + 1], [b_at[q], b_ut[q], b_hlru[l][c]], [b_hs[q]])
                        CP("pool", hlru[l][:, c:c + 1], hs[q][:, NT - 1:NT], [b_hs[q]], [b_hlru[l][c]])
                        bk = nextbank([4, 5, 6, 7])
                        proj_fm(sly[c // 4], c % 4, bk)
                        ACT(gl[q][:], PB[bk][:], AF.Gelu, [b_PB[bk]], [b_gl[q]])
                        TT("dve", ybr[:, c, :], hs[q][:], gl[q][:], ALU.mult, [b_hs[q], b_gl[q]], [b_ybr[c]])
                    w_rel(slg)
                    w_rel(sly[0])
                    w_rel(sly[1])
                    branch_merge(l, 1, gsb, b_gsb, tmpm, b_tmpm)
                    S.barrier()

                with Phase('ret', STAGES) as ph:
                    ph.check()
                    qr = sb("qr", [128, 4, NT], BF16, ph); b_qr = bufs(4)
                    qrd = sb("qrd", [128, 4, NT], BF16, ph); b_qrd = bufs(4)
                    kr = sb("kr", [128, 4, NT], BF16, ph); b_kr = bufs(4)
                    ra = [sb("ra%d" % i, [128, NT], F32, ph) for i in range(2)]; b_ra = bufs(2)
                    rb = [sb("rb%d" % i, [128, NT], F32, ph) for i in range(2)]; b_rb = bufs(2)
                    v_tm2 = [sb("v_tm%d" % i, [128, 1024], BF16, ph) for i in range(2)]; b_v2 = bufs(2)
                    silg2 = [sb("silg%d" % i, [128, 1024], F32, ph) for i in range(2)]; b_silg2 = bufs(2)
                    PT2 = [sb("PT%d" % i, [128, 4, 128], BF16, ph) for i in range(2)]; b_PT2 = bufs(2)
                    k_tm2 = [sb("k_tm%d" % i, [128, 4, 128], BF16, ph) for i in range(2)]; b_ktm2 = bufs(2)
                    ryn2 = [sb("ryn%d" % i, [128, 1024], BF16, ph) for i in range(2)]; b_ryn2 = bufs(2)
                    junk = sb("rjunk", [128, 256], BF16, ph); b_junk = Buf()
                    ss4 = sb("ss4", [128, 12], F32, ph); b_ss4 = Buf()
                    gsb = [r_[:] for r_ in ra]; b_gsb = b_ra
                    tmpm = rb; b_tmpm = b_rb
                    merged_b = sb("merged_b", [128, 8, NT], BF16, ph); b_mgb = bufs(8)
                    cos_t = sb("cos_t", [128, NT], F32, ph); sin_t = sb("sin_t", [128, NT], F32, ph); b_cs = bufs(2)
                    S.dma("pool", cos_t[:], rc_d[:, t0:t0 + NT], (), [b_cs[0]])
                    S.dma("pool", sin_t[:], rs_d[:, t0:t0 + NT], (), [b_cs[1]])
                    for (un, uns, isq) in (("Q", "QS", True), ("K", "KS", False)):
                        s1 = w_get(l, un)
                        s2 = w_get(l, uns)
                        for h in range(4):
                            if not on("r1"):
                                break
                            q = h % 2
                            ba = nextbank([0, 1, 2, 3])
                            proj_fm(s1, h, ba)
                            bb = nextbank([4, 5, 6, 7])
                            proj_fm(s2, h, bb)
                            TT("dve", ra[q][:], PB[ba][:], cos_t[:], ALU.mult, [b_PB[ba], b_cs[0]], [b_ra[q]])
                            TT("dve", rb[q][:], PB[bb][:], sin_t[:], ALU.mult, [b_PB[bb], b_cs[1]], [b_rb[q]])
                            if isq:
                                TT("dve", ra[q][:], ra[q][:], rb[q][:], ALU.add, [b_ra[q], b_rb[q]], [b_ra[q]])
                                CP("act", qr[:, h, :], ra[q][:], [b_ra[q]], [b_qr[h]])
                                TT("pool", qrd[:, h, :].rearrange("p (a b) -> p a b", a=4), ra[q][:].rearrange("p (a b) -> p a b", a=4),
                                   cst[:, COFF["qd"] + h * 128:COFF["qd"] + (h + 1) * 128].unsqueeze(1).to_broadcast([128, 4, 128]),
                                   ALU.mult, [b_ra[q], b_cst], [b_qrd[h]])
                            else:
                                TT("dve", kr[:, h, :], ra[q][:], rb[q][:], ALU.add, [b_ra[q], b_rb[q]], [b_kr[h]])
                        w_rel(s1)
                        w_rel(s2)
                    sv = [w_get(l, "V0"), w_get(l, "V1")]
                    sg_ = [w_get(l, "GR0"), w_get(l, "GR1")]
                    def ret_A(tc):
                        p = tc % 2
                        ts_ = slice(tc * 128, (tc + 1) * 128)
                        v_tm, b_v, silg, b_silg, PT, b_PT, k_tm, b_ktm = v_tm2[p], b_v2[p], silg2[p], b_silg2[p], PT2[p], b_PT2[p], k_tm2[p], b_ktm2[p]
                        for vh in range(2):
                            proj_tm(sv[vh], tc, vh)
                            CP("act", v_tm[:, vh * 512:(vh + 1) * 512], PB[vh][:], [b_PB[vh]], [b_v])
                        for gh in range(2):
                            proj_tm(sg_[gh], tc, 2 + gh)
                            ACT(silg[:, gh * 512:(gh + 1) * 512], PB[2 + gh][:], AF.Silu, [b_PB[2 + gh]], [b_silg])
                        for h in range(4):
                            PE(PB[4][:, h * 128:(h + 1) * 128], kr[:, h, ts_], qr[:, h, ts_], True, True, [b_kr[h], b_qr[h]], [b_PB[4]])
                        for h in range(4):
                            PE(PB[5][:, h * 128:(h + 1) * 128], kr[:, h, ts_], ident_b[:], True, True, [b_kr[h], b_c2], [b_PB[5]])
                        TT("dve", PT[:].rearrange("p h l -> p (h l)"), PB[4][:], cst[:, COFF["dmask"]:COFF["dmask"] + 512], ALU.mult,
                           [b_PB[4], b_cst], [b_PT])
                        for h in range(4):
                            TS("dve", k_tm[:, h, :], PB[5][:, h * 128:(h + 1) * 128], cst[:, COFF["kdec"] + h:COFF["kdec"] + h + 1], None,
                               ALU.mult, None, [b_PB[5], b_cst], [b_ktm])

                    def ret_B(tc):
                        p = tc % 2
                        ts_ = slice(tc * 128, (tc + 1) * 128)
                        v_tm, b_v, silg, b_silg, PT, b_PT, k_tm, b_ktm = v_tm2[p], b_v2[p], silg2[p], b_silg2[p], PT2[p], b_PT2[p], k_tm2[p], b_ktm2[p]
                        for h in range(4):
                            bk = 6 + h // 2
                            o_ = PB[bk][:, (h % 2) * 256:(h % 2 + 1) * 256]
                            PE(o_, PT[:, h, :], v_tm[:, h * 256:(h + 1) * 256], True, False, [b_PT, b_v], [b_PB[bk]])
                            PE(o_, qrd[:, h, ts_], Sret_b[l][:, h * 256:(h + 1) * 256], False, True, [b_qrd[h], b_Sret_b[l]], [b_PB[bk]])
                        for h in range(4):
                            bk = 2 + h // 2
                            PE(PB[bk][:, (h % 2) * 256:(h % 2 + 1) * 256], k_tm[:, h, :], v_tm[:, h * 256:(h + 1) * 256], True, True,
                               [b_ktm, b_v], [b_PB[bk]])
                        for h in range(4):
                            bk = 2 + h // 2
                            S.op("act", lambda e, o_=Sret[l][:, h * 256:(h + 1) * 256], g_=float(GAM[h] ** 128): e.mul(out=o_, in_=o_, mul=g_),
                                 [b_Sret[l]], [b_Sret[l]])
                            TT("dve", Sret[l][:, h * 256:(h + 1) * 256], Sret[l][:, h * 256:(h + 1) * 256],
                               PB[bk][:, (h % 2) * 256:(h % 2 + 1) * 256], ALU.add, [b_Sret[l], b_PB[bk]], [b_Sret[l]])
                        CP("act", Sret_b[l][:], Sret[l][:], [b_Sret[l]], [b_Sret_b[l]])
                        yn = ryn2[p]; b_yn = b_ryn2[p]
                        MSET("pool", ss4[:, 0:4], 0.0, [b_ss4])
                        for h in range(4):
                            bk = 6 + h // 2
                            ACT(junk[:], PB[bk][:, (h % 2) * 256:(h % 2 + 1) * 256], AF.Square, [b_PB[bk]], [b_junk, b_ss4],
                                accum=ss4[:, h:h + 1])
                        ACT(ss4[:, 4:8], ss4[:, 0:4], AF.Ln, [b_ss4], [b_ss4], bias=EPS, scale=1.0 / 256)
                        ACT(ss4[:, 8:12], ss4[:, 4:8], AF.Exp, [b_ss4], [b_ss4], scale=-0.5)
                        for h in range(4):
                            bk = 6 + h // 2
                            STT(yn[:, h * 256:(h + 1) * 256], PB[bk][:, (h % 2) * 256:(h % 2 + 1) * 256], ss4[:, 8 + h:9 + h],
                                silg[:, h * 256:(h + 1) * 256], ALU.mult, ALU.mult, [b_PB[bk], b_ss4, b_silg], [b_yn])

                    def ret_B2(tc):
                        p = tc % 2
                        ts_ = slice(tc * 128, (tc + 1) * 128)
                        yn = ryn2[p]; b_yn = b_ryn2[p]
                        for c in range(8):
                            bk = c // 4
                            PE(PB[bk][:, (c % 4) * 128:(c % 4 + 1) * 128], yn[:, c * 128:(c + 1) * 128], ident_b[:], True, True,
                               [b_yn, b_c2], [b_PB[bk]])
                        for c in range(8):
                            bk = c // 4
                            if c % 2:
                                CP("act", ybr[:, c, ts_], PB[bk][:, (c % 4) * 128:(c % 4 + 1) * 128], [b_PB[bk]], [b_ybr[c]])
                            else:
                                TS("dve", ybr[:, c, ts_], PB[bk][:, (c % 4) * 128:(c % 4 + 1) * 128], 1.0, None, ALU.mult, None,
                                   [b_PB[bk]], [b_ybr[c]])

                    ret_A(0)
                    for tc in range(4):
                        if tc < 3:
                            ret_A(tc + 1)
                        if tc > 0:
                            ret_B2(tc - 1)
                        ret_B(tc)
                    ret_B2(3)
                    for s_ in sv + sg_:
                        w_rel(s_)
                    branch_merge(l, 2, gsb, b_gsb, tmpm, b_tmpm, merged_b, b_mgb)
                    for half in range(2):
                        sl = w_get(l, "WO" + "ab"[half])
                        for j in range(4):
                            oc = half * 4 + j
                            bk = nextbank([0, 1, 2, 3])
                            for kc in range(8):
                                PE(PB[bk][:], wbuf[sl][:, kc * 512 + j * 128:kc * 512 + (j + 1) * 128], merged_b[:, kc, :],
                                   kc == 0, kc == 7, [b_wbuf[sl], b_mgb[kc]], [b_PB[bk]])
                            TT("dve", x_fm[:, oc, :], x_fm[:, oc, :], PB[bk][:], ALU.add, [b_x[oc], b_PB[bk]], [b_x[oc]])
                        w_rel(sl)
                    S.barrier()

                rmsnorm_to_hT(POFF["g_ffn"], P, bP)
                with Phase('ffn', STAGES) as ph:
                    ph.check()
                    raw = [sb("fraw%d" % i, [128, NT + 2], F32, ph) for i in range(2)]; b_raw = bufs(2)
                    acc = [sb("facc%d" % i, [128, NT], F32, ph) for i in range(2)]; b_acc = bufs(2)
                    gel = [sb("fgel%d" % i, [128, NT], F32, ph) for i in range(2)]; b_gel = bufs(2)
                    g_fm = sb("g_fm", [128, 24, NT], BF16, ph); b_g = bufs(24)
                    for ui in range(12):
                        sl = w_get(l, "F%d" % ui)
                        for jj in range(2):
                            j = ui * 2 + jj
                            q = j % 2
                            ba = nextbank([0, 1, 2, 3])
                            proj_fm(sl, jj, ba)
                            bu = nextbank([4, 5, 6, 7])
                            proj_fm(sl, 2 + jj, bu)
                            CP("act", raw[q][:, 2:NT + 2], PB[ba][:], [b_PB[ba]], [b_raw[q]])
                            CP("pool", raw[q][:, 0:2], halo_f[l][:, j, :], [b_halo_f[l][j]], [b_raw[q]])
                            CP("pool", halo_f[l][:, j, :], raw[q][:, NT:NT + 2], [b_raw[q]], [b_halo_f[l][j]])
                            wo_ = POFF["fcw"] + j * 3
                            TS("dve", acc[q][:], raw[q][:, 0:NT], P[:, wo_:wo_ + 1], P[:, POFF["fcb"] + j:POFF["fcb"] + j + 1],
                               ALU.mult, ALU.add, [b_raw[q], bP], [b_acc[q]])
                            for k in range(1, 3):
                                STT(acc[q][:], raw[q][:, k:k + NT], P[:, wo_ + k:wo_ + k + 1], acc[q][:], ALU.mult, ALU.add,
                                    [b_raw[q], bP, b_acc[q]], [b_acc[q]])
                            ACT(gel[q][:], acc[q][:], AF.Gelu, [b_acc[q]], [b_gel[q]])
                            TT("dve", g_fm[:, j, :], gel[q][:], PB[bu][:], ALU.mult, [b_gel[q], b_PB[bu]], [b_g[j]])
                        w_rel(sl)
                    for oc in range(8):
                        sl = w_get(l, "FO%d" % oc)
                        bk = nextbank([0, 1, 2, 3])
                        for kc in range(24):
                            PE(PB[bk][:], wbuf[sl][:, kc * 128:(kc + 1) * 128], g_fm[:, kc, :], kc == 0, kc == 23,
                               [b_wbuf[sl], b_g[kc]], [b_PB[bk]])
                        TT("dve", x_fm[:, oc, :], x_fm[:, oc, :], PB[bk][:], ALU.add, [b_x[oc], b_PB[bk]], [b_x[oc]])
                        w_rel(sl)
                    S.barrier()

            with Phase('final', STAGES) as ph:
                ph.check()
                yf = sb("yf", [128, 8, NT], F32, ph); b_yf = bufs(8)
                otm = [sb("otm%d" % i, [128, D], F32, ph) for i in range(2)]; b_otm = bufs(2)
                sq32 = [sb("sq32_%d" % i, [128, NT], F32, ph) for i in range(2)]; b_sq32 = bufs(2)
                P = par[0]; bP = b_par[0]
                for c in range(8):
                    q = c % 2
                    ACT(sq32[q][:], x_fm[:, c, :], AF.Square, [b_x[c]], [b_sq32[q]])
                    PE(PB[7][:], ones_f[:], sq32[q][:], c == 0, c == 7, [b_c2, b_sq32[q]], [b_PB[7]])
                ACT(lnv[:], PB[7][:], AF.Ln, [b_PB[7]], [b_lnv], bias=EPS, scale=1.0 / D)
                ACT(rstd[:], lnv[:], AF.Exp, [b_lnv], [b_rstd], scale=-0.5)
                for c in range(8):
                    STT(yf[:, c, :], x_fm[:, c, :], P[:, POFF["g_fin"] + c:POFF["g_fin"] + c + 1], rstd[:], ALU.mult, ALU.mult,
                        [b_x[c], bP, b_rstd], [b_yf[c]])
                for tc in range(4):
                    q = tc % 2
                    for c in range(8):
                        bk = (tc % 2) * 2 + c // 4
                        PE(PB[bk][:, (c % 4) * 128:(c % 4 + 1) * 128], yf[:, c, tc * 128:(tc + 1) * 128], IDf, True, True,
                           [b_yf[c], b_cst], [b_PB[bk]])
                    for hh in range(2):
                        bk = (tc % 2) * 2 + hh
                        CP("act" if hh else "dve", otm[q][:, hh * 512:(hh + 1) * 512], PB[bk][:], [b_PB[bk]], [b_otm[q]])
                    S.dma("sp", out_d[t0 + tc * 128:t0 + (tc + 1) * 128, :], otm[q][:], [b_otm[q]], [b_out])
                S.final_wait("sp", [b_out])
                S.barrier(with_sp=True)
        S.prog["sp"].append(([((("d", j)), 16 * S.dcnt[j]) for j in range(NDS) if S.dcnt[j] > 0], None, None))
        S.emit()
    return nc


def _bias_ap_fix():
    pass


_CACHE = {}


def prep_inputs(inputs, NBLK, batch):
    T = NBLK * NT
    key = "shared"
    ws = np.stack([host_wstream(inputs, l) for l in range(2)], axis=0)
    par = np.stack([host_params(inputs, l) for l in range(2)], axis=0)
    cst = host_consts()
    rc, rs = host_rope(T)
    maps = []
    for b in batch:
        maps.append({"x": np.ascontiguousarray(inputs["x"][b, :T, :]), "ws": ws, "par": par, "cst": cst,
                     "rcos": rc, "rsin": rs})
    return maps


def kernel(**inputs):
    inputs = {k: np.asarray(v) for k, v in inputs.items()}
    B, SEQ, _ = inputs["x"].shape
    NBLK = SEQ // NT
    if NBLK not in _CACHE:
        _CACHE[NBLK] = build(NBLK)
    nc = _CACHE[NBLK]
    n_cores = 8
    batch = [i % B for i in range(n_cores)]
    maps = prep_inputs(inputs, NBLK, batch)
    res = run_bass_kernel_spmd(nc, maps, core_ids=list(range(n_cores)))
    out = np.stack([res.results[b]["out"] for b in range(B)], axis=0)
    return out.astype(np.float32)
```
